# Optimizing a Trainium2 kernel written in Bass

```python
import math
import jax
import jax.numpy as jnp
from jax import lax
import numpy as np

D_MODEL = 1024
BATCH = 8
SEQ = 2048
DEPTH = 2

CTX_LEN = 256
GRID_W = 64
HEAD_DIM = 64
Q_BLOCK = 128
MIX_HALF = D_MODEL // 2
A_HEADS = MIX_HALF // HEAD_DIM
A_KV_HEADS = max(1, A_HEADS // 4)
A_GROUP = A_HEADS // A_KV_HEADS
B_HEADS = MIX_HALF // (2 * HEAD_DIM)
B_V_DIM = 2 * HEAD_DIM
C_HEADS = MIX_HALF // HEAD_DIM
WIN_H = 8
WIN_W = 16
D_GROUPS = 4
D_GROUP_DIM = MIX_HALF // D_GROUPS
FFN_DIM = 4 * D_MODEL
ROPE_THETA = 10000.0
N_EVEN = (DEPTH + 1) // 2
N_ODD = DEPTH // 2
ALPHA = (2.0 * DEPTH) ** 0.25
BETA = (8.0 * DEPTH) ** -0.25
EPS = 1e-6
ATTN_SCALE = HEAD_DIM ** -0.5
QA_W = A_HEADS * HEAD_DIM
QB_W = B_HEADS * 2 * HEAD_DIM
KA_W = A_KV_HEADS * HEAD_DIM
VA_W = KA_W
KB_W = QB_W
VB_W = B_HEADS * B_V_DIM
AB_Q_W = QA_W + QB_W
AB_IN_W = AB_Q_W + KA_W + VA_W + KB_W + VB_W
AB_OUT_IN = QA_W + VB_W
C_W = C_HEADS * HEAD_DIM
CD_Q_W = C_W + MIX_HALF
CD_IN_W = CD_Q_W + 2 * C_W
CD_OUT_IN = C_W + MIX_HALF

kernel_name = 'hybrid_gqa_diff_natten_fnet_prefix_dit'


def _rms_norm(x, g):
    xf = x.astype(jnp.float32)
    y = xf * lax.rsqrt(jnp.mean(xf * xf, axis=-1, keepdims=True) + EPS)
    return (y * g.astype(jnp.float32)).astype(x.dtype)


def _layer_norm(x, g, b):
    xf = x.astype(jnp.float32)
    mu = jnp.mean(xf, axis=-1, keepdims=True)
    xc = xf - mu
    var = jnp.mean(xc * xc, axis=-1, keepdims=True)
    return (xc * lax.rsqrt(var + EPS) * g.astype(jnp.float32) + b.astype(jnp.float32)).astype(x.dtype)


def _axial_rope_tables(n, dtype):
    pos = jnp.arange(n)
    row = (pos // GRID_W).astype(jnp.float32)
    col = (pos % GRID_W).astype(jnp.float32)
    half = HEAD_DIM // 2
    freqs = jnp.power(ROPE_THETA, -jnp.arange(0, half, 2, dtype=jnp.float32) / half)
    def axis_angles(p):
        a = p[:, None] * freqs[None, :]
        return jnp.concatenate([a, a], axis=-1)
    ang = jnp.concatenate([axis_angles(row), axis_angles(col)], axis=-1)
    return jnp.cos(ang).astype(dtype), jnp.sin(ang).astype(dtype)


def _rotate_axial(x):
    xa = x.reshape(x.shape[:-1] + (2, 2, HEAD_DIM // 4))
    return jnp.concatenate([-xa[..., 1:, :], xa[..., :1, :]], axis=-2).reshape(x.shape)


def _apply_rope(x, cos, sin):
    shp = (x.shape[1],) + (1,) * (x.ndim - 3) + (x.shape[-1],)
    return x * cos.reshape(shp) + _rotate_axial(x) * sin.reshape(shp)


def _sweep(q, fn):
    b, n = q.shape[:2]
    nb = n // Q_BLOCK
    qb = jnp.moveaxis(q.reshape((b, nb, Q_BLOCK) + q.shape[2:]), 1, 0)
    out = lax.map(fn, qb)
    return jnp.moveaxis(out, 0, 1).reshape((b, n) + out.shape[3:])


def _gqa_core(q, k, v):
    s = jnp.einsum('bqkgd,bmkd->bkgqm', q, k).astype(jnp.float32) * ATTN_SCALE
    p = jax.nn.softmax(s, axis=-1).astype(v.dtype)
    o = jnp.einsum('bkgqm,bmkd->bqkgd', p, v)
    return o.reshape(o.shape[:2] + (-1,))


def _diff_core(q, k, v, lam, lam_init, subln):
    s = jnp.einsum('bqhjd,bmhjd->bhjqm', q, k).astype(jnp.float32) * ATTN_SCALE
    p = jax.nn.softmax(s, axis=-1)
    a = (p[:, :, 0] - lam * p[:, :, 1]).astype(v.dtype)
    o = jnp.einsum('bhqm,bmhe->bqhe', a, v)
    o = _rms_norm(o, subln) * (1.0 - lam_init)
    return o.reshape(o.shape[:2] + (-1,))


def _neighbourhood_attention(q, k, v, k_ctx, v_ctx, rpb):
    b, n, h, d = q.shape
    rows = n // GRID_W
    kh = min(WIN_H, rows)
    r = jnp.arange(rows)
    cidx = jnp.arange(GRID_W)
    row_idx = jnp.clip(r - kh // 2, 0, rows - kh)[:, None] + jnp.arange(kh)[None, :]
    col_idx = jnp.clip(cidx - WIN_W // 2, 0, GRID_W - WIN_W)[:, None] + jnp.arange(WIN_W)[None, :]
    dr = (row_idx - r[:, None] + WIN_H - 1)[:, None, :, None]
    dc = (col_idx - cidx[:, None] + WIN_W - 1)[None, :, None, :]
    bias = rpb[:, dr, dc].astype(jnp.float32)
    kg = k.reshape(b, rows, GRID_W, h, d)
    vg = v.reshape(b, rows, GRID_W, h, d)
    rb = Q_BLOCK // GRID_W
    nb = rows // rb
    qg = jnp.moveaxis(q.reshape(b, nb, rb, GRID_W, h, d), 1, 0)
    ridx = row_idx.reshape(nb, rb, kh)
    bb = jnp.moveaxis(bias.reshape(h, nb, rb, GRID_W, kh, WIN_W), 1, 0)
    nwin = kh * WIN_W

    def block(args):
        qblk, ri, bi = args
        k_win = kg[:, ri][:, :, :, col_idx]
        v_win = vg[:, ri][:, :, :, col_idx]
        s_win = jnp.einsum('brchd,brkcwhd->bhrckw', qblk, k_win).astype(jnp.float32) * ATTN_SCALE + bi
        s_ctx = jnp.einsum('brchd,blhd->bhrcl', qblk, k_ctx).astype(jnp.float32) * ATTN_SCALE
        s = jnp.concatenate([s_win.reshape(b, h, rb, GRID_W, nwin), s_ctx], axis=-1)
        p = jax.nn.softmax(s, axis=-1).astype(v.dtype)
        p_win = p[..., :nwin].reshape(b, h, rb, GRID_W, kh, WIN_W)
        return (jnp.einsum('bhrckw,brkcwhd->brchd', p_win, v_win)
                + jnp.einsum('bhrcl,blhd->brchd', p[..., nwin:], v_ctx))

    out = lax.map(block, (qg, ridx, bb))
    return jnp.moveaxis(out, 0, 1).reshape(b, n, h * d)


def _fourier_mix(f):
    b, t, _ = f.shape
    fg = f.reshape(b, t, D_GROUPS, D_GROUP_DIM).astype(jnp.float32)
    y = jnp.fft.fft2(fg, axes=(1, 3), norm='ortho').real
    return y.reshape(b, t, MIX_HALF).astype(f.dtype)


def _ffn(u, w1, w2):
    hid = jax.nn.relu(u @ w1)
    return (hid * hid) @ w2


def _mixer_ab(ux, uc, w_in, w_out, q_norm, k_norm, lam_p, subln, lam_init, cos, sin, last):
    b, n, _ = ux.shape
    l = uc.shape[1]

    def split_q(hq, t):
        qa = hq[..., :QA_W].reshape(b, t, A_KV_HEADS, A_GROUP, HEAD_DIM)
        qb = hq[..., QA_W:].reshape(b, t, B_HEADS, 2, HEAD_DIM)
        return qa, qb

    def split_kv(hkv, t):
        o1 = KA_W
        o2 = o1 + VA_W
        o3 = o2 + KB_W
        ka = hkv[..., :o1].reshape(b, t, A_KV_HEADS, HEAD_DIM)
        va = hkv[..., o1:o2].reshape(b, t, A_KV_HEADS, HEAD_DIM)
        kb = hkv[..., o2:o3].reshape(b, t, B_HEADS, 2, HEAD_DIM)
        vb = hkv[..., o3:].reshape(b, t, B_HEADS, B_V_DIM)
        return ka, va, kb, vb

    hx = ux @ w_in
    qa, qb = split_q(hx[..., :AB_Q_W], n)
    ka, va, kb, vb = split_kv(hx[..., AB_Q_W:], n)
    ka_c, va_c, kb_c, vb_c = split_kv(uc @ w_in[:, AB_Q_W:], l)
    ka_c = _rms_norm(ka_c, k_norm)
    lp = lam_p.astype(jnp.float32)
    lam = jnp.exp(jnp.sum(lp[0] * lp[1])) - jnp.exp(jnp.sum(lp[2] * lp[3])) + lam_init

    qa = _apply_rope(_rms_norm(qa, q_norm), cos, sin)
    ka_all = jnp.concatenate([_apply_rope(_rms_norm(ka, k_norm), cos, sin), ka_c], axis=1)
    va_all = jnp.concatenate([va, va_c], axis=1)
    qb = _apply_rope(qb, cos, sin)
    kb_all = jnp.concatenate([_apply_rope(kb, cos, sin), kb_c], axis=1)
    vb_all = jnp.concatenate([vb, vb_c], axis=1)
    o_a = _sweep(qa, lambda qblk: _gqa_core(qblk, ka_all, va_all))
    o_b = _sweep(qb, lambda qblk: _diff_core(qblk, kb_all, vb_all, lam, lam_init, subln))
    out_x = jnp.concatenate([o_a, o_b], axis=-1) @ w_out
    if last:
        return out_x, None
    qa_c, qb_c = split_q(uc @ w_in[:, :AB_Q_W], l)
    o_a_c = _gqa_core(_rms_norm(qa_c, q_norm), ka_c, va_c)
    o_b_c = _diff_core(qb_c, kb_c, vb_c, lam, lam_init, subln)
    out_c = jnp.concatenate([o_a_c, o_b_c], axis=-1) @ w_out
    return out_x, out_c


def _mixer_cd(ux, uc, w_in, w_out, rpb, last):
    b, n, _ = ux.shape
    l = uc.shape[1]
    hx = ux @ w_in
    qc = hx[..., :C_W].reshape(b, n, C_HEADS, HEAD_DIM)
    fd = hx[..., C_W:CD_Q_W]
    kc = hx[..., CD_Q_W:CD_Q_W + C_W].reshape(b, n, C_HEADS, HEAD_DIM)
    vc = hx[..., CD_Q_W + C_W:].reshape(b, n, C_HEADS, HEAD_DIM)
    hc = uc @ w_in[:, CD_Q_W:]
    kc_c = hc[..., :C_W].reshape(b, l, C_HEADS, HEAD_DIM)
    vc_c = hc[..., C_W:].reshape(b, l, C_HEADS, HEAD_DIM)
    o_c = _neighbourhood_attention(qc, kc, vc, kc_c, vc_c, rpb)
    o_d = _fourier_mix(fd)
    out_x = jnp.concatenate([o_c, o_d], axis=-1) @ w_out
    if last:
        return out_x, None
    hq_c = uc @ w_in[:, :CD_Q_W]
    qc_c = hq_c[..., :C_W].reshape(b, l, C_HEADS, 1, HEAD_DIM)
    o_c_c = _gqa_core(qc_c, kc_c, vc_c)
    o_d_c = _fourier_mix(hq_c[..., C_W:])
    out_c = jnp.concatenate([o_c_c, o_d_c], axis=-1) @ w_out
    return out_x, out_c


def setup_inputs(seed: int = 0) -> dict:
    key = jax.random.key(seed)
    ks = jax.random.split(key, 19)

    def nrm(k, shape, s):
        return jax.random.normal(k, shape, jnp.float32) * s

    return {
        'x': nrm(ks[0], (BATCH, SEQ, D_MODEL), 1.0),
        'c': nrm(ks[1], (BATCH, D_MODEL), 1.0),
        'ctx': nrm(ks[2], (BATCH, CTX_LEN, D_MODEL), 1.0),
        'c_ctx': nrm(ks[3], (D_MODEL,), 1.0),
        'mod_w': nrm(ks[4], (DEPTH, D_MODEL, 6 * D_MODEL), 0.5 * D_MODEL ** -0.5),
        'mod_b': nrm(ks[5], (DEPTH, 6 * D_MODEL), 0.02),
        'ln_g': 1.0 + nrm(ks[6], (DEPTH, 2, D_MODEL), 0.02),
        'ln_b': nrm(ks[7], (DEPTH, 2, D_MODEL), 0.02),
        'ffn_w1': nrm(ks[8], (DEPTH, D_MODEL, FFN_DIM), D_MODEL ** -0.5),
        'ffn_w2': nrm(ks[9], (DEPTH, FFN_DIM, D_MODEL), BETA * FFN_DIM ** -0.5),
        'ab_w_in': nrm(ks[10], (N_EVEN, D_MODEL, AB_IN_W), D_MODEL ** -0.5),
        'ab_w_out': nrm(ks[11], (N_EVEN, AB_OUT_IN, D_MODEL), BETA * AB_OUT_IN ** -0.5),
        'a_q_norm': 1.0 + nrm(ks[12], (N_EVEN, HEAD_DIM), 0.02),
        'a_k_norm': 1.0 + nrm(ks[13], (N_EVEN, HEAD_DIM), 0.02),
        'b_lambda': nrm(ks[14], (N_EVEN, 4, HEAD_DIM), 0.1),
        'b_subln': 1.0 + nrm(ks[15], (N_EVEN, B_V_DIM), 0.02),
        'cd_w_in': nrm(ks[16], (N_ODD, D_MODEL, CD_IN_W), D_MODEL ** -0.5),
        'cd_w_out': nrm(ks[17], (N_ODD, CD_OUT_IN, D_MODEL), BETA * CD_OUT_IN ** -0.5),
        'c_rpb': nrm(ks[18], (N_ODD, C_HEADS, 2 * WIN_H - 1, 2 * WIN_W - 1), 0.1),
    }


def reference(x, c, ctx, c_ctx, mod_w, mod_b, ln_g, ln_b, ffn_w1, ffn_w2, ab_w_in, ab_w_out,
              a_q_norm, a_k_norm, b_lambda, b_subln, cd_w_in, cd_w_out, c_rpb):
    cos, sin = _axial_rope_tables(x.shape[1], x.dtype)
    sc = jax.nn.silu(c)
    scc = jax.nn.silu(c_ctx)
    for i in range(DEPTH):
        last = i == DEPTH - 1
        j = i // 2
        mx = (sc @ mod_w[i] + mod_b[i])[:, None, :]
        sh1, s1, g1, sh2, s2, g2 = jnp.split(mx, 6, axis=-1)
        n_ctx_mod = 2 if last else 6
        mc = scc @ mod_w[i][:, :n_ctx_mod * D_MODEL] + mod_b[i][:n_ctx_mod * D_MODEL]
        mcs = jnp.split(mc, n_ctx_mod)
        ux = x * (1 + s1) + sh1
        uc = ctx * (1 + mcs[1]) + mcs[0]
        if i % 2 == 0:
            lam_init = 0.8 - 0.6 * math.exp(-0.3 * i)
            ox, oc = _mixer_ab(ux, uc, ab_w_in[j], ab_w_out[j], a_q_norm[j], a_k_norm[j],
                               b_lambda[j], b_subln[j], lam_init, cos, sin, last)
        else:
            ox, oc = _mixer_cd(ux, uc, cd_w_in[j], cd_w_out[j], c_rpb[j], last)
        x = _layer_norm(ALPHA * x + g1 * ox, ln_g[i, 0], ln_b[i, 0])
        x = _layer_norm(ALPHA * x + g2 * _ffn(x * (1 + s2) + sh2, ffn_w1[i], ffn_w2[i]), ln_g[i, 1], ln_b[i, 1])
        if not last:
            ctx = _layer_norm(ALPHA * ctx + mcs[2] * oc, ln_g[i, 0], ln_b[i, 0])
            ctx = _layer_norm(ALPHA * ctx + mcs[5] * _ffn(ctx * (1 + mcs[4]) + mcs[3], ffn_w1[i], ffn_w2[i]),
                              ln_g[i, 1], ln_b[i, 1])
    return x
```

```python
import numpy as np
import concourse.bass as bass
import concourse.mybir as mybir

F32 = mybir.dt.float32
BF16 = mybir.dt.bfloat16
AF = mybir.ActivationFunctionType
ALU = mybir.AluOpType
AX = mybir.AxisListType

COMPUTE = ("pe", "act", "dve", "pool")
ENGS = ("pe", "act", "dve", "pool", "sp")
QUEUES = ("sp", "actq", "poolq")
Q_ENG = {"sp": "sp", "actq": "act", "poolq": "pool"}


class Op:
    __slots__ = ("eng", "emit", "deps", "is_dma", "queue", "ndep", "seq", "sem", "val", "order")

    def __init__(self, eng, emit, is_dma=False, queue=None):
        self.eng = eng
        self.emit = emit
        self.deps = []
        self.is_dma = is_dma
        self.queue = queue
        self.ndep = 0
        self.seq = None
        self.sem = None
        self.val = None
        self.order = 0


def _prod(xs):
    r = 1
    for v in xs:
        r *= int(v)
    return r


class Arena:
    def __init__(self, tensor, nbytes):
        self.t = tensor
        self.free_list = [(0, nbytes)]
        self.live = {}

    def alloc(self, nbytes, align=64, top=False):
        nbytes = (nbytes + align - 1) // align * align
        if top:
            for i in range(len(self.free_list) - 1, -1, -1):
                o, n = self.free_list[i]
                if n >= nbytes:
                    if n == nbytes:
                        self.free_list.pop(i)
                    else:
                        self.free_list[i] = (o, n - nbytes)
                    self.live[o + n - nbytes] = nbytes
                    return o + n - nbytes
            raise RuntimeError(f"arena OOM(top): need {nbytes}, free={self.free_list}")
        for i, (o, n) in enumerate(self.free_list):
            if n >= nbytes:
                if n == nbytes:
                    self.free_list.pop(i)
                else:
                    self.free_list[i] = (o + nbytes, n - nbytes)
                self.live[o] = nbytes
                return o
        raise RuntimeError(f"arena OOM: need {nbytes}, free={self.free_list}")

    def free(self, off):
        n = self.live.pop(off)
        fl = self.free_list + [(off, n)]
        fl.sort()
        merged = []
        for o, m in fl:
            if merged and merged[-1][0] + merged[-1][1] == o:
                merged[-1] = (merged[-1][0], merged[-1][1] + m)
            else:
                merged.append((o, m))
        self.free_list = merged

    def tile(self, shape, dtype, top=False):
        es = 4 if dtype == F32 else 2
        n = _prod(shape) * es
        off = self.alloc(n, top=top)
        ap = self.t[:, off // 2:(off + n) // 2]
        if dtype == F32:
            ap = ap.bitcast(F32)
        if len(shape) == 2:
            ap = ap.rearrange("p (a b) -> p a b", a=shape[0])
        elif len(shape) == 3:
            ap = ap.rearrange("p (a b c) -> p a b c", a=shape[0], b=shape[1])
        elif len(shape) == 4:
            ap = ap.rearrange("p (a b c d) -> p a b c d", a=shape[0], b=shape[1], c=shape[2])
        return off, ap


class Prog:
    def __init__(self, nc, stack, dma_pool=None, untracked=()):
        self.nc = nc
        self.ops = {e: [] for e in ENGS}
        self.track = {}
        self.untracked = set(untracked)
        self.dma_pool = dma_pool or {"sp": 24, "actq": 8, "poolq": 16}
        self.dma_count = {q: 0 for q in QUEUES}
        self.dma_ops = {q: [] for q in QUEUES}
        self.nops = 0
        self.order = {e: 0 for e in ENGS}
        self.sems = {e: stack.enter_context(nc.semaphore(f"s_{e}")) for e in COMPUTE}
        self.dsems = {q: [stack.enter_context(nc.semaphore(f"d_{q}{i}"))
                          for i in range(self.dma_pool[q])] for q in QUEUES}
        self.sig = {e: 0 for e in COMPUTE}
        self.waited = {e: {} for e in ENGS}
        self.pending_dma = {e: [] for e in ENGS}
        self.nblocks = 0

    def box(self, r):
        if isinstance(r, tuple):
            return (("key",) + r, 0, 1, 0, 1, 1 << 30)
        name = r.name
        if name in self.untracked:
            return None
        aps = r.ap
        off = int(r.offset)
        space = str(r.space)
        es = 4 if r.tensor.dtype == F32 else (2 if r.tensor.dtype == BF16 else mybir.dt.size(r.tensor.dtype))
        if "SB" in space or "PSUM" in space:
            row = _prod(r.tensor.shape[1:])
            p0 = off // row
            f0 = off % row
            pc = aps[0][1]
            f1 = f0 + sum((c - 1) * abs(s) for s, c in aps[1:]) + 1
            if "PSUM" in space:
                return (name, 0, 128, (f0 * es) // 2048 * 2048, ((f1 * es - 1) // 2048 + 1) * 2048, 2048)
            return (name, p0, p0 + pc, f0 * es, f1 * es, 2048)
        lo = off
        hi = off + sum((c - 1) * abs(s) for s, c in aps) + 1
        return (name, 0, 1, lo * es, hi * es, 1 << 18)

    @staticmethod
    def _ov(a, b):
        return a[1] < b[2] and b[1] < a[2] and a[3] < b[4] and b[3] < a[4]

    @staticmethod
    def _inside(a, b):
        return a[1] >= b[1] and a[2] <= b[2] and a[3] >= b[3] and a[4] <= b[4]

    def _pages(self, b):
        ps = b[5]
        return range(b[3] // ps, (b[4] - 1) // ps + 1)

    def add(self, eng, emit, reads=(), writes=(), queue=None):
        is_dma = queue is not None
        if is_dma:
            eng = Q_ENG[queue]
        op = Op(eng, emit, is_dma, queue)
        self.order[eng] += 1
        op.order = self.order[eng]
        deps = []
        rboxes = []
        for r in reads:
            if r is None or isinstance(r, (int, float)):
                continue
            b = self.box(r)
            if b is None:
                continue
            rboxes.append(b)
            st = self.track.get(b[0])
            if st is None:
                continue
            for pg in self._pages(b):
                ent = st.get(pg)
                if ent is None:
                    continue
                for wb, wop in ent[0]:
                    if self._ov(b, wb):
                        deps.append(wop)
                if b[0] == "PS":
                    for rb, rop in ent[1]:
                        if rop.eng != eng and self._ov(b, rb):
                            deps.append(rop)
        wboxes = []
        for w in writes:
            if w is None:
                continue
            b = self.box(w)
            if b is None:
                continue
            wboxes.append(b)
            st = self.track.get(b[0])
            if st is None:
                continue
            for pg in self._pages(b):
                ent = st.get(pg)
                if ent is None:
                    continue
                for wb, wop in ent[0]:
                    if self._ov(b, wb):
                        deps.append(wop)
                for rb, rop in ent[1]:
                    if self._ov(b, rb):
                        deps.append(rop)
        for b in wboxes:
            st = self.track.setdefault(b[0], {})
            for pg in self._pages(b):
                ent = st.setdefault(pg, [[], []])
                ent[0] = [(wb, wop) for wb, wop in ent[0] if not self._inside(wb, b)]
                ent[1] = [(rb, rop) for rb, rop in ent[1] if not self._inside(rb, b)]
                ent[0].append((b, op))
        for b in rboxes:
            st = self.track.setdefault(b[0], {})
            for pg in self._pages(b):
                ent = st.setdefault(pg, [[], []])
                if not is_dma:
                    ent[1] = [(rb, rop) for rb, rop in ent[1]
                              if not (rop.eng == eng and not rop.is_dma and self._inside(rb, b))]
                ent[1].append((b, op))
        best = {}
        seen = set()
        for d in deps:
            if d is op:
                continue
            if d.is_dma:
                if id(d) not in seen:
                    seen.add(id(d))
                    op.deps.append(d)
                continue
            if eng == "pe" and d.eng == "pe" and not is_dma:
                continue
            cur = best.get(d.eng)
            if cur is None or cur.order < d.order:
                best[d.eng] = d
        op.deps.extend(best.values())
        if is_dma:
            q = queue
            j = self.dma_count[q]
            n = self.dma_pool[q]
            if j >= n:
                prev = self.dma_ops[q][j - n]
                if id(prev) not in seen:
                    op.deps.append(prev)
            self.dma_count[q] = j + 1
            self.dma_ops[q].append(op)
            op.seq = j
        for d in op.deps:
            d.ndep += 1
        self.ops[eng].append(op)
        self.nops += 1
        return op

    def mm(self, out, lhsT, rhs, start=True, stop=True, **kw):
        return self.add("pe", lambda e: e.matmul(out, lhsT, rhs, start=start, stop=stop, **kw),
                        reads=[lhsT, rhs], writes=[out])

    def tr(self, out, in_, ident):
        return self.add("pe", lambda e: e.transpose(out, in_, ident), reads=[in_, ident], writes=[out])

    def act(self, out, in_, func, bias=0.0, scale=1.0, accum_out=None):
        kw = {}
        if accum_out is not None:
            kw["accum_out"] = accum_out
        return self.add("act", lambda e: e.activation(out, in_, func, bias=bias, scale=scale, **kw),
                        reads=[in_, bias, scale], writes=[out, accum_out])

    def tt(self, eng, out, a, b, op):
        return self.add(eng, lambda e: e.tensor_tensor(out, a, b, op), reads=[a, b], writes=[out])

    def ts(self, eng, out, a, s1, s2, op0, op1=None):
        if op1 is None:
            return self.add(eng, lambda e: e.tensor_scalar(out, a, s1, None, op0),
                            reads=[a, s1], writes=[out])
        return self.add(eng, lambda e: e.tensor_scalar(out, a, s1, s2, op0, op1),
                        reads=[a, s1, s2], writes=[out])

    def stt(self, eng, out, a, scalar, b, op0, op1):
        return self.add(eng, lambda e: e.scalar_tensor_tensor(out, a, scalar, b, op0, op1),
                        reads=[a, scalar, b], writes=[out])

    def copy(self, eng, out, in_):
        if eng == "act":
            return self.add(eng, lambda e: e.copy(out, in_), reads=[in_], writes=[out])
        return self.add(eng, lambda e: e.tensor_copy(out, in_), reads=[in_], writes=[out])

    def memset(self, eng, out, val):
        return self.add(eng, lambda e: e.memset(out, val), reads=[], writes=[out])

    def recip(self, out, in_):
        return self.add("dve", lambda e: e.reciprocal(out, in_), reads=[in_], writes=[out])

    def rsum(self, eng, out, in_):
        return self.add(eng, lambda e: e.reduce_sum(out, in_, AX.X), reads=[in_], writes=[out])

    def dma(self, queue, out, in_, carry=False, reads=None, writes=None, **kw):
        op = self.add(None, lambda e: e.dma_start(out=out, in_=in_, **kw),
                      reads=[in_] if reads is None else reads,
                      writes=[out] if writes is None else writes, queue=queue)
        if not carry:
            self.pending_dma[op.eng].append(op)
        return op

    def fence(self, eng, reads):
        return self.add(eng, None, reads=reads, writes=[])

    def flush(self):
        nc = self.nc
        sems, dsems = self.sems, self.dsems
        for e in ENGS:
            for op in self.ops[e]:
                if op.is_dma:
                    n = self.dma_pool[op.queue]
                    op.sem = dsems[op.queue][op.seq % n]
                    op.val = 16 * (op.seq // n + 1)
                elif op.emit is not None and op.ndep > 0:
                    self.sig[e] += 1
                    op.sem = sems[e]
                    op.val = self.sig[e]

        def emit_engine(ename, eng):
            waited = self.waited[ename]

            def wait_for(dlist):
                need = {}
                for d in dlist:
                    if d.sem is None:
                        continue
                    k = id(d.sem)
                    if k not in need or need[k][1] < d.val:
                        need[k] = (d.sem, d.val)
                for k, (s_, v) in need.items():
                    if waited.get(k, 0) >= v:
                        continue
                    eng.wait_ge(s_, v)
                    waited[k] = v

            for op in self.ops[ename]:
                wait_for(op.deps)
                if op.emit is None:
                    continue
                ins = op.emit(eng)
                if op.is_dma:
                    ins.then_inc(op.sem, 16)
                elif op.sem is not None:
                    ins.then_inc(op.sem, 1)
            wait_for(self.pending_dma[ename])
            self.pending_dma[ename] = []

        with nc.Block() as block:
            @block.tensor
            def _(e):
                emit_engine("pe", e)

            @block.scalar
            def _(e):
                emit_engine("act", e)

            @block.vector
            def _(e):
                emit_engine("dve", e)

            @block.gpsimd
            def _(e):
                emit_engine("pool", e)

            @block.sync
            def _(e):
                emit_engine("sp", e)
        self.ops = {e: [] for e in ENGS}
        self.nblocks += 1


import math
import numpy as np
import ml_dtypes
from contextlib import ExitStack
import concourse.bass as bass
import concourse.mybir as mybir
from concourse.bass_utils import run_bass_kernel_spmd

D = 1024
NX = 2048
NCTX = 256
NT = NX + NCTX
ALPHA = (2.0 * 2) ** 0.25
EPS = 1e-6
ATTN_SCALE = 64 ** -0.5
NEG = -30000.0
CHUNKS_ALL = [(0, 512), (512, 512), (1024, 512), (1536, 512), (2048, 256)]
CHUNKS_X = CHUNKS_ALL[:4]


def host_consts():
    f32 = np.float32
    pos = np.arange(NX)
    row = (pos // 64).astype(f32)
    col = (pos % 64).astype(f32)
    half = 32
    freqs = np.power(f32(10000.0), -np.arange(0, half, 2, dtype=f32) / f32(half)).astype(f32)

    def axis_angles(p):
        a = p[:, None] * freqs[None, :]
        return np.concatenate([a, a], axis=-1)

    ang = np.concatenate([axis_angles(row), axis_angles(col)], axis=-1).astype(f32)
    cosT = np.tile(np.cos(ang).astype(f32).T, (2, 1)).copy()
    sinT = np.tile(np.sin(ang).astype(f32).T, (2, 1)).copy()
    ident = np.eye(128, dtype=f32)
    Rm = np.zeros((128, 128), f32)
    for m in range(128):
        if m % 32 < 16:
            Rm[m + 16, m] = -1.0
        else:
            Rm[m - 16, m] = 1.0
    bones = np.zeros((128, 128), f32)
    bones[:64, :64] = 1.0
    bones[64:, 64:] = 1.0
    cmat = np.concatenate([ident, Rm, bones], axis=1)
    k = np.arange(128)
    angc = 2 * np.pi * ((k[:, None] * k[None, :]) % 128) / 128.0
    CS = np.concatenate([np.cos(angc), -np.sin(angc)], axis=1) / np.sqrt(128.0)
    t = np.arange(NX, dtype=np.int64)
    angt = 2 * np.pi * ((t[:, None] * t[None, :]) % NX) / float(NX)
    Ct = (np.cos(angt) / np.sqrt(float(NX))).astype(f32).astype(ml_dtypes.bfloat16)
    St = (np.sin(angt) / np.sqrt(float(NX))).astype(f32).astype(ml_dtypes.bfloat16)
    return dict(cosT=cosT, sinT=sinT, cmat=cmat, CS=CS.astype(f32), Ct=Ct, St=St)


TI0 = 9
NTI = 9 + 8 + 8
TF0 = NTI + 9
NTILE = NTI + 9 + 15 + 8


def rpb_table(rpb):
    H = rpb.shape[0]
    cp = np.arange(64)[:, None]
    c = np.arange(64)[None, :]
    cs = np.clip(c - 8, 0, 48)
    valid = (cp >= cs) & (cp < cs + 16)
    dc = np.clip(cp - c + 15, 0, 30)
    tiles = np.full((H, NTILE, 64, 64), NEG, np.float32)

    def tile_for(idx):
        dr = 7 - idx
        g = rpb[:, dr + 7][:, dc]
        return np.where(valid[None], g, np.float32(NEG))

    for idx in range(4, 12):
        tiles[:, TI0 + idx - 4] = tile_for(idx)
    for idx in range(0, 15):
        tiles[:, TF0 + idx] = tile_for(idx)
    t = tiles.transpose(0, 2, 1, 3)
    sh = np.full_like(t, NEG)
    sh[:, :, 1:, :] = t[:, :, :-1, :]
    return np.ascontiguousarray(np.concatenate([t, sh], axis=1)).astype(ml_dtypes.bfloat16)


def natten_segments(kb, c):
    rp = 2 * kb
    segs = []
    rows = list(range(8 * c, 8 * c + 8))
    i = 0
    while i < 8:
        r = rows[i]
        j = i
        if 4 <= r <= 28:
            while j < 8 and 4 <= rows[j] <= 28:
                j += 1
            pos0 = TI0 + (r - rp + 7) - 4
            assert 1 <= pos0 and pos0 + (j - i) <= NTI, (kb, c, pos0)
            segs.append((i, j - i, pos0))
        else:
            lo = r <= 3
            while j < 8 and ((rows[j] <= 3) if lo else (rows[j] >= 29)):
                j += 1
            w0 = 0 if lo else 24
            if w0 <= rp < w0 + 8:
                pos0 = TF0 + (r - rp + 7)
                assert TF0 <= pos0 - 1 and pos0 + (j - i) <= TF0 + 15, (kb, c, pos0)
            else:
                pos0 = NTI + 1
            segs.append((i, j - i, pos0))
        i = j
    return segs


class K:
    pass


def build_program(stop_after=99, debug=False, sub=99):
    nc = bass.Bass("TRN2", target_bir_lowering=False)
    k = K()
    k.nc = nc
    k.sub = sub
    k.fmp = FMPipe()
    k.ut_it = 0

    def din(name, shape, dt=F32):
        return nc.dram_tensor(name, list(shape), dt, kind="ExternalInput").ap()

    k.x_d = din("x", [NX, D])
    k.ctx_d = din("ctx", [NCTX, D])
    k.c_d = din("c", [8, 128])
    k.cctx_d = din("c_ctx", [8, 128])
    k.modw_d = din("mod_w", [2, D, 6 * D])
    k.modb48_d = din("mod_b48", [2, 48, 128])
    k.modb6_d = din("mod_b6", [2, 6, D])
    k.lng_d = din("ln_g", [2, 2, D])
    k.lnb_d = din("ln_b", [2, 2, D])
    k.w1_d = din("ffn_w1", [2, D, 4 * D])
    k.w2_d = din("ffn_w2", [2, 4 * D, D])
    k.abwin_d = din("ab_w_in", [D, 2304])
    k.abwout_d = din("ab_w_out", [D, D])
    k.small_d = din("small", [128, 4])
    k.lam_d = din("b_lambda", [1, 256])
    k.cdwin_d = din("cd_w_in", [D, 2048])
    k.cdwout_d = din("cd_w_out", [D, D])
    k.rpbt_d = din("rpbt", [8, 128, NTILE * 64], BF16)
    k.cosT_d = din("cosT", [128, NX])
    k.sinT_d = din("sinT", [128, NX])
    k.cmat_d = din("cmat", [128, 384])
    k.CS_d = din("CS", [128, 256])
    k.Ct_d = din("Ct", [NX, NX], BF16)
    k.St_d = din("St", [NX, NX], BF16)
    k.out_d = nc.dram_tensor("out", [NX, D], F32, kind="ExternalOutput").ap()
    k.QS_d = nc.dram_tensor("qs_scratch", [8, 128, NT], BF16, kind="Internal").ap()
    k.GS_d = nc.dram_tensor("gs_scratch", [2, 2, 2, D], F32, kind="Internal").ap()
    k.MW1_d = nc.dram_tensor("modw1_bf16", [6, D, D], BF16, kind="Internal").ap()
    if debug:
        k.dbg_d = nc.dram_tensor("dbg", [128, 18, D], F32, kind="ExternalOutput").ap()

    untracked = ["x", "ctx", "c", "c_ctx", "mod_w", "mod_b48", "mod_b6", "ln_g", "ln_b", "ffn_w1", "ffn_w2",
                 "ab_w_in", "ab_w_out", "small", "b_lambda", "cd_w_in", "cd_w_out", "rpbt", "cosT", "sinT",
                 "cmat", "CS", "Ct", "St"]

    with ExitStack() as g:
        P = Prog(nc, g, untracked=untracked)
        k.P = P
        sb = lambda n, s, d: g.enter_context(nc.sbuf_tensor(n, s, d))
        k.XR = sb("XR", [128, 18, D], F32)
        k.cmat = sb("cmat_sb", [128, 384], F32)
        k.identb = sb("identb", [128, 128], BF16)[:]
        k.cmatb = sb("cmatb", [128, 256], BF16)
        k.Rmb = k.cmatb[:, 0:128]
        k.bonesb = k.cmatb[:, 128:256]
        k.FM = sb("FM", [128, 2, 2, 4, 8], F32)
        k.small = sb("small_sb", [128, 4], F32)
        k.epsc = sb("epsc", [128, 1], F32)
        k.misc = sb("misc", [128, 16], F32)
        k.sc2 = sb("sc2", [128, 8, 2], BF16)
        k.zb = sb("zb", [128, 384], BF16)
        k.mbT = sb("mbT", [128, 2, 48], F32)
        k.PS = g.enter_context(nc.psum_tensor("PS", [128, 8, 512], F32))
        ARENA_BYTES = 130 * 1024
        at = sb("arena", [128, ARENA_BYTES // 2], BF16)
        k.A = Arena(at, ARENA_BYTES)
        k.ident = k.cmat[:, 0:128]
        k.Rm = k.cmat[:, 128:256]
        k.bones = k.cmat[:, 256:384]

        for i in range(4):
            P.dma("sp", k.XR[:, 4 * i:4 * i + 4, :],
                  k.x_d[512 * i:512 * (i + 1), :].rearrange("(t p) d -> p t d", p=128))
        P.dma("sp", k.XR[:, 16:18, :], k.ctx_d.rearrange("(t p) d -> p t d", p=128))
        P.dma("sp", k.cmat[:], k.cmat_d)
        P.dma("sp", k.small[:], k.small_d)
        P.memset("dve", k.epsc[:], EPS)
        P.copy("dve", k.identb, k.ident)
        P.memset("pool", k.zb[:], 0.0)
        P.copy("dve", k.cmatb[:], k.cmat[:, 128:384])

        mod_prologue(k)
        phase_mod(k, 0, (0, 1))

        def ln_with_ut(l_ln, j, nb_ln, l_ut, vsh, vsc, nb_ut, mod_l=None, mod_vs=(), out=False, mod_early=False):
            pre = k.A.tile([8, nb_ut * 128], BF16) if nb_ut else None
            tiles = [k.A.tile([8, 1024], BF16) for _ in range(3)] if mod_vs else []
            ring = [t for _, t in tiles]
            groups = {t0 + nt - 1: (t0, nt) for (t0, nt) in ut_groups(nb_ut)} if nb_ut else {}
            mv = list(mod_vs)

            def hook(tb):
                if mv and (mod_early or tb % 3 == 0):
                    mod_step(k, mod_l, mv.pop(0), ring, banks=(6, 7, 6, 7))
                if tb in groups:
                    assert not (mod_early and mv)
                    t0, nt = groups[tb]
                    build_UT_group(k, pre[1], l_ut, vsh, vsc, t0, nt)

            layer_norm(k, l_ln, j, nb_ln, out=out, hook=hook)
            while mv:
                mod_step(k, mod_l, mv.pop(0), ring, banks=(6, 7, 6, 7))
            for o, _ in tiles:
                k.A.free(o)
            return pre

        if stop_after >= 1:
            layer0_mixer(k)
        if stop_after >= 3:
            pre = ln_with_ut(0, 0, 18, 0, 2, 3, 18, mod_l=1, mod_vs=(0, 1, 2))
            ffn(k, 0, 18, pre)
            pre = ln_with_ut(0, 1, 18, 1, 0, 1, 18)
        if stop_after >= 4:
            layer1_mixer(k, pre)
        if stop_after >= 6:
            pre = ln_with_ut(1, 0, 16, 1, 2, 3, 16, mod_l=1, mod_vs=(3, 4, 5), mod_early=True)
            ffn(k, 1, 16, pre)
            ln_with_ut(1, 1, 16, 1, 0, 1, 0, out=True)
        if debug:
            P.dma("sp", k.dbg_d, k.XR[:])
            P.fence("sp", [k.dbg_d])
        P.fence("sp", [k.out_d])
        P.flush()
    return nc


def ps_bank(k, b, n=512):
    return k.PS[:, b, 0:n]


def mod_prologue(k):
    P, A, PS = k.P, k.A, k.PS
    o_c16, c16 = A.tile([128], F32)
    o_m48, m48 = A.tile([2, 128], F32)
    sc2 = k.sc2
    mbT = k.mbT
    P.dma("sp", c16[0:8, :], k.c_d)
    P.dma("sp", c16[8:16, :], k.cctx_d)
    P.tr(PS[:, 0, 0:16], c16[0:16, :], k.ident[0:16, 0:16])
    P.act(sc2[:, :, 0], PS[:, 0, 0:8], AF.Silu)
    P.act(sc2[:, :, 1], PS[:, 0, 8:16], AF.Silu)
    for l in range(2):
        P.dma("sp", m48[0:48, l, :], k.modb48_d[l])
        P.tr(PS[:, 1, l * 48:(l + 1) * 48], m48[0:48, l, :], k.ident[0:48, 0:48])
        P.copy("dve", mbT[:, l, :], PS[:, 1, l * 48:(l + 1) * 48])
    A.free(o_c16)
    A.free(o_m48)
    k.mod_it = 0


def mod_step(k, l, v, wv_ring, banks=(2, 3, 4, 5)):
    P, A, PS = k.P, k.A, k.PS
    sc2, mbT = k.sc2, k.mbT
    it = k.mod_it
    k.mod_it += 1
    Wv = wv_ring[it % len(wv_ring)]
    if l == 0:
        src = k.modw_d[l, :, v * 1024:(v + 1) * 1024].rearrange("(k p) n -> p k n", p=128)
        P.dma("poolq", Wv[:, 0:4, :], src[:, 0:4, :])
        P.dma("poolq", Wv[:, 4:8, :], src[:, 4:8, :])
    else:
        src = k.MW1_d[v].rearrange("(k p) n -> p k n", p=128)
        P.dma("sp", Wv[:, 0:4, :], src[:, 0:4, :], reads=[("MW1", v)])
        P.dma("sp", Wv[:, 4:8, :], src[:, 4:8, :], reads=[("MW1", v)])
    if v in (0, 1, 3, 4):
        vi = {0: 0, 1: 1, 3: 2, 4: 3}[v]
        bank = banks[it % 2]
        for cc in range(8):
            for kk in range(8):
                P.mm(PS[:, bank, cc * 2:cc * 2 + 2], Wv[:, kk, cc * 128:(cc + 1) * 128], sc2[:, kk, :],
                     kk == 0, kk == 7)
        psv = PS[:, bank, 0:16].rearrange("p (c w) -> p c w", w=2)
        for w in range(2):
            P.stt("dve", k.FM[:, l, w, vi, :], psv[:, :, w], 1.0 if v in (1, 4) else 0.0,
                  mbT[:, l, v * 8:(v + 1) * 8], ALU.add, ALU.add)
    else:
        vi = 0 if v == 2 else 1
        o_brow, brow = A.tile([1024], F32)
        o_grow, grow = A.tile([2, 1024], F32)
        P.dma("sp", brow[0:1, :], k.modb6_d[l, v:v + 1, :])
        for w in range(2):
            if l == 1 and w == 1:
                continue
            for hf in range(2):
                bank = banks[2 + hf]
                for kk in range(8):
                    P.mm(PS[0:1, bank, 0:512], sc2[:, kk, w:w + 1], Wv[:, kk, hf * 512:(hf + 1) * 512],
                         kk == 0, kk == 7)
                P.tt("dve", grow[0:1, w, hf * 512:(hf + 1) * 512], PS[0:1, bank, 0:512],
                     brow[0:1, hf * 512:(hf + 1) * 512], ALU.add)
            P.dma("sp", k.GS_d[l, w, vi:vi + 1, :], grow[0:1, w, :], writes=[("GS", l, w, vi)])
        A.free(o_brow)
        A.free(o_grow)


def phase_mod(k, l, vs, ring=None, banks=(2, 3, 4, 5)):
    P, A = k.P, k.A
    tiles = None
    if ring is None:
        tiles = [A.tile([8, 1024], BF16) for _ in range(2)]
        ring = [t for _, t in tiles]
    for v in vs:
        mod_step(k, l, v, ring, banks=banks)
    if tiles is not None:
        for o, _ in tiles:
            A.free(o)


def load_gate(k, dst, l, w, vi, queue="sp"):
    k.P.dma(queue, dst, k.GS_d[l, w, vi:vi + 1, :].partition_broadcast(128), reads=[("GS", l, w, vi)])


def build_UT_group(k, UT, l, v_shift, v_scale, t0, nt):
    P, PS = k.P, k.PS
    w = 0 if t0 < 16 else 1
    for kk in range(8):
        it = k.ut_it
        k.ut_it += 1
        bank = PS[:, it % 2, :]
        for j in range(nt):
            P.tr(bank[:, j * 128:(j + 1) * 128], k.XR[:, t0 + j, kk * 128:(kk + 1) * 128], k.ident)
        dst = UT[:, kk, t0 * 128:(t0 + nt) * 128]
        sc = k.FM[:, l, w, v_scale, kk:kk + 1]
        bi = k.FM[:, l, w, v_shift, kk:kk + 1]
        if it % 2 == 0:
            P.act(dst, bank[:, 0:nt * 128], AF.Identity, bias=bi, scale=sc)
        else:
            P.ts("dve", dst, bank[:, 0:nt * 128], sc, bi, ALU.mult, ALU.add)


def ut_groups(nblocks):
    groups = [(0, 4), (4, 4), (8, 4), (12, 4)]
    if nblocks > 16:
        groups.append((16, 2))
    return groups


def build_UT(k, UT, l, v_shift, v_scale, nblocks):
    for (t0, nt) in ut_groups(nblocks):
        build_UT_group(k, UT, l, v_shift, v_scale, t0, nt)


class FMPipe:
    def __init__(self):
        self.items = []

    def push(self, st):
        self.items.append([st, 0])
        self._step()

    def _step(self):
        for it in list(self.items[::-1]):
            st, idx = it
            st[idx]()
            it[1] += 1
        self.items = [it for it in self.items if it[1] < 3]

    def drain(self):
        while self.items:
            self._step()


def fm_block(k, UT, wt, chunks, dest_fn, tmps, norm_col=None, rope=False, after=None, itbase=0, pre_scale=None):
    P, PS = k.P, k.PS
    nt = len(tmps)
    for ci, (c0, n) in enumerate(chunks):
        it = itbase + ci
        bank = PS[:, it % 2, 0:n]
        is_ctx = c0 >= NX
        do_rope = rope and not is_ctx
        q32, sq, t1, qb = [t[:, 0:n] for t in tmps[it % nt]]
        bn = PS[:, 2 + it % 2, 0:n]
        br = PS[:, 4 + it % 2, 0:n]

        def st1(bank=bank, c0=c0, n=n, ci=ci, do_rope=do_rope, q32=q32, qb=qb):
            for kk in range(8):
                P.mm(bank, wt[:, kk, :], UT[:, kk, c0:c0 + n], kk == 0, kk == 7)
            if norm_col is None and not do_rope:
                dest = dest_fn(ci, c0, n)
                if pre_scale is not None:
                    P.add("act", lambda e, o_=dest, i_=bank, m_=pre_scale: e.mul(o_, i_, m_), reads=[bank],
                          writes=[dest])
                else:
                    P.copy("act", dest, bank)
                if after is not None:
                    after(ci, c0, n, dest)
                return
            P.copy("act", q32, bank)
            if norm_col is not None:
                P.act(qb, bank, AF.Square)
            else:
                P.copy("act", qb, bank)

        def st2(bn=bn, q32=q32, sq=sq, qb=qb, do_rope=do_rope):
            if norm_col is None:
                return
            P.mm(bn, k.bonesb, qb, True, True)
            P.act(sq, bn, AF.Ln, bias=k.epsc[:, 0:1], scale=1.0 / 64.0)
            P.act(sq, sq, AF.Exp, scale=-0.5)
            P.stt("dve", q32, q32, norm_col, sq, ALU.mult, ALU.mult)
            if do_rope:
                P.copy("act", qb, q32)

        def st3(br=br, q32=q32, sq=sq, t1=t1, qb=qb, do_rope=do_rope, c0=c0, n=n, ci=ci):
            if norm_col is None and not do_rope:
                return
            dest = dest_fn(ci, c0, n)
            if do_rope:
                P.mm(br, k.Rmb, qb, True, True)
                P.tt("pool", t1, q32, k.cosT[:, c0:c0 + n], ALU.mult)
                P.tt("dve", sq, br, k.sinT[:, c0:c0 + n], ALU.mult)
                P.tt("dve", dest, t1, sq, ALU.add)
            else:
                P.copy("act", dest, q32)
            if after is not None:
                after(ci, c0, n, dest)

        k.fmp.push([st1, st2, st3])


def alloc_fm_tmps(k, nsets=3):
    offs, tmps = [], []
    for _ in range(nsets):
        s = []
        for _ in range(3):
            o, t = k.A.tile([512], F32)
            offs.append(o)
            s.append(t)
        o, t = k.A.tile([512], BF16)
        offs.append(o)
        s.append(t)
        tmps.append(s)
    return offs, tmps


def load_w_cols(k, dst, wd, c0, n, queue="poolq"):
    k.P.dma(queue, dst, wd[:, c0:c0 + n].rearrange("(k p) n -> p k n", p=128))


class Pipe:
    def __init__(self, L=2):
        self.q = []
        self.L = L
        self.tasks = []

    def push(self, S, E_PV, post=None):
        bank = S()
        self.q.append((bank, E_PV, post))
        while len(self.q) > self.L:
            self._pop()

    def _pop(self):
        bank, E_PV, post = self.q.pop(0)
        E_PV(bank)
        if self.tasks:
            self.tasks.pop(0)()
        if post is not None:
            post()

    def drain(self):
        while self.q:
            self._pop()
        while self.tasks:
            self.tasks.pop(0)()


def attn_head_chunk(k, kT, qT, c0, n, kbs, v_fn, po_fn, nv, bias_fn=None, cnt=[0], bank_first=(0,), post=None,
                    exp_scale=ATTN_SCALE, bias_mm=None, zero_regs=()):
    P, PS = k.P, k.PS
    nq = n // 128
    nk = len(kbs)

    def S(kb):
        bank = PS[:, cnt[0] % 3, 0:n]
        extra = bias_mm(kb) if bias_mm is not None else None
        P.mm(bank, kT[:, kb * 128:(kb + 1) * 128], qT[:, c0:c0 + n], True, not extra)
        if extra:
            for ei, (col0, ncol, rhs) in enumerate(extra):
                P.mm(bank[:, col0:col0 + ncol], k.identb, rhs, False, ei == len(extra) - 1)
        cnt[0] += 1
        return bank

    def E_PV(bank, idx, kb):
        pt = k.pt_ring[k.pt_cnt % len(k.pt_ring)][:, 0:n]
        k.pt_cnt += 1
        sbias = bias_fn(kb, bank, n) if bias_fn is not None else None
        if sbias is not None:
            P.act(pt, sbias, AF.Exp)
        else:
            P.act(pt, bank, AF.Exp, scale=exp_scale)
        if idx == 0:
            for (reg, ncol) in zero_regs:
                P.mm(reg, k.zb[:, 0:128], k.zb[:, 0:ncol], True, False)
        for qs in range(nq):
            P.mm(po_fn(qs), pt[:, qs * 128:(qs + 1) * 128], v_fn(kb), False,
                 idx == nk - 1 and (qs == nq - 1 or (qs + 1) in bank_first))

    for idx, kb in enumerate(kbs):
        k.pipe.push(lambda kb=kb: S(kb), lambda bank, idx=idx, kb=kb: E_PV(bank, idx, kb),
                    post if idx == nk - 1 else None)


def layer0_mixer(k):
    P, A, PS = k.P, k.A, k.PS
    l = 0
    o_UT, UT = A.tile([8, NT], BF16)
    o_cos, cosT = A.tile([NX], F32)
    o_sin, sinT = A.tile([NX], F32)
    k.cosT, k.sinT = cosT, sinT
    P.dma("sp", cosT, k.cosT_d)
    P.dma("sp", sinT, k.sinT_d)
    build_UT(k, UT, l, 0, 1, 18)
    if k.sub <= 0:
        return
    toffs, tmps = alloc_fm_tmps(k)
    qn_col = k.small[:, 0:1]
    kn_col = k.small[:, 1:2]
    wq_ring = [A.tile([8, 256], BF16) for _ in range(2)]
    qst_ring = [A.tile([512], BF16) for _ in range(3)]
    mtiles = [A.tile([8, 1024], BF16) for _ in range(2)]
    mring = [t for _, t in mtiles]
    mvs = [2, 3, 4, 5]
    itb = 0
    load_w_cols(k, wq_ring[0][1], k.abwin_d, 0, 256)
    for pi in range(4):
        _, wq = wq_ring[pi % 2]
        if pi + 1 < 4:
            load_w_cols(k, wq_ring[(pi + 1) % 2][1], k.abwin_d, (pi + 1) * 256, 256)
        if mvs:
            mod_step(k, 0, mvs.pop(0), mring, banks=(6, 7, 6, 7))
        for sub in range(2):
            i = pi * 2 + sub
            st = {"c": 0}

            def dest_fn(ci, c0, n, itb=itb):
                return qst_ring[(itb + ci) % 3][1][:, 0:n]

            def after(ci, c0, n, dest, i=i):
                P.dma("sp", k.QS_d[i, :, c0:c0 + n], dest, writes=[("QS", i, ci)])

            fm_block(k, UT, wq[:, :, sub * 128:(sub + 1) * 128], CHUNKS_ALL, dest_fn, tmps,
                     norm_col=qn_col if i < 4 else None, rope=True, after=after, itbase=itb)
            itb += 5
    k.fmp.drain()
    for o, _ in wq_ring + qst_ring + mtiles:
        A.free(o)
    if k.sub <= 1:
        return
    o_KT, KT = A.tile([6, NT], BF16, top=True)
    o_wk, wk = A.tile([8, 512], BF16)
    load_w_cols(k, wk, k.abwin_d, 1280, 512)
    o_wkd, wkd = A.tile([2, 8, 128], BF16)
    for kvh in range(2):
        for hf in range(2):
            load_w_cols(k, wkd[:, kvh, :, hf * 64:(hf + 1) * 64], k.abwin_d, 1024 + kvh * 64, 64)
    for j in range(6):
        wt = wkd[:, j, :, :] if j < 2 else wk[:, :, (j - 2) * 128:(j - 1) * 128]
        fm_block(k, UT, wt, CHUNKS_ALL, lambda ci, c0, n, j=j: KT[:, j, c0:c0 + n], tmps,
                 norm_col=kn_col if j < 2 else None, rope=True, itbase=itb)
        itb += 5
    k.fmp.drain()
    A.free(o_wk)
    A.free(o_wkd)
    for o in toffs:
        A.free(o)
    A.free(o_cos)
    A.free(o_sin)
    if k.sub <= 2:
        return
    o_Va, Va1 = A.tile([18, 2, 80], BF16, top=True)
    o_Vb, Vb1 = A.tile([18, 4, 144], BF16, top=True)
    o_wv, wv = A.tile([8, 640], BF16)
    load_w_cols(k, wv[:, :, 0:128], k.abwin_d, 1152, 128)
    load_w_cols(k, wv[:, :, 128:640], k.abwin_d, 1792, 512)
    P.memset("pool", Va1[:, :, :, 64:65], 1.0)
    P.memset("pool", Vb1[:, :, :, 128:129], 1.0)
    for t in range(18):
        ba = PS[:, 2 * (t % 2), 0:128]
        bb = PS[:, 2 * (t % 2) + 1, 0:512]
        for kk in range(8):
            P.mm(ba, UT[:, kk, t * 128:(t + 1) * 128], wv[:, kk, 0:128], kk == 0, kk == 7)
        for kk in range(8):
            P.mm(bb, UT[:, kk, t * 128:(t + 1) * 128], wv[:, kk, 128:640], kk == 0, kk == 7)
        P.copy("act", Va1[:, t, :, 0:64], ba.rearrange("p (h d) -> p h d", h=2))
        P.copy("dve", Vb1[:, t, :, 0:128], bb.rearrange("p (h d) -> p h d", h=4))
    A.free(o_wv)
    A.free(o_UT)

    if k.sub <= 3:
        return
    o_lb, lb = A.tile([256], F32)
    P.dma("sp", lb, k.lam_d.partition_broadcast(128))
    lb4 = lb.rearrange("p (a b d) -> p a b d", a=2, b=2)
    o_lp, lp = A.tile([2, 64], F32)
    P.tt("dve", lp, lb4[:, :, 0, :], lb4[:, :, 1, :], ALU.mult)
    lsum = k.misc[:, 0:2]
    P.rsum("dve", lsum, lp)
    P.act(lsum, lsum, AF.Exp)
    lam_init = 0.8 - 0.6 * math.exp(0.0)
    neglam = k.misc[:, 2:3]
    P.stt("dve", neglam, k.misc[:, 1:2], -lam_init, k.misc[:, 0:1], ALU.add, ALU.subtract)
    rowsc = k.misc[:, 3:4]
    P.ts("dve", rowsc, k.small[:, 2:3], 1.0 - lam_init, None, ALU.mult)
    A.free(o_lb)
    A.free(o_lp)
    setup = attention_setup(k)
    o_wox, WoX = A.tile([8, D], BF16)
    o_woc, WoC = A.tile([8, D], BF16)
    o_g, Gt = A.tile([2, D], F32)
    o_ws, wst = A.tile([2, D], F32)
    load_gate(k, Gt[:, 0, :], l, 0, 0)
    load_gate(k, Gt[:, 1, :], l, 1, 0)
    for kk in range(8):
        P.dma("sp", wst[:, kk % 2, :], k.abwout_d[kk * 128:(kk + 1) * 128, :])
        rs = rowsc if kk >= 4 else 1.0
        P.stt("dve", WoX[:, kk, :], wst[:, kk % 2, :], rs, Gt[:, 0, :], ALU.mult, ALU.mult)
        P.stt("dve", WoC[:, kk, :], wst[:, kk % 2, :], rs, Gt[:, 1, :], ALU.mult, ALU.mult)
    A.free(o_g)
    A.free(o_ws)
    if k.sub <= 4:
        return
    k.bg_dmas = [(lambda v=v: P.dma("poolq", k.MW1_d[v], k.modw_d[1, :, v * 1024:(v + 1) * 1024],
                                    writes=[("MW1", v)], carry=True)) for v in range(6)]
    attention_core(k, 0, KT, (Va1, Vb1), (WoX, WoC), setup)
    while k.bg_dmas:
        k.bg_dmas.pop(0)()
    for o in (o_KT, o_Va, o_Vb, o_wox, o_woc):
        A.free(o)


def attention_setup(k):
    P, A = k.P, k.A
    qt_ring = [A.tile([NT], BF16) for _ in range(4)]
    for sl in range(2):
        P.memset("pool", qt_ring[2 * sl][1][64:128, :], 0.0)
        P.memset("pool", qt_ring[2 * sl + 1][1][0:64, :], 0.0)

    def load_q(i):
        sl = i % 2
        rd = [("QS", i, ci) for ci in range(5)]
        P.dma("sp", qt_ring[2 * sl][1][0:64, :], k.QS_d[i, 0:64, :], reads=rd)
        P.dma("sp", qt_ring[2 * sl + 1][1][64:128, :], k.QS_d[i, 64:128, :], reads=rd)

    load_q(0)
    return qt_ring, load_q


def attention_core(k, l, KT, V, Wo, setup):
    P, A, PS = k.P, k.A, k.PS
    Va1, Vb1 = V
    WoX, WoC = Wo
    qt_ring, load_q = setup

    pt_tiles = [A.tile([512], BF16) for _ in range(4)]
    k.pt_ring = [t for _, t in pt_tiles]
    k.pt_cnt = 0
    k.pipe = Pipe(2)
    o_ot, otok_r = A.tile([3, 4, 128], BF16)
    o_oc, otc_r = A.tile([2, 512], BF16)
    o_ta, tacc_r = A.tile([2, 4, 128], F32)
    o_sq, sqt = A.tile([4, 128], F32)
    o_rc, rcs = A.tile([4, 16], F32)
    PST = PS[:, 5, :].bitcast(BF16)
    chunks = CHUNKS_ALL
    pocnt = [0]
    ycnt = [0]
    cc = 0

    def finish_chunk(i, c0, n, otok, otc, rstd_col):
        nq = n // 128

        def t_transpose():
            for qs in range(nq):
                P.tr(PST[:, qs * 128:(qs + 1) * 128], otok[:, qs, :], k.identb)
            P.copy("dve", otc[:, 0:n], PST[:, 0:n])

        k.pipe.tasks.append(t_transpose)
        for qs in range(nq):
            tb = c0 // 128 + qs
            W = WoX if tb < 16 else WoC
            for h2 in range(2):
                def t_y(qs=qs, tb=tb, W=W, h2=h2):
                    bank = PS[:, 6 + ycnt[0] % 2, :]
                    ycnt[0] += 1
                    P.mm(bank, otc[:, qs * 128:(qs + 1) * 128], W[:, i, h2 * 512:(h2 + 1) * 512], True, True)
                    xs = k.XR[:, tb, h2 * 512:(h2 + 1) * 512]
                    if rstd_col is not None:
                        P.stt("dve", xs, bank, rstd_col[:, qs:qs + 1], xs, ALU.mult, ALU.add)
                    elif i == 0:
                        P.stt("dve", xs, xs, ALPHA, bank, ALU.mult, ALU.add)
                    else:
                        P.tt("dve", xs, bank, xs, ALU.add)

                k.pipe.tasks.append(t_y)

    for i in range(8):
        qth = (qt_ring[2 * (i % 2)][1], qt_ring[2 * (i % 2) + 1][1])
        if i + 1 < 8:
            load_q(i + 1)
        if i >= 1 and k.bg_dmas:
            k.bg_dmas.pop(0)()
        for (c0, n) in chunks:
            nq = n // 128
            kbs = list(range(18)) if c0 < NX else [16, 17]
            otok = otok_r[:, cc % 3]
            otc = otc_r[:, cc % 2, :]
            tacc = tacc_r[:, cc % 2]
            rcv = rcs[:, cc % 4, :]
            cc += 1
            while len(k.pipe.tasks) > 17:
                k.pipe.tasks.pop(0)()
            if i < 4:
                kvh = i // 2
                for hf in range(2):
                    po = PS[:, 3 + pocnt[0] % 2, 0:260].rearrange("p (q e) -> p q e", e=65)
                    pocnt[0] += 1

                    def post(hf=hf, po=po, otok=otok, otc=otc, nq=nq, rcv=rcv, i=i, c0=c0, n=n):
                        rc = rcv[:, 4 * hf:4 * hf + 4]
                        P.recip(rc[:, 0:nq], po[:, 0:nq, 64])
                        P.tt("dve", otok[:, 0:nq, 64 * hf:64 * hf + 64], po[:, 0:nq, 0:64],
                             rc[:, 0:nq].unsqueeze(2).to_broadcast([128, nq, 64]), ALU.mult)
                        if hf == 1:
                            finish_chunk(i, c0, n, otok, otc, None)

                    attn_head_chunk(k, KT[:, kvh, :], qth[hf], c0, n, kbs,
                                    lambda kb, kvh=kvh: Va1[:, kb, kvh, 0:65], lambda qs, po=po: po[:, qs, :], 65,
                                    post=post, zero_regs=[(po.rearrange("p q e -> p (q e)")[:, 0:65 * nq], 65 * nq)])
            else:
                h = i - 4
                for j in range(2):
                    poA = PS[:, 3, 0:258].rearrange("p (q e) -> p q e", e=129)
                    poB = PS[:, 4, 0:258].rearrange("p (q e) -> p q e", e=129)
                    pof = lambda qs, poA=poA, poB=poB: (poA if qs < 2 else poB)[:, qs % 2, :]

                    def post(j=j, pof=pof, otok=otok, otc=otc, tacc=tacc, nq=nq, rcv=rcv, i=i, c0=c0, n=n,
                             poA_=poA, poB_=poB):
                        rc = rcv[:, 0:4]
                        rc1 = rcv[:, 4:8]
                        ss4 = rcv[:, 8:12]
                        pA = pof(0)
                        banks2 = [(0, poA_)] + ([(2, poB_)] if nq > 2 else [])
                        for q0, pb in banks2:
                            P.recip(rc[:, q0:q0 + 2], pb[:, 0:2, 128])
                        if j == 0:
                            for q0, pb in banks2:
                                P.tt("dve", tacc[:, q0:q0 + 2, :], pb[:, 0:2, 0:128],
                                     rc[:, q0:q0 + 2].unsqueeze(2).to_broadcast([128, 2, 128]), ALU.mult)
                            return
                        P.ts("dve", rc1[:, 0:nq], rc[:, 0:nq], k.misc[:, 2:3], None, ALU.mult)
                        for q0, pb in banks2:
                            P.tt("dve", sqt[:, q0:q0 + 2, :], pb[:, 0:2, 0:128],
                                 rc1[:, q0:q0 + 2].unsqueeze(2).to_broadcast([128, 2, 128]), ALU.mult)
                        P.tt("dve", tacc[:, 0:nq, :], tacc[:, 0:nq, :], sqt[:, 0:nq, :], ALU.add)
                        P.tt("dve", sqt[:, 0:nq, :], tacc[:, 0:nq, :], tacc[:, 0:nq, :], ALU.mult)
                        P.rsum("dve", ss4[:, 0:nq], sqt[:, 0:nq, :])
                        P.act(ss4[:, 0:nq], ss4[:, 0:nq], AF.Ln, bias=k.epsc[:, 0:1], scale=1.0 / 128.0)
                        P.act(ss4[:, 0:nq], ss4[:, 0:nq], AF.Exp, scale=-0.5)
                        P.copy("dve", otok[:, 0:nq, :], tacc[:, 0:nq, :])
                        finish_chunk(i, c0, n, otok, otc, ss4)

                    attn_head_chunk(k, KT[:, 2 + h, :], qth[j], c0, n, kbs,
                                    lambda kb, h=h: Vb1[:, kb, h, 0:129], pof, 129, bank_first=(0, 2), post=post,
                                    zero_regs=[(pb.rearrange("p q e -> p (q e)"), 258)
                                               for pb in ([poA, poB] if nq > 2 else [poA])])
    k.pipe.drain()
    for o, _ in qt_ring:
        A.free(o)
    for o in (o_ot, o_oc, o_ta, o_sq, o_rc):
        A.free(o)
    for o, _ in pt_tiles:
        A.free(o)


def layer_norm(k, l, j, nblocks, out=False, hook=None):
    P, A = k.P, k.A
    o_g, gam = A.tile([D], F32)
    o_b, bet = A.tile([D], F32)
    P.dma("sp", gam, k.lng_d[l, j:j + 1, :].partition_broadcast(128))
    P.dma("sp", bet, k.lnb_d[l, j:j + 1, :].partition_broadcast(128))
    o_st, st = A.tile([nblocks, 2, 6], F32)
    o_mv, mv = A.tile([nblocks, 2], F32)
    o_rs, rs = A.tile([2, nblocks], F32)
    o_y, ytmp = A.tile([2, D], F32)
    for tb in range(nblocks):
        for hh in range(2):
            P.add("dve", lambda e, o_=st[:, tb, hh, :], i_=k.XR[:, tb, hh * 512:(hh + 1) * 512]: e.bn_stats(o_, i_),
                  reads=[k.XR[:, tb, hh * 512:(hh + 1) * 512]], writes=[st[:, tb, hh, :]])
        P.add("dve", lambda e, o_=mv[:, tb, :], i_=st[:, tb, :, :].rearrange("p a b -> p (a b)"): e.bn_aggr(o_, i_),
              reads=[st[:, tb, :, :]], writes=[mv[:, tb, :]])
    P.act(rs[:, 0, :], mv[:, :, 1], AF.Ln, bias=k.epsc[:, 0:1], scale=1.0)
    P.act(rs[:, 0, :], rs[:, 0, :], AF.Exp, scale=-0.5)
    P.stt("dve", rs[:, 1, :], mv[:, :, 0], -1.0, rs[:, 0, :], ALU.mult, ALU.mult)
    for tb in range(nblocks):
        r = tb % 2
        xs = k.XR[:, tb, :]
        y = ytmp[:, r, :]
        P.act(y, xs, AF.Identity, bias=rs[:, 1, tb:tb + 1], scale=rs[:, 0, tb:tb + 1])
        P.tt("dve", y, y, gam, ALU.mult)
        P.tt("dve", k.XR[:, tb, 0:640], y[:, 0:640], bet[:, 0:640], ALU.add)
        P.tt("pool", k.XR[:, tb, 640:D], y[:, 640:D], bet[:, 640:D], ALU.add)
        if out:
            P.dma("sp", k.out_d[tb * 128:(tb + 1) * 128, :], xs)
        if hook is not None:
            hook(tb)
    for o in (o_g, o_b, o_st, o_mv, o_rs, o_y):
        A.free(o)


def ffn(k, l, nblocks, pre):
    P, A, PS = k.P, k.A, k.PS
    ntok = nblocks * 128
    chunks = CHUNKS_ALL if nblocks == 18 else CHUNKS_X
    o_u, u2T = pre
    o_h, hT = A.tile([8, ntok], BF16)
    o_g, Gt = A.tile([2, D], F32)
    load_gate(k, Gt[:, 0, :], l, 0, 1)
    if nblocks > 16:
        load_gate(k, Gt[:, 1, :], l, 1, 1)
    w1r = [A.tile([8, 512], BF16) for _ in range(2)]
    w2r = [A.tile([4, D], BF16) for _ in range(2)]
    o_r, rl = A.tile([2, 512], F32)
    o_t2, tmp2 = A.tile([2, 512], F32)
    hcnt = 0
    ycnt = 0
    for qd in range(4):
        for hh in range(2):
            c0 = qd * 1024 + hh * 512
            P.dma("poolq", w1r[hh][1], k.w1_d[l, :, c0:c0 + 512].rearrange("(k p) n -> p k n", p=128))
        for hh in range(2):
            r0 = qd * 1024 + hh * 512
            P.dma("poolq", w2r[hh][1], k.w2_d[l, r0:r0 + 512, :].rearrange("(j p) n -> p j n", p=128))
        if k.sub <= 10:
            break
        for hc in range(8):
            w1t = w1r[hc // 4][1]
            for (c0, n) in chunks:
                bank = PS[:, hcnt % 3, 0:n]
                for kk in range(8):
                    P.mm(bank, w1t[:, kk, (hc % 4) * 128:(hc % 4 + 1) * 128], u2T[:, kk, c0:c0 + n], kk == 0, kk == 7)
                r = rl[:, hcnt % 2, 0:n]
                P.act(r, bank, AF.Relu)
                P.act(hT[:, hc, c0:c0 + n], r, AF.Square)
                hcnt += 1
        if k.sub <= 11:
            break
        for tb in range(nblocks):
            G = Gt[:, 0, :] if tb < 16 else Gt[:, 1, :]
            for h2 in range(2):
                bank = PS[:, 3 + ycnt % 4, :]
                for hc in range(8):
                    P.mm(bank, hT[:, hc, tb * 128:(tb + 1) * 128], w2r[hc // 4][1][:, hc % 4, h2 * 512:(h2 + 1) * 512],
                         hc == 0, hc == 7)
                t2 = tmp2[:, ycnt % 2, :]
                ycnt += 1
                P.tt("dve", t2, bank, G[:, h2 * 512:(h2 + 1) * 512], ALU.mult)
                xs = k.XR[:, tb, h2 * 512:(h2 + 1) * 512]
                if qd == 0:
                    P.stt("dve", xs, xs, ALPHA, t2, ALU.mult, ALU.add)
                else:
                    P.tt("dve", xs, xs, t2, ALU.add)
        if k.sub <= 12:
            break
    for o in (o_u, o_h, o_g, o_r, o_t2, w1r[0][0], w1r[1][0], w2r[0][0], w2r[1][0]):
        A.free(o)


def layer1_mixer(k, pre):
    P, A, PS = k.P, k.A, k.PS
    l = 1
    o_UT, UT = pre
    o_cs, CS = A.tile([256], F32)
    P.dma("sp", CS, k.CS_d)
    toffs, tmps = alloc_fm_tmps(k)
    itb = 0
    wq_ring = [A.tile([8, 256], BF16) for _ in range(2)]
    qst_ring = [A.tile([512], BF16) for _ in range(3)]
    for pi in range(2):
        _, wq = wq_ring[pi % 2]
        load_w_cols(k, wq, k.cdwin_d, pi * 256, 256)
        for sub in range(2):
            i = pi * 2 + sub

            def dest_fn(ci, c0, n, itb=itb):
                return qst_ring[(itb + ci) % 3][1][:, 0:n]

            def after(ci, c0, n, dest, i=i):
                P.dma("sp", k.QS_d[i, :, c0:c0 + n], dest, writes=[("QS", i, ci)])

            fm_block(k, UT, wq[:, :, sub * 128:(sub + 1) * 128], CHUNKS_X, dest_fn, tmps, after=after, itbase=itb,
                     pre_scale=ATTN_SCALE)
            itb += 4
    k.fmp.drain()
    for o, _ in wq_ring + qst_ring:
        A.free(o)
    o_AB, AB = A.tile([16, 4, 256], BF16, top=True)
    o_wf, wf = A.tile([8, 512], BF16)
    load_w_cols(k, wf, k.cdwin_d, 512, 512)
    abcnt = [0]
    k.fd_pending = []
    for gi in range(4):
        def dest_fn(ci, c0, n, itb=itb):
            return tmps[(itb + ci) % 3][0][:, 0:n]

        def after_now(ci, c0, n, dest, gi=gi):
            k.fd_pending.append(lambda: after(ci, c0, n, dest, gi))
            while len(k.fd_pending) > 1:
                k.fd_pending.pop(0)()

        def after(ci, c0, n, dest, gi=gi):
            for pair in range(2):
                bank = PS[:, 2 + abcnt[0] % 2, :]
                abcnt[0] += 1
                for j in range(2):
                    qs = pair * 2 + j
                    P.mm(bank[:, j * 256:(j + 1) * 256], dest[:, qs * 128:(qs + 1) * 128], CS, True, True)
                tb = c0 // 128 + pair * 2
                P.copy("dve", AB[:, tb:tb + 2, gi, :], bank.rearrange("p (a b) -> p a b", a=2))

        fm_block(k, UT, wf[:, :, gi * 128:(gi + 1) * 128], CHUNKS_X, dest_fn, tmps, after=after_now, itbase=itb)
        itb += 4
    k.fmp.drain()
    while k.fd_pending:
        k.fd_pending.pop(0)()
    A.free(o_wf)
    o_KT, KT = A.tile([4, NT], BF16, top=True)
    o_wk, wk = A.tile([8, 512], BF16)
    load_w_cols(k, wk, k.cdwin_d, 1024, 512)
    for j in range(4):
        fm_block(k, UT, wk[:, :, j * 128:(j + 1) * 128], CHUNKS_ALL, lambda ci, c0, n, j=j: KT[:, j, c0:c0 + n], tmps,
                 itbase=itb)
        itb += 5
    k.fmp.drain()
    A.free(o_wk)
    for o in toffs:
        A.free(o)
    o_V, V1 = A.tile([18, 8, 80], BF16, top=True)
    o_wv, wv = A.tile([8, 512], BF16)
    load_w_cols(k, wv, k.cdwin_d, 1536, 512)
    P.memset("pool", V1[:, :, :, 64:65], 1.0)
    for t in range(18):
        bb = PS[:, t % 2, :]
        for kk in range(8):
            P.mm(bb, UT[:, kk, t * 128:(t + 1) * 128], wv[:, kk, :], kk == 0, kk == 7)
        P.copy("act" if t % 2 == 0 else "dve", V1[:, t, :, 0:64], bb.rearrange("p (h d) -> p h d", h=8))
    A.free(o_wv)
    A.free(o_UT)
    A.free(o_cs)
    o_wox, WoX = A.tile([8, D], BF16)
    o_g, Gt = A.tile([D], F32)
    o_ws, wst = A.tile([2, D], F32)
    load_gate(k, Gt, l, 0, 0)
    for kk in range(8):
        P.dma("sp", wst[:, kk % 2, :], k.cdwout_d[kk * 128:(kk + 1) * 128, :])
        P.tt("dve" if kk % 2 == 0 else "pool", WoX[:, kk, :], wst[:, kk % 2, :], Gt, ALU.mult)
    A.free(o_g)
    A.free(o_ws)
    qt_ring = [A.tile([NX], BF16) for _ in range(2)]
    P.memset("pool", qt_ring[0][1][64:128, :], 0.0)
    P.memset("pool", qt_ring[1][1][0:64, :], 0.0)

    def load_q(i):
        rd = [("QS", i, ci) for ci in range(4)]
        P.dma("sp", qt_ring[0][1][0:64, :], k.QS_d[i, 0:64, 0:NX], reads=rd)
        P.dma("sp", qt_ring[1][1][64:128, :], k.QS_d[i, 64:128, 0:NX], reads=rd)

    tb_ring = [A.tile([NTILE, 64], BF16) for _ in range(2)]
    sb_ring = []
    pt_tiles = [A.tile([512], BF16) for _ in range(4)]
    k.pt_ring = [t for _, t in pt_tiles]
    k.pt_cnt = 0
    o_ot, otok = A.tile([16, 128], BF16)
    o_oc, otc_r = A.tile([2, 512], BF16)
    rc = k.misc[:, 4:8]
    PST = PS[:, 5, :].bitcast(BF16)
    ycnt = [0]
    sbc = [0]

    def wout_accum(otc, kchunk, c0, first, defer=None):
        for qs in range(4):
            tb = c0 // 128 + qs
            for h2 in range(2):
                def t_y(qs=qs, tb=tb, h2=h2):
                    bank = PS[:, 6 + ycnt[0] % 2, :]
                    ycnt[0] += 1
                    P.mm(bank, otc[:, qs * 128:(qs + 1) * 128], WoX[:, kchunk, h2 * 512:(h2 + 1) * 512], True, True)
                    xs = k.XR[:, tb, h2 * 512:(h2 + 1) * 512]
                    if first:
                        P.stt("dve", xs, xs, ALPHA, bank, ALU.mult, ALU.add)
                    else:
                        P.tt("dve", xs, bank, xs, ALU.add)

                if defer is not None:
                    defer.append(t_y)
                else:
                    t_y()

    P.dma("sp", tb_ring[0][1], k.rpbt_d[0].rearrange("p (a b) -> p a b", b=64))
    k.pipe = Pipe(2)
    o_rc, rcs = A.tile([4, 4], F32)
    pcnt = 0
    for i in range(4):
        load_q(i)
        for hf in range(2):
            h = 2 * i + hf
            TBt = tb_ring[h % 2][1]
            if h + 1 < 8:
                P.dma("sp", tb_ring[(h + 1) % 2][1], k.rpbt_d[h + 1].rearrange("p (a b) -> p a b", b=64))
            for c in range(4):
                c0, n = c * 512, 512
                kbs = list(range(max(0, 4 * c - 2), min(16, 4 * c + 6))) + [16, 17]

                def bias_mm(kb, c=c, TBt=TBt):
                    if kb >= 16:
                        return None
                    return [(r0 * 64, nr * 64, TBt[:, pos0:pos0 + nr, :].rearrange("p a b -> p (a b)"))
                            for (r0, nr, pos0) in natten_segments(kb, c)]

                po = PS[:, 3 + pcnt % 2, 0:260].rearrange("p (q e) -> p q e", e=65)
                rc = rcs[:, pcnt % 4, :]
                pcnt += 1

                def post(po=po, rc=rc, c=c, hf=hf, i=i):
                    P.recip(rc[:, 0:4], po[:, 0:4, 64])
                    P.tt("dve", otok[:, c * 4:c * 4 + 4, 64 * hf:64 * hf + 64], po[:, 0:4, 0:64],
                         rc[:, 0:4].unsqueeze(2).to_broadcast([128, 4, 64]), ALU.mult)
                    if hf == 1:
                        otc = otc_r[:, c % 2, :]

                        def t_transpose(otc=otc, c=c):
                            for qs in range(4):
                                P.tr(PST[:, qs * 128:(qs + 1) * 128], otok[:, c * 4 + qs, :], k.identb)
                            P.copy("dve", otc, PST[:, 0:512])

                        k.pipe.tasks.append(t_transpose)
                        wout_accum(otc, i, c * 512, i == 0, defer=k.pipe.tasks)

                attn_head_chunk(k, KT[:, i, :], qt_ring[hf][1], c0, n, kbs,
                                lambda kb, h=h: V1[:, kb, h, 0:65], lambda qs, po=po: po[:, qs, :], 65, bias_mm=bias_mm,
                                post=post, exp_scale=1.0, zero_regs=[(po.rearrange("p q e -> p (q e)"), 260)])
    k.pipe.drain()
    A.free(o_rc)
    for o, _ in qt_ring + tb_ring + sb_ring + pt_tiles:
        A.free(o)
    A.free(o_ot)
    A.free(o_KT)
    A.free(o_V)
    c_ring = [A.tile([16, 512], BF16) for _ in range(2)]
    s_ring = [A.tile([16, 512], BF16) for _ in range(2)]
    fcnt = 0
    fpend = []
    for tc in range(4):
        Cs = c_ring[tc % 2][1]
        Ss = s_ring[tc % 2][1]
        P.dma("sp", Cs, k.Ct_d[:, tc * 512:(tc + 1) * 512].rearrange("(j p) t -> p j t", p=128))
        P.dma("sp", Ss, k.St_d[:, tc * 512:(tc + 1) * 512].rearrange("(j p) t -> p j t", p=128))
        for gi in range(4):
            bank = PS[:, fcnt % 2, :]
            for j in range(16):
                P.mm(bank, AB[:, j, gi, 0:128], Cs[:, j, :], j == 0, False)
                P.mm(bank, AB[:, j, gi, 128:256], Ss[:, j, :], False, j == 15)
            otc = otc_r[:, fcnt % 2, :]
            fcnt += 1
            P.copy("act", otc, bank)
            prev = fpend
            fpend = []
            wout_accum(otc, 4 + gi, tc * 512, False, defer=fpend)
            for t_ in prev:
                t_()
    for t_ in fpend:
        t_()
    for o, _ in c_ring + s_ring:
        A.free(o)
    for o in (o_oc, o_AB, o_wox):
        A.free(o)


_NC_CACHE = {}


def _in_maps(inp, cores):
    hc = host_consts()
    rp = rpb_table(np.asarray(inp["c_rpb"], np.float32)[0])
    small = np.zeros((128, 4), np.float32)
    small[:, 0] = np.tile(np.asarray(inp["a_q_norm"], np.float32)[0], 2)
    small[:, 1] = np.tile(np.asarray(inp["a_k_norm"], np.float32)[0], 2)
    small[:, 2] = np.asarray(inp["b_subln"], np.float32)[0]
    f = lambda a: np.ascontiguousarray(np.asarray(a, np.float32))
    mod_w = f(inp["mod_w"])
    mod_b = f(inp["mod_b"])
    shared = {
        "c_ctx": f(inp["c_ctx"]).reshape(8, 128),
        "mod_w": mod_w, "mod_b48": mod_b.reshape(2, 48, 128), "mod_b6": mod_b.reshape(2, 6, 1024),
        "ln_g": f(inp["ln_g"]), "ln_b": f(inp["ln_b"]), "ffn_w1": f(inp["ffn_w1"]), "ffn_w2": f(inp["ffn_w2"]),
        "ab_w_in": f(inp["ab_w_in"])[0], "ab_w_out": f(inp["ab_w_out"])[0], "small": small,
        "b_lambda": f(inp["b_lambda"])[0].reshape(1, 256), "cd_w_in": f(inp["cd_w_in"])[0],
        "cd_w_out": f(inp["cd_w_out"])[0], "rpbt": rp.reshape(8, 128, -1),
        "cosT": hc["cosT"], "sinT": hc["sinT"], "cmat": hc["cmat"], "CS": hc["CS"], "Ct": hc["Ct"], "St": hc["St"],
    }
    x = f(inp["x"])
    ctx = f(inp["ctx"])
    c = f(inp["c"])
    maps = []
    for b in cores:
        m = dict(shared)
        m["x"] = np.ascontiguousarray(x[b])
        m["ctx"] = np.ascontiguousarray(ctx[b])
        m["c"] = np.ascontiguousarray(c[b].reshape(8, 128))
        maps.append(m)
    return maps


def kernel(**inputs):
    n = 8
    nc = build_program()
    maps = _in_maps(inputs, list(range(n)))
    res = run_bass_kernel_spmd(nc, maps, core_ids=list(range(n)))
    out = np.stack([np.asarray(r["out"], np.float32) for r in res.results], axis=0)
    return out
```

```python
import numpy as np
import concourse.bass as bass
import concourse.mybir as mybir

F32 = mybir.dt.float32
BF16 = mybir.dt.bfloat16
AF = mybir.ActivationFunctionType
ALU = mybir.AluOpType
AX = mybir.AxisListType

COMPUTE = ("pe", "act", "dve", "pool")
ENGS = ("pe", "act", "dve", "pool", "sp")
QUEUES = ("sp", "actq", "poolq")
Q_ENG = {"sp": "sp", "actq": "act", "poolq": "pool"}


class Op:
    __slots__ = ("eng", "emit", "deps", "is_dma", "queue", "ndep", "seq", "sem", "val", "order")

    def __init__(self, eng, emit, is_dma=False, queue=None):
        self.eng = eng
        self.emit = emit
        self.deps = []
        self.is_dma = is_dma
        self.queue = queue
        self.ndep = 0
        self.seq = None
        self.sem = None
        self.val = None
        self.order = 0


def _prod(xs):
    r = 1
    for v in xs:
        r *= int(v)
    return r


class Arena:
    def __init__(self, tensor, nbytes):
        self.t = tensor
        self.free_list = [(0, nbytes)]
        self.live = {}

    def alloc(self, nbytes, align=64, top=False):
        nbytes = (nbytes + align - 1) // align * align
        if top:
            for i in range(len(self.free_list) - 1, -1, -1):
                o, n = self.free_list[i]
                if n >= nbytes:
                    if n == nbytes:
                        self.free_list.pop(i)
                    else:
                        self.free_list[i] = (o, n - nbytes)
                    self.live[o + n - nbytes] = nbytes
                    return o + n - nbytes
            raise RuntimeError(f"arena OOM(top): need {nbytes}, free={self.free_list}")
        for i, (o, n) in enumerate(self.free_list):
            if n >= nbytes:
                if n == nbytes:
                    self.free_list.pop(i)
                else:
                    self.free_list[i] = (o + nbytes, n - nbytes)
                self.live[o] = nbytes
                return o
        raise RuntimeError(f"arena OOM: need {nbytes}, free={self.free_list}")

    def free(self, off):
        n = self.live.pop(off)
        fl = self.free_list + [(off, n)]
        fl.sort()
        merged = []
        for o, m in fl:
            if merged and merged[-1][0] + merged[-1][1] == o:
                merged[-1] = (merged[-1][0], merged[-1][1] + m)
            else:
                merged.append((o, m))
        self.free_list = merged

    def tile(self, shape, dtype, top=False):
        es = 4 if dtype == F32 else 2
        n = _prod(shape) * es
        off = self.alloc(n, top=top)
        ap = self.t[:, off // 2:(off + n) // 2]
        if dtype == F32:
            ap = ap.bitcast(F32)
        if len(shape) == 2:
            ap = ap.rearrange("p (a b) -> p a b", a=shape[0])
        elif len(shape) == 3:
            ap = ap.rearrange("p (a b c) -> p a b c", a=shape[0], b=shape[1])
        elif len(shape) == 4:
            ap = ap.rearrange("p (a b c d) -> p a b c d", a=shape[0], b=shape[1], c=shape[2])
        return off, ap


class Prog:
    def __init__(self, nc, stack, dma_pool=None, untracked=()):
        self.nc = nc
        self.ops = {e: [] for e in ENGS}
        self.track = {}
        self.untracked = set(untracked)
        self.dma_pool = dma_pool or {"sp": 24, "actq": 8, "poolq": 16}
        self.dma_count = {q: 0 for q in QUEUES}
        self.dma_ops = {q: [] for q in QUEUES}
        self.nops = 0
        self.order = {e: 0 for e in ENGS}
        self.sems = {e: stack.enter_context(nc.semaphore(f"s_{e}")) for e in COMPUTE}
        self.dsems = {q: [stack.enter_context(nc.semaphore(f"d_{q}{i}"))
                          for i in range(self.dma_pool[q])] for q in QUEUES}
        self.sig = {e: 0 for e in COMPUTE}
        self.waited = {e: {} for e in ENGS}
        self.pending_dma = {e: [] for e in ENGS}
        self.nblocks = 0

    def box(self, r):
        if isinstance(r, tuple):
            return (("key",) + r, 0, 1, 0, 1, 1 << 30)
        name = r.name
        if name in self.untracked:
            return None
        aps = r.ap
        off = int(r.offset)
        space = str(r.space)
        es = 4 if r.tensor.dtype == F32 else (2 if r.tensor.dtype == BF16 else mybir.dt.size(r.tensor.dtype))
        if "SB" in space or "PSUM" in space:
            row = _prod(r.tensor.shape[1:])
            p0 = off // row
            f0 = off % row
            pc = aps[0][1]
            f1 = f0 + sum((c - 1) * abs(s) for s, c in aps[1:]) + 1
            if "PSUM" in space:
                return (name, 0, 128, (f0 * es) // 2048 * 2048, ((f1 * es - 1) // 2048 + 1) * 2048, 2048)
            return (name, p0, p0 + pc, f0 * es, f1 * es, 2048)
        lo = off
        hi = off + sum((c - 1) * abs(s) for s, c in aps) + 1
        return (name, 0, 1, lo * es, hi * es, 1 << 18)

    @staticmethod
    def _ov(a, b):
        return a[1] < b[2] and b[1] < a[2] and a[3] < b[4] and b[3] < a[4]

    @staticmethod
    def _inside(a, b):
        return a[1] >= b[1] and a[2] <= b[2] and a[3] >= b[3] and a[4] <= b[4]

    def _pages(self, b):
        ps = b[5]
        return range(b[3] // ps, (b[4] - 1) // ps + 1)

    def add(self, eng, emit, reads=(), writes=(), queue=None):
        is_dma = queue is not None
        if is_dma:
            eng = Q_ENG[queue]
        op = Op(eng, emit, is_dma, queue)
        self.order[eng] += 1
        op.order = self.order[eng]
        deps = []
        rboxes = []
        for r in reads:
            if r is None or isinstance(r, (int, float)):
                continue
            b = self.box(r)
            if b is None:
                continue
            rboxes.append(b)
            st = self.track.get(b[0])
            if st is None:
                continue
            for pg in self._pages(b):
                ent = st.get(pg)
                if ent is None:
                    continue
                for wb, wop in ent[0]:
                    if self._ov(b, wb):
                        deps.append(wop)
                if b[0] == "PS":
                    for rb, rop in ent[1]:
                        if rop.eng != eng and self._ov(b, rb):
                            deps.append(rop)
        wboxes = []
        for w in writes:
            if w is None:
                continue
            b = self.box(w)
            if b is None:
                continue
            wboxes.append(b)
            st = self.track.get(b[0])
            if st is None:
                continue
            for pg in self._pages(b):
                ent = st.get(pg)
                if ent is None:
                    continue
                for wb, wop in ent[0]:
                    if self._ov(b, wb):
                        deps.append(wop)
                for rb, rop in ent[1]:
                    if self._ov(b, rb):
                        deps.append(rop)
        for b in wboxes:
            st = self.track.setdefault(b[0], {})
            for pg in self._pages(b):
                ent = st.setdefault(pg, [[], []])
                ent[0] = [(wb, wop) for wb, wop in ent[0] if not self._inside(wb, b)]
                ent[1] = [(rb, rop) for rb, rop in ent[1] if not self._inside(rb, b)]
                ent[0].append((b, op))
        for b in rboxes:
            st = self.track.setdefault(b[0], {})
            for pg in self._pages(b):
                ent = st.setdefault(pg, [[], []])
                if not is_dma:
                    ent[1] = [(rb, rop) for rb, rop in ent[1]
                              if not (rop.eng == eng and not rop.is_dma and self._inside(rb, b))]
                ent[1].append((b, op))
        best = {}
        seen = set()
        for d in deps:
            if d is op:
                continue
            if d.is_dma:
                if id(d) not in seen:
                    seen.add(id(d))
                    op.deps.append(d)
                continue
            if eng == "pe" and d.eng == "pe" and not is_dma:
                continue
            cur = best.get(d.eng)
            if cur is None or cur.order < d.order:
                best[d.eng] = d
        op.deps.extend(best.values())
        if is_dma:
            q = queue
            j = self.dma_count[q]
            n = self.dma_pool[q]
            if j >= n:
                prev = self.dma_ops[q][j - n]
                if id(prev) not in seen:
                    op.deps.append(prev)
            self.dma_count[q] = j + 1
            self.dma_ops[q].append(op)
            op.seq = j
        for d in op.deps:
            d.ndep += 1
        self.ops[eng].append(op)
        self.nops += 1
        return op

    def mm(self, out, lhsT, rhs, start=True, stop=True, **kw):
        return self.add("pe", lambda e: e.matmul(out, lhsT, rhs, start=start, stop=stop, **kw),
                        reads=[lhsT, rhs], writes=[out])

    def tr(self, out, in_, ident):
        return self.add("pe", lambda e: e.transpose(out, in_, ident), reads=[in_, ident], writes=[out])

    def act(self, out, in_, func, bias=0.0, scale=1.0, accum_out=None):
        kw = {}
        if accum_out is not None:
            kw["accum_out"] = accum_out
        return self.add("act", lambda e: e.activation(out, in_, func, bias=bias, scale=scale, **kw),
                        reads=[in_, bias, scale], writes=[out, accum_out])

    def tt(self, eng, out, a, b, op):
        return self.add(eng, lambda e: e.tensor_tensor(out, a, b, op), reads=[a, b], writes=[out])

    def ts(self, eng, out, a, s1, s2, op0, op1=None):
        if op1 is None:
            return self.add(eng, lambda e: e.tensor_scalar(out, a, s1, None, op0),
                            reads=[a, s1], writes=[out])
        return self.add(eng, lambda e: e.tensor_scalar(out, a, s1, s2, op0, op1),
                        reads=[a, s1, s2], writes=[out])

    def stt(self, eng, out, a, scalar, b, op0, op1):
        return self.add(eng, lambda e: e.scalar_tensor_tensor(out, a, scalar, b, op0, op1),
                        reads=[a, scalar, b], writes=[out])

    def copy(self, eng, out, in_):
        if eng == "act":
            return self.add(eng, lambda e: e.copy(out, in_), reads=[in_], writes=[out])
        return self.add(eng, lambda e: e.tensor_copy(out, in_), reads=[in_], writes=[out])

    def memset(self, eng, out, val):
        return self.add(eng, lambda e: e.memset(out, val), reads=[], writes=[out])

    def recip(self, out, in_):
        return self.add("dve", lambda e: e.reciprocal(out, in_), reads=[in_], writes=[out])

    def rsum(self, eng, out, in_):
        return self.add(eng, lambda e: e.reduce_sum(out, in_, AX.X), reads=[in_], writes=[out])

    def dma(self, queue, out, in_, carry=False, reads=None, writes=None, **kw):
        op = self.add(None, lambda e: e.dma_start(out=out, in_=in_, **kw),
                      reads=[in_] if reads is None else reads,
                      writes=[out] if writes is None else writes, queue=queue)
        if not carry:
            self.pending_dma[op.eng].append(op)
        return op

    def fence(self, eng, reads):
        return self.add(eng, None, reads=reads, writes=[])

    def flush(self):
        nc = self.nc
        sems, dsems = self.sems, self.dsems
        for e in ENGS:
            for op in self.ops[e]:
                if op.is_dma:
                    n = self.dma_pool[op.queue]
                    op.sem = dsems[op.queue][op.seq % n]
                    op.val = 16 * (op.seq // n + 1)
                elif op.emit is not None and op.ndep > 0:
                    self.sig[e] += 1
                    op.sem = sems[e]
                    op.val = self.sig[e]

        def emit_engine(ename, eng):
            waited = self.waited[ename]

            def wait_for(dlist):
                need = {}
                for d in dlist:
                    if d.sem is None:
                        continue
                    k = id(d.sem)
                    if k not in need or need[k][1] < d.val:
                        need[k] = (d.sem, d.val)
                for k, (s_, v) in need.items():
                    if waited.get(k, 0) >= v:
                        continue
                    eng.wait_ge(s_, v)
                    waited[k] = v

            for op in self.ops[ename]:
                wait_for(op.deps)
                if op.emit is None:
                    continue
                ins = op.emit(eng)
                if op.is_dma:
                    ins.then_inc(op.sem, 16)
                elif op.sem is not None:
                    ins.then_inc(op.sem, 1)
            wait_for(self.pending_dma[ename])
            self.pending_dma[ename] = []

        with nc.Block() as block:
            @block.tensor
            def _(e):
                emit_engine("pe", e)

            @block.scalar
            def _(e):
                emit_engine("act", e)

            @block.vector
            def _(e):
                emit_engine("dve", e)

            @block.gpsimd
            def _(e):
                emit_engine("pool", e)

            @block.sync
            def _(e):
                emit_engine("sp", e)
        self.ops = {e: [] for e in ENGS}
        self.nblocks += 1


import math
import numpy as np
import ml_dtypes
from contextlib import ExitStack
import concourse.bass as bass
import concourse.mybir as mybir
from concourse.bass_utils import run_bass_kernel_spmd

D = 1024
NX = 2048
NCTX = 256
NT = NX + NCTX
ALPHA = (2.0 * 2) ** 0.25
EPS = 1e-6
ATTN_SCALE = 64 ** -0.5
NEG = -30000.0
CHUNKS_ALL = [(0, 512), (512, 512), (1024, 512), (1536, 512), (2048, 256)]
CHUNKS_X = CHUNKS_ALL[:4]


def host_consts():
    f32 = np.float32
    pos = np.arange(NX)
    row = (pos // 64).astype(f32)
    col = (pos % 64).astype(f32)
    half = 32
    freqs = np.power(f32(10000.0), -np.arange(0, half, 2, dtype=f32) / f32(half)).astype(f32)

    def axis_angles(p):
        a = p[:, None] * freqs[None, :]
        return np.concatenate([a, a], axis=-1)

    ang = np.concatenate([axis_angles(row), axis_angles(col)], axis=-1).astype(f32)
    cosT = np.tile(np.cos(ang).astype(f32).T, (2, 1)).copy()
    sinT = np.tile(np.sin(ang).astype(f32).T, (2, 1)).copy()
    ident = np.eye(128, dtype=f32)
    Rm = np.zeros((128, 128), f32)
    for m in range(128):
        if m % 32 < 16:
            Rm[m + 16, m] = -1.0
        else:
            Rm[m - 16, m] = 1.0
    bones = np.zeros((128, 128), f32)
    bones[:64, :64] = 1.0
    bones[64:, 64:] = 1.0
    cmat = np.concatenate([ident, Rm, bones], axis=1)
    k = np.arange(128)
    angc = 2 * np.pi * ((k[:, None] * k[None, :]) % 128) / 128.0
    CS = np.concatenate([np.cos(angc), -np.sin(angc)], axis=1) / np.sqrt(128.0)
    t = np.arange(NX, dtype=np.int64)
    angt = 2 * np.pi * ((t[:, None] * t[None, :]) % NX) / float(NX)
    Ct = (np.cos(angt) / np.sqrt(float(NX))).astype(f32).astype(ml_dtypes.bfloat16)
    St = (np.sin(angt) / np.sqrt(float(NX))).astype(f32).astype(ml_dtypes.bfloat16)
    return dict(cosT=cosT, sinT=sinT, cmat=cmat, CS=CS.astype(f32), Ct=Ct, St=St)


TI0 = 9
NTI = 9 + 8 + 8
TF0 = NTI + 9
NTILE = NTI + 9 + 15 + 8


def rpb_table(rpb):
    H = rpb.shape[0]
    cp = np.arange(64)[:, None]
    c = np.arange(64)[None, :]
    cs = np.clip(c - 8, 0, 48)
    valid = (cp >= cs) & (cp < cs + 16)
    dc = np.clip(cp - c + 15, 0, 30)
    tiles = np.full((H, NTILE, 64, 64), NEG, np.float32)

    def tile_for(idx):
        dr = 7 - idx
        g = rpb[:, dr + 7][:, dc]
        return np.where(valid[None], g, np.float32(NEG))

    for idx in range(4, 12):
        tiles[:, TI0 + idx - 4] = tile_for(idx)
    for idx in range(0, 15):
        tiles[:, TF0 + idx] = tile_for(idx)
    t = tiles.transpose(0, 2, 1, 3)
    sh = np.full_like(t, NEG)
    sh[:, :, 1:, :] = t[:, :, :-1, :]
    return np.ascontiguousarray(np.concatenate([t, sh], axis=1)).astype(ml_dtypes.bfloat16)


def natten_segments(kb, c):
    rp = 2 * kb
    segs = []
    rows = list(range(8 * c, 8 * c + 8))
    i = 0
    while i < 8:
        r = rows[i]
        j = i
        if 4 <= r <= 28:
            while j < 8 and 4 <= rows[j] <= 28:
                j += 1
            pos0 = TI0 + (r - rp + 7) - 4
            assert 1 <= pos0 and pos0 + (j - i) <= NTI, (kb, c, pos0)
            segs.append((i, j - i, pos0))
        else:
            lo = r <= 3
            while j < 8 and ((rows[j] <= 3) if lo else (rows[j] >= 29)):
                j += 1
            w0 = 0 if lo else 24
            if w0 <= rp < w0 + 8:
                pos0 = TF0 + (r - rp + 7)
                assert TF0 <= pos0 - 1 and pos0 + (j - i) <= TF0 + 15, (kb, c, pos0)
            else:
                pos0 = NTI + 1
            segs.append((i, j - i, pos0))
        i = j
    return segs


class K:
    pass


def build_program(stop_after=99, debug=False, sub=99):
    nc = bass.Bass("TRN2", target_bir_lowering=False)
    k = K()
    k.nc = nc
    k.sub = sub
    k.fmp = FMPipe()
    k.ut_it = 0

    def din(name, shape, dt=F32):
        return nc.dram_tensor(name, list(shape), dt, kind="ExternalInput").ap()

    k.x_d = din("x", [NX, D])
    k.ctx_d = din("ctx", [NCTX, D])
    k.c_d = din("c", [8, 128])
    k.cctx_d = din("c_ctx", [8, 128])
    k.modw_d = din("mod_w", [2, D, 6 * D])
    k.modb48_d = din("mod_b48", [2, 48, 128])
    k.modb6_d = din("mod_b6", [2, 6, D])
    k.lng_d = din("ln_g", [2, 2, D])
    k.lnb_d = din("ln_b", [2, 2, D])
    k.w1_d = din("ffn_w1", [2, D, 4 * D])
    k.w2_d = din("ffn_w2", [2, 4 * D, D])
    k.abwin_d = din("ab_w_in", [D, 2304])
    k.abwout_d = din("ab_w_out", [D, D])
    k.small_d = din("small", [128, 4])
    k.lam_d = din("b_lambda", [1, 256])
    k.cdwin_d = din("cd_w_in", [D, 2048])
    k.cdwout_d = din("cd_w_out", [D, D])
    k.rpbt_d = din("rpbt", [8, 128, NTILE * 64], BF16)
    k.cosT_d = din("cosT", [128, NX])
    k.sinT_d = din("sinT", [128, NX])
    k.cmat_d = din("cmat", [128, 384])
    k.CS_d = din("CS", [128, 256])
    k.Ct_d = din("Ct", [NX, NX], BF16)
    k.St_d = din("St", [NX, NX], BF16)
    k.out_d = nc.dram_tensor("out", [NX, D], F32, kind="ExternalOutput").ap()
    k.QS_d = nc.dram_tensor("qs_scratch", [8, 128, NT], BF16, kind="Internal").ap()
    k.GS_d = nc.dram_tensor("gs_scratch", [2, 2, 2, D], F32, kind="Internal").ap()
    k.MW1_d = nc.dram_tensor("modw1_bf16", [6, D, D], BF16, kind="Internal").ap()
    if debug:
        k.dbg_d = nc.dram_tensor("dbg", [128, 18, D], F32, kind="ExternalOutput").ap()

    untracked = ["x", "ctx", "c", "c_ctx", "mod_w", "mod_b48", "mod_b6", "ln_g", "ln_b", "ffn_w1", "ffn_w2",
                 "ab_w_in", "ab_w_out", "small", "b_lambda", "cd_w_in", "cd_w_out", "rpbt", "cosT", "sinT",
                 "cmat", "CS", "Ct", "St"]

    with ExitStack() as g:
        P = Prog(nc, g, untracked=untracked)
        k.P = P
        sb = lambda n, s, d: g.enter_context(nc.sbuf_tensor(n, s, d))
        k.XR = sb("XR", [128, 18, D], F32)
        k.cmat = sb("cmat_sb", [128, 384], F32)
        k.identb = sb("identb", [128, 128], BF16)[:]
        k.cmatb = sb("cmatb", [128, 256], BF16)
        k.Rmb = k.cmatb[:, 0:128]
        k.bonesb = k.cmatb[:, 128:256]
        k.FM = sb("FM", [128, 2, 2, 4, 8], F32)
        k.small = sb("small_sb", [128, 4], F32)
        k.epsc = sb("epsc", [128, 1], F32)
        k.misc = sb("misc", [128, 16], F32)
        k.sc2 = sb("sc2", [128, 8, 2], BF16)
        k.zb = sb("zb", [128, 384], BF16)
        k.mbT = sb("mbT", [128, 2, 48], F32)
        k.PS = g.enter_context(nc.psum_tensor("PS", [128, 8, 512], F32))
        ARENA_BYTES = 130 * 1024
        at = sb("arena", [128, ARENA_BYTES // 2], BF16)
        k.A = Arena(at, ARENA_BYTES)
        k.ident = k.cmat[:, 0:128]
        k.Rm = k.cmat[:, 128:256]
        k.bones = k.cmat[:, 256:384]

        for i in range(4):
            P.dma("sp", k.XR[:, 4 * i:4 * i + 4, :],
                  k.x_d[512 * i:512 * (i + 1), :].rearrange("(t p) d -> p t d", p=128))
        P.dma("sp", k.XR[:, 16:18, :], k.ctx_d.rearrange("(t p) d -> p t d", p=128))
        P.dma("sp", k.cmat[:], k.cmat_d)
        P.dma("sp", k.small[:], k.small_d)
        P.memset("dve", k.epsc[:], EPS)
        P.copy("dve", k.identb, k.ident)
        P.memset("pool", k.zb[:], 0.0)
        P.copy("dve", k.cmatb[:], k.cmat[:, 128:384])

        mod_prologue(k)
        phase_mod(k, 0, (0, 1))

        def ln_with_ut(l_ln, j, nb_ln, l_ut, vsh, vsc, nb_ut, mod_l=None, mod_vs=(), out=False, mod_early=False):
            pre = k.A.tile([8, nb_ut * 128], BF16) if nb_ut else None
            tiles = [k.A.tile([8, 1024], BF16) for _ in range(3)] if mod_vs else []
            ring = [t for _, t in tiles]
            groups = {t0 + nt - 1 + 2: (t0, nt) for (t0, nt) in ut_groups(nb_ut)} if nb_ut else {}
            mv = list(mod_vs)

            def hook(tb):
                if mv and (mod_early or tb % 3 == 0):
                    mod_step(k, mod_l, mv.pop(0), ring, banks=(6, 7, 6, 7))
                if tb in groups:
                    assert not (mod_early and mv)
                    t0, nt = groups.pop(tb)
                    build_UT_group(k, pre[1], l_ut, vsh, vsc, t0, nt)

            layer_norm(k, l_ln, j, nb_ln, out=out, hook=hook)
            while mv:
                mod_step(k, mod_l, mv.pop(0), ring, banks=(6, 7, 6, 7))
            for tb in sorted(groups):
                t0, nt = groups[tb]
                build_UT_group(k, pre[1], l_ut, vsh, vsc, t0, nt)
            for o, _ in tiles:
                k.A.free(o)
            return pre

        if stop_after >= 1:
            layer0_mixer(k)
        if stop_after >= 3:
            pre = ln_with_ut(0, 0, 18, 0, 2, 3, 18, mod_l=1, mod_vs=(0, 1, 2))
            ffn(k, 0, 18, pre)
            pre = ln_with_ut(0, 1, 18, 1, 0, 1, 18)
        if stop_after >= 4:
            layer1_mixer(k, pre)
        if stop_after >= 6:
            pre = ln_with_ut(1, 0, 16, 1, 2, 3, 16, mod_l=1, mod_vs=(3, 4, 5), mod_early=True)
            ffn(k, 1, 16, pre)
            ln_with_ut(1, 1, 16, 1, 0, 1, 0, out=True)
        if debug:
            P.dma("sp", k.dbg_d, k.XR[:])
            P.fence("sp", [k.dbg_d])
        P.fence("sp", [k.out_d])
        P.flush()
    return nc


def ps_bank(k, b, n=512):
    return k.PS[:, b, 0:n]


def mod_prologue(k):
    P, A, PS = k.P, k.A, k.PS
    o_c16, c16 = A.tile([128], F32)
    o_m48, m48 = A.tile([2, 128], F32)
    sc2 = k.sc2
    mbT = k.mbT
    P.dma("sp", c16[0:8, :], k.c_d)
    P.dma("sp", c16[8:16, :], k.cctx_d)
    P.tr(PS[:, 0, 0:16], c16[0:16, :], k.ident[0:16, 0:16])
    P.act(sc2[:, :, 0], PS[:, 0, 0:8], AF.Silu)
    P.act(sc2[:, :, 1], PS[:, 0, 8:16], AF.Silu)
    for l in range(2):
        P.dma("sp", m48[0:48, l, :], k.modb48_d[l])
        P.tr(PS[:, 1, l * 48:(l + 1) * 48], m48[0:48, l, :], k.ident[0:48, 0:48])
        P.copy("dve", mbT[:, l, :], PS[:, 1, l * 48:(l + 1) * 48])
    A.free(o_c16)
    A.free(o_m48)
    k.mod_it = 0


def mod_step(k, l, v, wv_ring, banks=(2, 3, 4, 5)):
    P, A, PS = k.P, k.A, k.PS
    sc2, mbT = k.sc2, k.mbT
    it = k.mod_it
    k.mod_it += 1
    Wv = wv_ring[it % len(wv_ring)]
    if l == 0:
        src = k.modw_d[l, :, v * 1024:(v + 1) * 1024].rearrange("(k p) n -> p k n", p=128)
        P.dma("poolq", Wv[:, 0:4, :], src[:, 0:4, :])
        P.dma("poolq", Wv[:, 4:8, :], src[:, 4:8, :])
    else:
        src = k.MW1_d[v].rearrange("(k p) n -> p k n", p=128)
        P.dma("sp", Wv[:, 0:4, :], src[:, 0:4, :], reads=[("MW1", v)])
        P.dma("sp", Wv[:, 4:8, :], src[:, 4:8, :], reads=[("MW1", v)])
    if v in (0, 1, 3, 4):
        vi = {0: 0, 1: 1, 3: 2, 4: 3}[v]
        bank = banks[it % 2]
        for cc in range(8):
            for kk in range(8):
                P.mm(PS[:, bank, cc * 2:cc * 2 + 2], Wv[:, kk, cc * 128:(cc + 1) * 128], sc2[:, kk, :],
                     kk == 0, kk == 7)
        psv = PS[:, bank, 0:16].rearrange("p (c w) -> p c w", w=2)
        for w in range(2):
            P.stt("dve", k.FM[:, l, w, vi, :], psv[:, :, w], 1.0 if v in (1, 4) else 0.0,
                  mbT[:, l, v * 8:(v + 1) * 8], ALU.add, ALU.add)
    else:
        vi = 0 if v == 2 else 1
        o_brow, brow = A.tile([1024], F32)
        o_grow, grow = A.tile([2, 1024], F32)
        P.dma("sp", brow[0:1, :], k.modb6_d[l, v:v + 1, :])
        for w in range(2):
            if l == 1 and w == 1:
                continue
            for hf in range(2):
                bank = banks[2 + hf]
                for kk in range(8):
                    P.mm(PS[0:1, bank, 0:512], sc2[:, kk, w:w + 1], Wv[:, kk, hf * 512:(hf + 1) * 512],
                         kk == 0, kk == 7)
                P.tt("dve", grow[0:1, w, hf * 512:(hf + 1) * 512], PS[0:1, bank, 0:512],
                     brow[0:1, hf * 512:(hf + 1) * 512], ALU.add)
            P.dma("sp", k.GS_d[l, w, vi:vi + 1, :], grow[0:1, w, :], writes=[("GS", l, w, vi)])
        A.free(o_brow)
        A.free(o_grow)


def phase_mod(k, l, vs, ring=None, banks=(2, 3, 4, 5)):
    P, A = k.P, k.A
    tiles = None
    if ring is None:
        tiles = [A.tile([8, 1024], BF16) for _ in range(2)]
        ring = [t for _, t in tiles]
    for v in vs:
        mod_step(k, l, v, ring, banks=banks)
    if tiles is not None:
        for o, _ in tiles:
            A.free(o)


def load_gate(k, dst, l, w, vi, queue="sp"):
    k.P.dma(queue, dst, k.GS_d[l, w, vi:vi + 1, :].partition_broadcast(128), reads=[("GS", l, w, vi)])


def build_UT_group(k, UT, l, v_shift, v_scale, t0, nt):
    P, PS = k.P, k.PS
    w = 0 if t0 < 16 else 1
    for kk in range(8):
        it = k.ut_it
        k.ut_it += 1
        bank = PS[:, it % 2, :]
        for j in range(nt):
            P.tr(bank[:, j * 128:(j + 1) * 128], k.XR[:, t0 + j, kk * 128:(kk + 1) * 128], k.ident)
        dst = UT[:, kk, t0 * 128:(t0 + nt) * 128]
        sc = k.FM[:, l, w, v_scale, kk:kk + 1]
        bi = k.FM[:, l, w, v_shift, kk:kk + 1]
        if it % 2 == 0:
            P.act(dst, bank[:, 0:nt * 128], AF.Identity, bias=bi, scale=sc)
        else:
            P.ts("dve", dst, bank[:, 0:nt * 128], sc, bi, ALU.mult, ALU.add)


def ut_groups(nblocks):
    groups = [(0, 4), (4, 4), (8, 4), (12, 4)]
    if nblocks > 16:
        groups.append((16, 2))
    return groups


def build_UT(k, UT, l, v_shift, v_scale, nblocks):
    for (t0, nt) in ut_groups(nblocks):
        build_UT_group(k, UT, l, v_shift, v_scale, t0, nt)


class FMPipe:
    def __init__(self):
        self.items = []

    def push(self, st):
        self.items.append([st, 0])
        self._step()

    def _step(self):
        for it in list(self.items[::-1]):
            st, idx = it
            st[idx]()
            it[1] += 1
        self.items = [it for it in self.items if it[1] < 3]

    def drain(self):
        while self.items:
            self._step()


def fm_block(k, UT, wt, chunks, dest_fn, tmps, norm_col=None, rope=False, after=None, itbase=0, pre_scale=None):
    P, PS = k.P, k.PS
    nt = len(tmps)
    for ci, (c0, n) in enumerate(chunks):
        it = itbase + ci
        bank = PS[:, it % 2, 0:n]
        is_ctx = c0 >= NX
        do_rope = rope and not is_ctx
        q32, sq, t1, qb = [t[:, 0:n] for t in tmps[it % nt]]
        bn = PS[:, 2 + it % 2, 0:n]
        br = PS[:, 4 + it % 2, 0:n]

        def st1(bank=bank, c0=c0, n=n, ci=ci, do_rope=do_rope, q32=q32, qb=qb):
            for kk in range(8):
                P.mm(bank, wt[:, kk, :], UT[:, kk, c0:c0 + n], kk == 0, kk == 7)
            if norm_col is None and not do_rope:
                dest = dest_fn(ci, c0, n)
                if pre_scale is not None:
                    P.add("act", lambda e, o_=dest, i_=bank, m_=pre_scale: e.mul(o_, i_, m_), reads=[bank],
                          writes=[dest])
                else:
                    P.copy("act", dest, bank)
                if after is not None:
                    after(ci, c0, n, dest)
                return
            P.copy("act", q32, bank)
            if norm_col is not None:
                P.act(qb, bank, AF.Square)
            else:
                P.copy("act", qb, bank)

        def st2(bn=bn, q32=q32, sq=sq, qb=qb, do_rope=do_rope):
            if norm_col is None:
                return
            P.mm(bn, k.bonesb, qb, True, True)
            P.act(sq, bn, AF.Ln, bias=k.epsc[:, 0:1], scale=1.0 / 64.0)
            P.act(sq, sq, AF.Exp, scale=-0.5)
            P.stt("dve", q32, q32, norm_col, sq, ALU.mult, ALU.mult)
            if do_rope:
                P.copy("act", qb, q32)

        def st3(br=br, q32=q32, sq=sq, t1=t1, qb=qb, do_rope=do_rope, c0=c0, n=n, ci=ci):
            if norm_col is None and not do_rope:
                return
            dest = dest_fn(ci, c0, n)
            if do_rope:
                P.mm(br, k.Rmb, qb, True, True)
                P.tt("pool", t1, q32, k.cosT[:, c0:c0 + n], ALU.mult)
                P.tt("dve", sq, br, k.sinT[:, c0:c0 + n], ALU.mult)
                P.tt("dve", dest, t1, sq, ALU.add)
            else:
                P.copy("act", dest, q32)
            if after is not None:
                after(ci, c0, n, dest)

        k.fmp.push([st1, st2, st3])


def alloc_fm_tmps(k, nsets=3):
    offs, tmps = [], []
    for _ in range(nsets):
        s = []
        for _ in range(3):
            o, t = k.A.tile([512], F32)
            offs.append(o)
            s.append(t)
        o, t = k.A.tile([512], BF16)
        offs.append(o)
        s.append(t)
        tmps.append(s)
    return offs, tmps


def load_w_cols(k, dst, wd, c0, n, queue="poolq"):
    k.P.dma(queue, dst, wd[:, c0:c0 + n].rearrange("(k p) n -> p k n", p=128))


class Pipe:
    def __init__(self, L=2):
        self.q = []
        self.L = L
        self.tasks = []

    def push(self, S, E_PV, post=None):
        bank = S()
        self.q.append((bank, E_PV, post))
        while len(self.q) > self.L:
            self._pop()

    def _pop(self):
        bank, E_PV, post = self.q.pop(0)
        E_PV(bank)
        if self.tasks:
            self.tasks.pop(0)()
        if post is not None:
            post()

    def drain(self):
        while self.q:
            self._pop()
        while self.tasks:
            self.tasks.pop(0)()


def attn_head_chunk(k, kT, qT, c0, n, kbs, v_fn, po_fn, nv, bias_fn=None, cnt=[0], bank_first=(0,), post=None,
                    exp_scale=ATTN_SCALE, bias_mm=None, zero_regs=()):
    P, PS = k.P, k.PS
    nq = n // 128
    nk = len(kbs)

    def S(kb):
        bank = PS[:, cnt[0] % 3, 0:n]
        extra = bias_mm(kb) if bias_mm is not None else None
        P.mm(bank, kT[:, kb * 128:(kb + 1) * 128], qT[:, c0:c0 + n], True, not extra)
        if extra:
            for ei, (col0, ncol, rhs) in enumerate(extra):
                P.mm(bank[:, col0:col0 + ncol], k.identb, rhs, False, ei == len(extra) - 1)
        cnt[0] += 1
        return bank

    def E_PV(bank, idx, kb):
        pt = k.pt_ring[k.pt_cnt % len(k.pt_ring)][:, 0:n]
        k.pt_cnt += 1
        sbias = bias_fn(kb, bank, n) if bias_fn is not None else None
        if sbias is not None:
            P.act(pt, sbias, AF.Exp)
        else:
            P.act(pt, bank, AF.Exp, scale=exp_scale)
        if idx == 0:
            for (reg, ncol) in zero_regs:
                P.mm(reg, k.zb[:, 0:128], k.zb[:, 0:ncol], True, False)
        for qs in range(nq):
            P.mm(po_fn(qs), pt[:, qs * 128:(qs + 1) * 128], v_fn(kb), False,
                 idx == nk - 1 and (qs == nq - 1 or (qs + 1) in bank_first))

    for idx, kb in enumerate(kbs):
        k.pipe.push(lambda kb=kb: S(kb), lambda bank, idx=idx, kb=kb: E_PV(bank, idx, kb),
                    post if idx == nk - 1 else None)


def layer0_mixer(k):
    P, A, PS = k.P, k.A, k.PS
    l = 0
    o_UT, UT = A.tile([8, NT], BF16)
    o_cos, cosT = A.tile([NX], F32)
    o_sin, sinT = A.tile([NX], F32)
    k.cosT, k.sinT = cosT, sinT
    P.dma("sp", cosT, k.cosT_d)
    P.dma("sp", sinT, k.sinT_d)
    build_UT(k, UT, l, 0, 1, 18)
    if k.sub <= 0:
        return
    toffs, tmps = alloc_fm_tmps(k)
    qn_col = k.small[:, 0:1]
    kn_col = k.small[:, 1:2]
    wq_ring = [A.tile([8, 256], BF16) for _ in range(2)]
    qst_ring = [A.tile([512], BF16) for _ in range(3)]
    mtiles = [A.tile([8, 1024], BF16) for _ in range(2)]
    mring = [t for _, t in mtiles]
    mvs = [2, 3, 4, 5]
    itb = 0
    load_w_cols(k, wq_ring[0][1], k.abwin_d, 0, 256)
    for pi in range(4):
        _, wq = wq_ring[pi % 2]
        if pi + 1 < 4:
            load_w_cols(k, wq_ring[(pi + 1) % 2][1], k.abwin_d, (pi + 1) * 256, 256)
        if mvs:
            mod_step(k, 0, mvs.pop(0), mring, banks=(6, 7, 6, 7))
        for sub in range(2):
            i = pi * 2 + sub
            st = {"c": 0}

            def dest_fn(ci, c0, n, itb=itb):
                return qst_ring[(itb + ci) % 3][1][:, 0:n]

            def after(ci, c0, n, dest, i=i):
                P.dma("sp", k.QS_d[i, :, c0:c0 + n], dest, writes=[("QS", i, ci)])

            fm_block(k, UT, wq[:, :, sub * 128:(sub + 1) * 128], CHUNKS_ALL, dest_fn, tmps,
                     norm_col=qn_col if i < 4 else None, rope=True, after=after, itbase=itb)
            itb += 5
    k.fmp.drain()
    for o, _ in wq_ring + qst_ring + mtiles:
        A.free(o)
    if k.sub <= 1:
        return
    o_KT, KT = A.tile([6, NT], BF16, top=True)
    o_wk, wk = A.tile([8, 512], BF16)
    load_w_cols(k, wk, k.abwin_d, 1280, 512)
    o_wkd, wkd = A.tile([2, 8, 128], BF16)
    for kvh in range(2):
        for hf in range(2):
            load_w_cols(k, wkd[:, kvh, :, hf * 64:(hf + 1) * 64], k.abwin_d, 1024 + kvh * 64, 64)
    for j in range(6):
        wt = wkd[:, j, :, :] if j < 2 else wk[:, :, (j - 2) * 128:(j - 1) * 128]
        fm_block(k, UT, wt, CHUNKS_ALL, lambda ci, c0, n, j=j: KT[:, j, c0:c0 + n], tmps,
                 norm_col=kn_col if j < 2 else None, rope=True, itbase=itb)
        itb += 5
    k.fmp.drain()
    A.free(o_wk)
    A.free(o_wkd)
    for o in toffs:
        A.free(o)
    A.free(o_cos)
    A.free(o_sin)
    if k.sub <= 2:
        return
    o_Va, Va1 = A.tile([18, 2, 80], BF16, top=True)
    o_Vb, Vb1 = A.tile([18, 4, 144], BF16, top=True)
    o_wv, wv = A.tile([8, 640], BF16)
    load_w_cols(k, wv[:, :, 0:128], k.abwin_d, 1152, 128)
    load_w_cols(k, wv[:, :, 128:640], k.abwin_d, 1792, 512)
    P.memset("pool", Va1[:, :, :, 64:65], 1.0)
    P.memset("pool", Vb1[:, :, :, 128:129], 1.0)
    for t in range(18):
        ba = PS[:, 2 * (t % 2), 0:128]
        bb = PS[:, 2 * (t % 2) + 1, 0:512]
        for kk in range(8):
            P.mm(ba, UT[:, kk, t * 128:(t + 1) * 128], wv[:, kk, 0:128], kk == 0, kk == 7)
        for kk in range(8):
            P.mm(bb, UT[:, kk, t * 128:(t + 1) * 128], wv[:, kk, 128:640], kk == 0, kk == 7)
        P.copy("act", Va1[:, t, :, 0:64], ba.rearrange("p (h d) -> p h d", h=2))
        P.copy("dve", Vb1[:, t, :, 0:128], bb.rearrange("p (h d) -> p h d", h=4))
    A.free(o_wv)
    A.free(o_UT)

    if k.sub <= 3:
        return
    o_lb, lb = A.tile([256], F32)
    P.dma("sp", lb, k.lam_d.partition_broadcast(128))
    lb4 = lb.rearrange("p (a b d) -> p a b d", a=2, b=2)
    o_lp, lp = A.tile([2, 64], F32)
    P.tt("dve", lp, lb4[:, :, 0, :], lb4[:, :, 1, :], ALU.mult)
    lsum = k.misc[:, 0:2]
    P.rsum("dve", lsum, lp)
    P.act(lsum, lsum, AF.Exp)
    lam_init = 0.8 - 0.6 * math.exp(0.0)
    neglam = k.misc[:, 2:3]
    P.stt("dve", neglam, k.misc[:, 1:2], -lam_init, k.misc[:, 0:1], ALU.add, ALU.subtract)
    rowsc = k.misc[:, 3:4]
    P.ts("dve", rowsc, k.small[:, 2:3], 1.0 - lam_init, None, ALU.mult)
    A.free(o_lb)
    A.free(o_lp)
    setup = attention_setup(k)
    o_wox, WoX = A.tile([8, D], BF16)
    o_woc, WoC = A.tile([8, D], BF16)
    o_g, Gt = A.tile([2, D], F32)
    o_ws, wst = A.tile([2, D], F32)
    load_gate(k, Gt[:, 0, :], l, 0, 0)
    load_gate(k, Gt[:, 1, :], l, 1, 0)
    for kk in range(8):
        P.dma("sp", wst[:, kk % 2, :], k.abwout_d[kk * 128:(kk + 1) * 128, :])
        rs = rowsc if kk >= 4 else 1.0
        P.stt("dve", WoX[:, kk, :], wst[:, kk % 2, :], rs, Gt[:, 0, :], ALU.mult, ALU.mult)
        P.stt("dve", WoC[:, kk, :], wst[:, kk % 2, :], rs, Gt[:, 1, :], ALU.mult, ALU.mult)
    A.free(o_g)
    A.free(o_ws)
    if k.sub <= 4:
        return
    k.bg_dmas = [(lambda v=v: P.dma("poolq", k.MW1_d[v], k.modw_d[1, :, v * 1024:(v + 1) * 1024],
                                    writes=[("MW1", v)], carry=True)) for v in range(6)]
    attention_core(k, 0, KT, (Va1, Vb1), (WoX, WoC), setup)
    while k.bg_dmas:
        k.bg_dmas.pop(0)()
    for o in (o_KT, o_Va, o_Vb, o_wox, o_woc):
        A.free(o)


def attention_setup(k):
    P, A = k.P, k.A
    qt_ring = [A.tile([NT], BF16) for _ in range(4)]
    for sl in range(2):
        P.memset("pool", qt_ring[2 * sl][1][64:128, :], 0.0)
        P.memset("pool", qt_ring[2 * sl + 1][1][0:64, :], 0.0)

    def load_q(i):
        sl = i % 2
        rd = [("QS", i, ci) for ci in range(5)]
        P.dma("sp", qt_ring[2 * sl][1][0:64, :], k.QS_d[i, 0:64, :], reads=rd)
        P.dma("sp", qt_ring[2 * sl + 1][1][64:128, :], k.QS_d[i, 64:128, :], reads=rd)

    load_q(0)
    return qt_ring, load_q


def attention_core(k, l, KT, V, Wo, setup):
    P, A, PS = k.P, k.A, k.PS
    Va1, Vb1 = V
    WoX, WoC = Wo
    qt_ring, load_q = setup

    pt_tiles = [A.tile([512], BF16) for _ in range(4)]
    k.pt_ring = [t for _, t in pt_tiles]
    k.pt_cnt = 0
    k.pipe = Pipe(2)
    o_ot, otok_r = A.tile([3, 4, 128], BF16)
    o_oc, otc_r = A.tile([2, 512], BF16)
    o_ta, tacc_r = A.tile([2, 4, 128], F32)
    o_sq, sqt = A.tile([4, 128], F32)
    o_rc, rcs = A.tile([4, 16], F32)
    PST = PS[:, 5, :].bitcast(BF16)
    chunks = CHUNKS_ALL
    pocnt = [0]
    ycnt = [0]
    cc = 0

    def finish_chunk(i, c0, n, otok, otc, rstd_col):
        nq = n // 128

        def t_transpose():
            for qs in range(nq):
                P.tr(PST[:, qs * 128:(qs + 1) * 128], otok[:, qs, :], k.identb)
            P.copy("dve", otc[:, 0:n], PST[:, 0:n])

        k.pipe.tasks.append(t_transpose)
        for qs in range(nq):
            tb = c0 // 128 + qs
            W = WoX if tb < 16 else WoC
            for h2 in range(2):
                def t_y(qs=qs, tb=tb, W=W, h2=h2):
                    bank = PS[:, 6 + ycnt[0] % 2, :]
                    ycnt[0] += 1
                    P.mm(bank, otc[:, qs * 128:(qs + 1) * 128], W[:, i, h2 * 512:(h2 + 1) * 512], True, True)
                    xs = k.XR[:, tb, h2 * 512:(h2 + 1) * 512]
                    if rstd_col is not None:
                        P.stt("dve", xs, bank, rstd_col[:, qs:qs + 1], xs, ALU.mult, ALU.add)
                    elif i == 0:
                        P.stt("dve", xs, xs, ALPHA, bank, ALU.mult, ALU.add)
                    else:
                        P.tt("dve", xs, bank, xs, ALU.add)

                k.pipe.tasks.append(t_y)

    for i in range(8):
        qth = (qt_ring[2 * (i % 2)][1], qt_ring[2 * (i % 2) + 1][1])
        if i + 1 < 8:
            load_q(i + 1)
        if i >= 1 and k.bg_dmas:
            k.bg_dmas.pop(0)()
        for (c0, n) in chunks:
            nq = n // 128
            kbs = list(range(18)) if c0 < NX else [16, 17]
            otok = otok_r[:, cc % 3]
            otc = otc_r[:, cc % 2, :]
            tacc = tacc_r[:, cc % 2]
            rcv = rcs[:, cc % 4, :]
            cc += 1
            while len(k.pipe.tasks) > 17:
                k.pipe.tasks.pop(0)()
            if i < 4:
                kvh = i // 2
                for hf in range(2):
                    po = PS[:, 3 + pocnt[0] % 2, 0:260].rearrange("p (q e) -> p q e", e=65)
                    pocnt[0] += 1

                    def post(hf=hf, po=po, otok=otok, otc=otc, nq=nq, rcv=rcv, i=i, c0=c0, n=n):
                        rc = rcv[:, 4 * hf:4 * hf + 4]
                        P.recip(rc[:, 0:nq], po[:, 0:nq, 64])
                        P.tt("dve", otok[:, 0:nq, 64 * hf:64 * hf + 64], po[:, 0:nq, 0:64],
                             rc[:, 0:nq].unsqueeze(2).to_broadcast([128, nq, 64]), ALU.mult)
                        if hf == 1:
                            finish_chunk(i, c0, n, otok, otc, None)

                    attn_head_chunk(k, KT[:, kvh, :], qth[hf], c0, n, kbs,
                                    lambda kb, kvh=kvh: Va1[:, kb, kvh, 0:65], lambda qs, po=po: po[:, qs, :], 65,
                                    post=post, zero_regs=[(po.rearrange("p q e -> p (q e)")[:, 0:65 * nq], 65 * nq)])
            else:
                h = i - 4
                for j in range(2):
                    poA = PS[:, 3, 0:258].rearrange("p (q e) -> p q e", e=129)
                    poB = PS[:, 4, 0:258].rearrange("p (q e) -> p q e", e=129)
                    pof = lambda qs, poA=poA, poB=poB: (poA if qs < 2 else poB)[:, qs % 2, :]

                    def post(j=j, pof=pof, otok=otok, otc=otc, tacc=tacc, nq=nq, rcv=rcv, i=i, c0=c0, n=n,
                             poA_=poA, poB_=poB):
                        rc = rcv[:, 0:4]
                        rc1 = rcv[:, 4:8]
                        ss4 = rcv[:, 8:12]
                        pA = pof(0)
                        banks2 = [(0, poA_)] + ([(2, poB_)] if nq > 2 else [])
                        for q0, pb in banks2:
                            P.recip(rc[:, q0:q0 + 2], pb[:, 0:2, 128])
                        if j == 0:
                            for q0, pb in banks2:
                                P.tt("dve", tacc[:, q0:q0 + 2, :], pb[:, 0:2, 0:128],
                                     rc[:, q0:q0 + 2].unsqueeze(2).to_broadcast([128, 2, 128]), ALU.mult)
                            return
                        P.ts("dve", rc1[:, 0:nq], rc[:, 0:nq], k.misc[:, 2:3], None, ALU.mult)
                        for q0, pb in banks2:
                            P.tt("dve", sqt[:, q0:q0 + 2, :], pb[:, 0:2, 0:128],
                                 rc1[:, q0:q0 + 2].unsqueeze(2).to_broadcast([128, 2, 128]), ALU.mult)
                        P.tt("dve", tacc[:, 0:nq, :], tacc[:, 0:nq, :], sqt[:, 0:nq, :], ALU.add)
                        P.tt("dve", sqt[:, 0:nq, :], tacc[:, 0:nq, :], tacc[:, 0:nq, :], ALU.mult)
                        P.rsum("dve", ss4[:, 0:nq], sqt[:, 0:nq, :])
                        P.act(ss4[:, 0:nq], ss4[:, 0:nq], AF.Ln, bias=k.epsc[:, 0:1], scale=1.0 / 128.0)
                        P.act(ss4[:, 0:nq], ss4[:, 0:nq], AF.Exp, scale=-0.5)
                        P.copy("dve", otok[:, 0:nq, :], tacc[:, 0:nq, :])
                        finish_chunk(i, c0, n, otok, otc, ss4)

                    attn_head_chunk(k, KT[:, 2 + h, :], qth[j], c0, n, kbs,
                                    lambda kb, h=h: Vb1[:, kb, h, 0:129], pof, 129, bank_first=(0, 2), post=post,
                                    zero_regs=[(pb.rearrange("p q e -> p (q e)"), 258)
                                               for pb in ([poA, poB] if nq > 2 else [poA])])
    k.pipe.drain()
    for o, _ in qt_ring:
        A.free(o)
    for o in (o_ot, o_oc, o_ta, o_sq, o_rc):
        A.free(o)
    for o, _ in pt_tiles:
        A.free(o)


def layer_norm(k, l, j, nblocks, out=False, hook=None):
    P, A = k.P, k.A
    o_g, gam = A.tile([D], F32)
    o_b, bet = A.tile([D], F32)
    P.dma("sp", gam, k.lng_d[l, j:j + 1, :].partition_broadcast(128))
    P.dma("sp", bet, k.lnb_d[l, j:j + 1, :].partition_broadcast(128))
    o_st, st = A.tile([nblocks, 2, 6], F32)
    o_mv, mv = A.tile([nblocks, 2], F32)
    o_rs, rs = A.tile([2, nblocks], F32)
    o_y, ytmp = A.tile([2, D], F32)
    for tb in range(nblocks):
        for hh in range(2):
            P.add("dve", lambda e, o_=st[:, tb, hh, :], i_=k.XR[:, tb, hh * 512:(hh + 1) * 512]: e.bn_stats(o_, i_),
                  reads=[k.XR[:, tb, hh * 512:(hh + 1) * 512]], writes=[st[:, tb, hh, :]])
        P.add("dve", lambda e, o_=mv[:, tb, :], i_=st[:, tb, :, :].rearrange("p a b -> p (a b)"): e.bn_aggr(o_, i_),
              reads=[st[:, tb, :, :]], writes=[mv[:, tb, :]])
    P.act(rs[:, 0, :], mv[:, :, 1], AF.Ln, bias=k.epsc[:, 0:1], scale=1.0)
    P.act(rs[:, 0, :], rs[:, 0, :], AF.Exp, scale=-0.5)
    P.stt("dve", rs[:, 1, :], mv[:, :, 0], -1.0, rs[:, 0, :], ALU.mult, ALU.mult)
    for tb in range(nblocks):
        r = tb % 2
        xs = k.XR[:, tb, :]
        y = ytmp[:, r, :]
        P.act(y, xs, AF.Identity, bias=rs[:, 1, tb:tb + 1], scale=rs[:, 0, tb:tb + 1])
        P.tt("dve", y, y, gam, ALU.mult)
        P.tt("dve", k.XR[:, tb, 0:640], y[:, 0:640], bet[:, 0:640], ALU.add)
        P.tt("pool", k.XR[:, tb, 640:D], y[:, 640:D], bet[:, 640:D], ALU.add)
        if out:
            P.dma("sp", k.out_d[tb * 128:(tb + 1) * 128, :], xs)
        if hook is not None:
            hook(tb)
    for o in (o_g, o_b, o_st, o_mv, o_rs, o_y):
        A.free(o)


def ffn(k, l, nblocks, pre):
    P, A, PS = k.P, k.A, k.PS
    ntok = nblocks * 128
    chunks = CHUNKS_ALL if nblocks == 18 else CHUNKS_X
    o_u, u2T = pre
    o_h, hT = A.tile([8, ntok], BF16)
    o_g, Gt = A.tile([2, D], F32)
    load_gate(k, Gt[:, 0, :], l, 0, 1)
    if nblocks > 16:
        load_gate(k, Gt[:, 1, :], l, 1, 1)
    w1r = [A.tile([8, 512], BF16) for _ in range(2)]
    w2r = [A.tile([4, D], BF16) for _ in range(2)]
    o_r, rl = A.tile([2, 512], F32)
    o_t2, tmp2 = A.tile([2, 512], F32)
    hcnt = 0
    ycnt = 0
    for qd in range(4):
        for hh in range(2):
            c0 = qd * 1024 + hh * 512
            P.dma("poolq", w1r[hh][1], k.w1_d[l, :, c0:c0 + 512].rearrange("(k p) n -> p k n", p=128))
        for hh in range(2):
            r0 = qd * 1024 + hh * 512
            P.dma("poolq", w2r[hh][1], k.w2_d[l, r0:r0 + 512, :].rearrange("(j p) n -> p j n", p=128))
        if k.sub <= 10:
            break
        for hc in range(8):
            w1t = w1r[hc // 4][1]
            for (c0, n) in chunks:
                bank = PS[:, hcnt % 3, 0:n]
                for kk in range(8):
                    P.mm(bank, w1t[:, kk, (hc % 4) * 128:(hc % 4 + 1) * 128], u2T[:, kk, c0:c0 + n], kk == 0, kk == 7)
                r = rl[:, hcnt % 2, 0:n]
                P.act(r, bank, AF.Relu)
                P.act(hT[:, hc, c0:c0 + n], r, AF.Square)
                hcnt += 1
        if k.sub <= 11:
            break
        for tb in range(nblocks):
            G = Gt[:, 0, :] if tb < 16 else Gt[:, 1, :]
            for h2 in range(2):
                bank = PS[:, 3 + ycnt % 4, :]
                for hc in range(8):
                    P.mm(bank, hT[:, hc, tb * 128:(tb + 1) * 128], w2r[hc // 4][1][:, hc % 4, h2 * 512:(h2 + 1) * 512],
                         hc == 0, hc == 7)
                t2 = tmp2[:, ycnt % 2, :]
                ycnt += 1
                P.tt("dve", t2, bank, G[:, h2 * 512:(h2 + 1) * 512], ALU.mult)
                xs = k.XR[:, tb, h2 * 512:(h2 + 1) * 512]
                if qd == 0:
                    P.stt("dve", xs, xs, ALPHA, t2, ALU.mult, ALU.add)
                else:
                    P.tt("dve", xs, xs, t2, ALU.add)
        if k.sub <= 12:
            break
    for o in (o_u, o_h, o_g, o_r, o_t2, w1r[0][0], w1r[1][0], w2r[0][0], w2r[1][0]):
        A.free(o)


def layer1_mixer(k, pre):
    P, A, PS = k.P, k.A, k.PS
    l = 1
    o_UT, UT = pre
    o_cs, CS = A.tile([256], F32)
    P.dma("sp", CS, k.CS_d)
    toffs, tmps = alloc_fm_tmps(k)
    itb = 0
    wq_ring = [A.tile([8, 256], BF16) for _ in range(2)]
    qst_ring = [A.tile([512], BF16) for _ in range(3)]
    for pi in range(2):
        _, wq = wq_ring[pi % 2]
        load_w_cols(k, wq, k.cdwin_d, pi * 256, 256)
        for sub in range(2):
            i = pi * 2 + sub

            def dest_fn(ci, c0, n, itb=itb):
                return qst_ring[(itb + ci) % 3][1][:, 0:n]

            def after(ci, c0, n, dest, i=i):
                P.dma("sp", k.QS_d[i, :, c0:c0 + n], dest, writes=[("QS", i, ci)])

            fm_block(k, UT, wq[:, :, sub * 128:(sub + 1) * 128], CHUNKS_X, dest_fn, tmps, after=after, itbase=itb,
                     pre_scale=ATTN_SCALE)
            itb += 4
    k.fmp.drain()
    for o, _ in wq_ring + qst_ring:
        A.free(o)
    o_AB, AB = A.tile([16, 4, 256], BF16, top=True)
    o_wf, wf = A.tile([8, 512], BF16)
    load_w_cols(k, wf, k.cdwin_d, 512, 512)
    abcnt = [0]
    k.fd_pending = []
    for gi in range(4):
        def dest_fn(ci, c0, n, itb=itb):
            return tmps[(itb + ci) % 3][0][:, 0:n]

        def after_now(ci, c0, n, dest, gi=gi):
            k.fd_pending.append(lambda: after(ci, c0, n, dest, gi))
            while len(k.fd_pending) > 1:
                k.fd_pending.pop(0)()

        def after(ci, c0, n, dest, gi=gi):
            for pair in range(2):
                bank = PS[:, 2 + abcnt[0] % 2, :]
                abcnt[0] += 1
                for j in range(2):
                    qs = pair * 2 + j
                    P.mm(bank[:, j * 256:(j + 1) * 256], dest[:, qs * 128:(qs + 1) * 128], CS, True, True)
                tb = c0 // 128 + pair * 2
                P.copy("dve", AB[:, tb:tb + 2, gi, :], bank.rearrange("p (a b) -> p a b", a=2))

        fm_block(k, UT, wf[:, :, gi * 128:(gi + 1) * 128], CHUNKS_X, dest_fn, tmps, after=after_now, itbase=itb)
        itb += 4
    k.fmp.drain()
    while k.fd_pending:
        k.fd_pending.pop(0)()
    A.free(o_wf)
    o_KT, KT = A.tile([4, NT], BF16, top=True)
    o_wk, wk = A.tile([8, 512], BF16)
    load_w_cols(k, wk, k.cdwin_d, 1024, 512)
    for j in range(4):
        fm_block(k, UT, wk[:, :, j * 128:(j + 1) * 128], CHUNKS_ALL, lambda ci, c0, n, j=j: KT[:, j, c0:c0 + n], tmps,
                 itbase=itb)
        itb += 5
    k.fmp.drain()
    A.free(o_wk)
    for o in toffs:
        A.free(o)
    o_V, V1 = A.tile([18, 8, 80], BF16, top=True)
    o_wv, wv = A.tile([8, 512], BF16)
    load_w_cols(k, wv, k.cdwin_d, 1536, 512)
    P.memset("pool", V1[:, :, :, 64:65], 1.0)
    for t in range(18):
        bb = PS[:, t % 2, :]
        for kk in range(8):
            P.mm(bb, UT[:, kk, t * 128:(t + 1) * 128], wv[:, kk, :], kk == 0, kk == 7)
        P.copy("act" if t % 2 == 0 else "dve", V1[:, t, :, 0:64], bb.rearrange("p (h d) -> p h d", h=8))
    A.free(o_wv)
    A.free(o_UT)
    A.free(o_cs)
    o_wox, WoX = A.tile([8, D], BF16)
    o_g, Gt = A.tile([D], F32)
    o_ws, wst = A.tile([2, D], F32)
    load_gate(k, Gt, l, 0, 0)
    for kk in range(8):
        P.dma("sp", wst[:, kk % 2, :], k.cdwout_d[kk * 128:(kk + 1) * 128, :])
        P.tt("dve" if kk % 2 == 0 else "pool", WoX[:, kk, :], wst[:, kk % 2, :], Gt, ALU.mult)
    A.free(o_g)
    A.free(o_ws)
    qt_ring = [A.tile([NX], BF16) for _ in range(2)]
    P.memset("pool", qt_ring[0][1][64:128, :], 0.0)
    P.memset("pool", qt_ring[1][1][0:64, :], 0.0)

    def load_q(i):
        rd = [("QS", i, ci) for ci in range(4)]
        P.dma("sp", qt_ring[0][1][0:64, :], k.QS_d[i, 0:64, 0:NX], reads=rd)
        P.dma("sp", qt_ring[1][1][64:128, :], k.QS_d[i, 64:128, 0:NX], reads=rd)

    tb_ring = [A.tile([NTILE, 64], BF16) for _ in range(2)]
    sb_ring = []
    pt_tiles = [A.tile([512], BF16) for _ in range(4)]
    k.pt_ring = [t for _, t in pt_tiles]
    k.pt_cnt = 0
    o_ot, otok = A.tile([16, 128], BF16)
    o_oc, otc_r = A.tile([2, 512], BF16)
    rc = k.misc[:, 4:8]
    PST = PS[:, 5, :].bitcast(BF16)
    ycnt = [0]
    sbc = [0]

    def wout_accum(otc, kchunk, c0, first, defer=None):
        for qs in range(4):
            tb = c0 // 128 + qs
            for h2 in range(2):
                def t_y(qs=qs, tb=tb, h2=h2):
                    bank = PS[:, 6 + ycnt[0] % 2, :]
                    ycnt[0] += 1
                    P.mm(bank, otc[:, qs * 128:(qs + 1) * 128], WoX[:, kchunk, h2 * 512:(h2 + 1) * 512], True, True)
                    xs = k.XR[:, tb, h2 * 512:(h2 + 1) * 512]
                    if first:
                        P.stt("dve", xs, xs, ALPHA, bank, ALU.mult, ALU.add)
                    else:
                        P.tt("dve", xs, bank, xs, ALU.add)

                if defer is not None:
                    defer.append(t_y)
                else:
                    t_y()

    P.dma("sp", tb_ring[0][1], k.rpbt_d[0].rearrange("p (a b) -> p a b", b=64))
    k.pipe = Pipe(2)
    o_rc, rcs = A.tile([4, 4], F32)
    pcnt = 0
    for i in range(4):
        load_q(i)
        for hf in range(2):
            h = 2 * i + hf
            TBt = tb_ring[h % 2][1]
            if h + 1 < 8:
                P.dma("sp", tb_ring[(h + 1) % 2][1], k.rpbt_d[h + 1].rearrange("p (a b) -> p a b", b=64))
            for c in range(4):
                c0, n = c * 512, 512
                kbs = list(range(max(0, 4 * c - 2), min(16, 4 * c + 6))) + [16, 17]

                def bias_mm(kb, c=c, TBt=TBt):
                    if kb >= 16:
                        return None
                    return [(r0 * 64, nr * 64, TBt[:, pos0:pos0 + nr, :].rearrange("p a b -> p (a b)"))
                            for (r0, nr, pos0) in natten_segments(kb, c)]

                po = PS[:, 3 + pcnt % 2, 0:260].rearrange("p (q e) -> p q e", e=65)
                rc = rcs[:, pcnt % 4, :]
                pcnt += 1

                def post(po=po, rc=rc, c=c, hf=hf, i=i):
                    P.recip(rc[:, 0:4], po[:, 0:4, 64])
                    P.tt("dve", otok[:, c * 4:c * 4 + 4, 64 * hf:64 * hf + 64], po[:, 0:4, 0:64],
                         rc[:, 0:4].unsqueeze(2).to_broadcast([128, 4, 64]), ALU.mult)
                    if hf == 1:
                        otc = otc_r[:, c % 2, :]

                        def t_transpose(otc=otc, c=c):
                            for qs in range(4):
                                P.tr(PST[:, qs * 128:(qs + 1) * 128], otok[:, c * 4 + qs, :], k.identb)
                            P.copy("dve", otc, PST[:, 0:512])

                        k.pipe.tasks.append(t_transpose)
                        wout_accum(otc, i, c * 512, i == 0, defer=k.pipe.tasks)

                attn_head_chunk(k, KT[:, i, :], qt_ring[hf][1], c0, n, kbs,
                                lambda kb, h=h: V1[:, kb, h, 0:65], lambda qs, po=po: po[:, qs, :], 65, bias_mm=bias_mm,
                                post=post, exp_scale=1.0, zero_regs=[(po.rearrange("p q e -> p (q e)"), 260)])
    k.pipe.drain()
    A.free(o_rc)
    for o, _ in qt_ring + tb_ring + sb_ring + pt_tiles:
        A.free(o)
    A.free(o_ot)
    A.free(o_KT)
    A.free(o_V)
    c_ring = [A.tile([16, 512], BF16) for _ in range(2)]
    s_ring = [A.tile([16, 512], BF16) for _ in range(2)]
    fcnt = 0
    fpend = []
    for tc in range(4):
        Cs = c_ring[tc % 2][1]
        Ss = s_ring[tc % 2][1]
        P.dma("sp", Cs, k.Ct_d[:, tc * 512:(tc + 1) * 512].rearrange("(j p) t -> p j t", p=128))
        P.dma("sp", Ss, k.St_d[:, tc * 512:(tc + 1) * 512].rearrange("(j p) t -> p j t", p=128))
        for gi in range(4):
            bank = PS[:, fcnt % 2, :]
            for j in range(16):
                P.mm(bank, AB[:, j, gi, 0:128], Cs[:, j, :], j == 0, False)
                P.mm(bank, AB[:, j, gi, 128:256], Ss[:, j, :], False, j == 15)
            otc = otc_r[:, fcnt % 2, :]
            fcnt += 1
            P.copy("act", otc, bank)
            prev = fpend
            fpend = []
            wout_accum(otc, 4 + gi, tc * 512, False, defer=fpend)
            for t_ in prev:
                t_()
    for t_ in fpend:
        t_()
    for o, _ in c_ring + s_ring:
        A.free(o)
    for o in (o_oc, o_AB, o_wox):
        A.free(o)


_NC_CACHE = {}


def _in_maps(inp, cores):
    hc = host_consts()
    rp = rpb_table(np.asarray(inp["c_rpb"], np.float32)[0])
    small = np.zeros((128, 4), np.float32)
    small[:, 0] = np.tile(np.asarray(inp["a_q_norm"], np.float32)[0], 2)
    small[:, 1] = np.tile(np.asarray(inp["a_k_norm"], np.float32)[0], 2)
    small[:, 2] = np.asarray(inp["b_subln"], np.float32)[0]
    f = lambda a: np.ascontiguousarray(np.asarray(a, np.float32))
    mod_w = f(inp["mod_w"])
    mod_b = f(inp["mod_b"])
    shared = {
        "c_ctx": f(inp["c_ctx"]).reshape(8, 128),
        "mod_w": mod_w, "mod_b48": mod_b.reshape(2, 48, 128), "mod_b6": mod_b.reshape(2, 6, 1024),
        "ln_g": f(inp["ln_g"]), "ln_b": f(inp["ln_b"]), "ffn_w1": f(inp["ffn_w1"]), "ffn_w2": f(inp["ffn_w2"]),
        "ab_w_in": f(inp["ab_w_in"])[0], "ab_w_out": f(inp["ab_w_out"])[0], "small": small,
        "b_lambda": f(inp["b_lambda"])[0].reshape(1, 256), "cd_w_in": f(inp["cd_w_in"])[0],
        "cd_w_out": f(inp["cd_w_out"])[0], "rpbt": rp.reshape(8, 128, -1),
        "cosT": hc["cosT"], "sinT": hc["sinT"], "cmat": hc["cmat"], "CS": hc["CS"], "Ct": hc["Ct"], "St": hc["St"],
    }
    x = f(inp["x"])
    ctx = f(inp["ctx"])
    c = f(inp["c"])
    maps = []
    for b in cores:
        m = dict(shared)
        m["x"] = np.ascontiguousarray(x[b])
        m["ctx"] = np.ascontiguousarray(ctx[b])
        m["c"] = np.ascontiguousarray(c[b].reshape(8, 128))
        maps.append(m)
    return maps


def kernel(**inputs):
    n = 8
    nc = build_program()
    maps = _in_maps(inputs, list(range(n)))
    res = run_bass_kernel_spmd(nc, maps, core_ids=list(range(n)))
    out = np.stack([np.asarray(r["out"], np.float32) for r in res.results], axis=0)
    return out
```

```python
import numpy as np
import concourse.bass as bass
import concourse.mybir as mybir

F32 = mybir.dt.float32
BF16 = mybir.dt.bfloat16
AF = mybir.ActivationFunctionType
ALU = mybir.AluOpType
AX = mybir.AxisListType

COMPUTE = ("pe", "act", "dve", "pool")
ENGS = ("pe", "act", "dve", "pool", "sp")
QUEUES = ("sp", "actq", "poolq")
Q_ENG = {"sp": "sp", "actq": "act", "poolq": "pool"}


class Op:
    __slots__ = ("eng", "emit", "deps", "is_dma", "queue", "ndep", "seq", "sem", "val", "order")

    def __init__(self, eng, emit, is_dma=False, queue=None):
        self.eng = eng
        self.emit = emit
        self.deps = []
        self.is_dma = is_dma
        self.queue = queue
        self.ndep = 0
        self.seq = None
        self.sem = None
        self.val = None
        self.order = 0


def _prod(xs):
    r = 1
    for v in xs:
        r *= int(v)
    return r


class Arena:
    def __init__(self, tensor, nbytes):
        self.t = tensor
        self.free_list = [(0, nbytes)]
        self.live = {}

    def alloc(self, nbytes, align=64, top=False):
        nbytes = (nbytes + align - 1) // align * align
        if top:
            for i in range(len(self.free_list) - 1, -1, -1):
                o, n = self.free_list[i]
                if n >= nbytes:
                    if n == nbytes:
                        self.free_list.pop(i)
                    else:
                        self.free_list[i] = (o, n - nbytes)
                    self.live[o + n - nbytes] = nbytes
                    return o + n - nbytes
            raise RuntimeError(f"arena OOM(top): need {nbytes}, free={self.free_list}")
        for i, (o, n) in enumerate(self.free_list):
            if n >= nbytes:
                if n == nbytes:
                    self.free_list.pop(i)
                else:
                    self.free_list[i] = (o + nbytes, n - nbytes)
                self.live[o] = nbytes
                return o
        raise RuntimeError(f"arena OOM: need {nbytes}, free={self.free_list}")

    def free(self, off):
        n = self.live.pop(off)
        fl = self.free_list + [(off, n)]
        fl.sort()
        merged = []
        for o, m in fl:
            if merged and merged[-1][0] + merged[-1][1] == o:
                merged[-1] = (merged[-1][0], merged[-1][1] + m)
            else:
                merged.append((o, m))
        self.free_list = merged

    def tile(self, shape, dtype, top=False):
        es = 4 if dtype == F32 else 2
        n = _prod(shape) * es
        off = self.alloc(n, top=top)
        ap = self.t[:, off // 2:(off + n) // 2]
        if dtype == F32:
            ap = ap.bitcast(F32)
        if len(shape) == 2:
            ap = ap.rearrange("p (a b) -> p a b", a=shape[0])
        elif len(shape) == 3:
            ap = ap.rearrange("p (a b c) -> p a b c", a=shape[0], b=shape[1])
        elif len(shape) == 4:
            ap = ap.rearrange("p (a b c d) -> p a b c d", a=shape[0], b=shape[1], c=shape[2])
        return off, ap


class Prog:
    def __init__(self, nc, stack, dma_pool=None, untracked=()):
        self.nc = nc
        self.ops = {e: [] for e in ENGS}
        self.track = {}
        self.untracked = set(untracked)
        self.dma_pool = dma_pool or {"sp": 24, "actq": 8, "poolq": 16}
        self.dma_count = {q: 0 for q in QUEUES}
        self.dma_ops = {q: [] for q in QUEUES}
        self.nops = 0
        self.order = {e: 0 for e in ENGS}
        self.sems = {e: stack.enter_context(nc.semaphore(f"s_{e}")) for e in COMPUTE}
        self.dsems = {q: [stack.enter_context(nc.semaphore(f"d_{q}{i}"))
                          for i in range(self.dma_pool[q])] for q in QUEUES}
        self.sig = {e: 0 for e in COMPUTE}
        self.waited = {e: {} for e in ENGS}
        self.pending_dma = {e: [] for e in ENGS}
        self.nblocks = 0

    def box(self, r):
        if isinstance(r, tuple):
            return (("key",) + r, 0, 1, 0, 1, 1 << 30)
        name = r.name
        if name in self.untracked:
            return None
        aps = r.ap
        off = int(r.offset)
        space = str(r.space)
        es = 4 if r.tensor.dtype == F32 else (2 if r.tensor.dtype == BF16 else mybir.dt.size(r.tensor.dtype))
        if "SB" in space or "PSUM" in space:
            row = _prod(r.tensor.shape[1:])
            p0 = off // row
            f0 = off % row
            pc = aps[0][1]
            f1 = f0 + sum((c - 1) * abs(s) for s, c in aps[1:]) + 1
            if "PSUM" in space:
                return (name, 0, 128, (f0 * es) // 2048 * 2048, ((f1 * es - 1) // 2048 + 1) * 2048, 2048)
            return (name, p0, p0 + pc, f0 * es, f1 * es, 2048)
        lo = off
        hi = off + sum((c - 1) * abs(s) for s, c in aps) + 1
        return (name, 0, 1, lo * es, hi * es, 1 << 18)

    @staticmethod
    def _ov(a, b):
        return a[1] < b[2] and b[1] < a[2] and a[3] < b[4] and b[3] < a[4]

    @staticmethod
    def _inside(a, b):
        return a[1] >= b[1] and a[2] <= b[2] and a[3] >= b[3] and a[4] <= b[4]

    def _pages(self, b):
        ps = b[5]
        return range(b[3] // ps, (b[4] - 1) // ps + 1)

    def add(self, eng, emit, reads=(), writes=(), queue=None):
        is_dma = queue is not None
        if is_dma:
            eng = Q_ENG[queue]
        op = Op(eng, emit, is_dma, queue)
        self.order[eng] += 1
        op.order = self.order[eng]
        deps = []
        rboxes = []
        for r in reads:
            if r is None or isinstance(r, (int, float)):
                continue
            b = self.box(r)
            if b is None:
                continue
            rboxes.append(b)
            st = self.track.get(b[0])
            if st is None:
                continue
            for pg in self._pages(b):
                ent = st.get(pg)
                if ent is None:
                    continue
                for wb, wop in ent[0]:
                    if self._ov(b, wb):
                        deps.append(wop)
                if b[0] == "PS":
                    for rb, rop in ent[1]:
                        if rop.eng != eng and self._ov(b, rb):
                            deps.append(rop)
        wboxes = []
        for w in writes:
            if w is None:
                continue
            b = self.box(w)
            if b is None:
                continue
            wboxes.append(b)
            st = self.track.get(b[0])
            if st is None:
                continue
            for pg in self._pages(b):
                ent = st.get(pg)
                if ent is None:
                    continue
                for wb, wop in ent[0]:
                    if self._ov(b, wb):
                        deps.append(wop)
                for rb, rop in ent[1]:
                    if self._ov(b, rb):
                        deps.append(rop)
        for b in wboxes:
            st = self.track.setdefault(b[0], {})
            for pg in self._pages(b):
                ent = st.setdefault(pg, [[], []])
                ent[0] = [(wb, wop) for wb, wop in ent[0] if not self._inside(wb, b)]
                ent[1] = [(rb, rop) for rb, rop in ent[1] if not self._inside(rb, b)]
                ent[0].append((b, op))
        for b in rboxes:
            st = self.track.setdefault(b[0], {})
            for pg in self._pages(b):
                ent = st.setdefault(pg, [[], []])
                if not is_dma:
                    ent[1] = [(rb, rop) for rb, rop in ent[1]
                              if not (rop.eng == eng and not rop.is_dma and self._inside(rb, b))]
                ent[1].append((b, op))
        best = {}
        seen = set()
        for d in deps:
            if d is op:
                continue
            if d.is_dma:
                if id(d) not in seen:
                    seen.add(id(d))
                    op.deps.append(d)
                continue
            if eng == "pe" and d.eng == "pe" and not is_dma:
                continue
            cur = best.get(d.eng)
            if cur is None or cur.order < d.order:
                best[d.eng] = d
        op.deps.extend(best.values())
        if is_dma:
            q = queue
            j = self.dma_count[q]
            n = self.dma_pool[q]
            if j >= n:
                prev = self.dma_ops[q][j - n]
                if id(prev) not in seen:
                    op.deps.append(prev)
            self.dma_count[q] = j + 1
            self.dma_ops[q].append(op)
            op.seq = j
        for d in op.deps:
            d.ndep += 1
        self.ops[eng].append(op)
        self.nops += 1
        return op

    def mm(self, out, lhsT, rhs, start=True, stop=True, **kw):
        return self.add("pe", lambda e: e.matmul(out, lhsT, rhs, start=start, stop=stop, **kw),
                        reads=[lhsT, rhs], writes=[out])

    def tr(self, out, in_, ident):
        return self.add("pe", lambda e: e.transpose(out, in_, ident), reads=[in_, ident], writes=[out])

    def act(self, out, in_, func, bias=0.0, scale=1.0, accum_out=None):
        kw = {}
        if accum_out is not None:
            kw["accum_out"] = accum_out
        return self.add("act", lambda e: e.activation(out, in_, func, bias=bias, scale=scale, **kw),
                        reads=[in_, bias, scale], writes=[out, accum_out])

    def tt(self, eng, out, a, b, op):
        return self.add(eng, lambda e: e.tensor_tensor(out, a, b, op), reads=[a, b], writes=[out])

    def ts(self, eng, out, a, s1, s2, op0, op1=None):
        if op1 is None:
            return self.add(eng, lambda e: e.tensor_scalar(out, a, s1, None, op0),
                            reads=[a, s1], writes=[out])
        return self.add(eng, lambda e: e.tensor_scalar(out, a, s1, s2, op0, op1),
                        reads=[a, s1, s2], writes=[out])

    def stt(self, eng, out, a, scalar, b, op0, op1):
        return self.add(eng, lambda e: e.scalar_tensor_tensor(out, a, scalar, b, op0, op1),
                        reads=[a, scalar, b], writes=[out])

    def copy(self, eng, out, in_):
        if eng == "act":
            return self.add(eng, lambda e: e.copy(out, in_), reads=[in_], writes=[out])
        return self.add(eng, lambda e: e.tensor_copy(out, in_), reads=[in_], writes=[out])

    def memset(self, eng, out, val):
        return self.add(eng, lambda e: e.memset(out, val), reads=[], writes=[out])

    def recip(self, out, in_):
        return self.add("dve", lambda e: e.reciprocal(out, in_), reads=[in_], writes=[out])

    def rsum(self, eng, out, in_):
        return self.add(eng, lambda e: e.reduce_sum(out, in_, AX.X), reads=[in_], writes=[out])

    def dma(self, queue, out, in_, carry=False, reads=None, writes=None, **kw):
        op = self.add(None, lambda e: e.dma_start(out=out, in_=in_, **kw),
                      reads=[in_] if reads is None else reads,
                      writes=[out] if writes is None else writes, queue=queue)
        if not carry:
            self.pending_dma[op.eng].append(op)
        return op

    def fence(self, eng, reads):
        return self.add(eng, None, reads=reads, writes=[])

    def flush(self):
        nc = self.nc
        sems, dsems = self.sems, self.dsems
        for e in ENGS:
            for op in self.ops[e]:
                if op.is_dma:
                    n = self.dma_pool[op.queue]
                    op.sem = dsems[op.queue][op.seq % n]
                    op.val = 16 * (op.seq // n + 1)
                elif op.emit is not None and op.ndep > 0:
                    self.sig[e] += 1
                    op.sem = sems[e]
                    op.val = self.sig[e]

        def emit_engine(ename, eng):
            waited = self.waited[ename]

            def wait_for(dlist):
                need = {}
                for d in dlist:
                    if d.sem is None:
                        continue
                    k = id(d.sem)
                    if k not in need or need[k][1] < d.val:
                        need[k] = (d.sem, d.val)
                for k, (s_, v) in need.items():
                    if waited.get(k, 0) >= v:
                        continue
                    eng.wait_ge(s_, v)
                    waited[k] = v

            for op in self.ops[ename]:
                wait_for(op.deps)
                if op.emit is None:
                    continue
                ins = op.emit(eng)
                if op.is_dma:
                    ins.then_inc(op.sem, 16)
                elif op.sem is not None:
                    ins.then_inc(op.sem, 1)
            wait_for(self.pending_dma[ename])
            self.pending_dma[ename] = []

        with nc.Block() as block:
            @block.tensor
            def _(e):
                emit_engine("pe", e)

            @block.scalar
            def _(e):
                emit_engine("act", e)

            @block.vector
            def _(e):
                emit_engine("dve", e)

            @block.gpsimd
            def _(e):
                emit_engine("pool", e)

            @block.sync
            def _(e):
                emit_engine("sp", e)
        self.ops = {e: [] for e in ENGS}
        self.nblocks += 1


import math
import numpy as np
import ml_dtypes
from contextlib import ExitStack
import concourse.bass as bass
import concourse.mybir as mybir
from concourse.bass_utils import run_bass_kernel_spmd

D = 1024
NX = 2048
NCTX = 256
NT = NX + NCTX
ALPHA = (2.0 * 2) ** 0.25
EPS = 1e-6
ATTN_SCALE = 64 ** -0.5
NEG = -30000.0
CHUNKS_ALL = [(0, 512), (512, 512), (1024, 512), (1536, 512), (2048, 256)]
CHUNKS_X = CHUNKS_ALL[:4]


def host_consts():
    f32 = np.float32
    pos = np.arange(NX)
    row = (pos // 64).astype(f32)
    col = (pos % 64).astype(f32)
    half = 32
    freqs = np.power(f32(10000.0), -np.arange(0, half, 2, dtype=f32) / f32(half)).astype(f32)

    def axis_angles(p):
        a = p[:, None] * freqs[None, :]
        return np.concatenate([a, a], axis=-1)

    ang = np.concatenate([axis_angles(row), axis_angles(col)], axis=-1).astype(f32)
    cosT = np.tile(np.cos(ang).astype(f32).T, (2, 1)).copy()
    sinT = np.tile(np.sin(ang).astype(f32).T, (2, 1)).copy()
    ident = np.eye(128, dtype=f32)
    Rm = np.zeros((128, 128), f32)
    for m in range(128):
        if m % 32 < 16:
            Rm[m + 16, m] = -1.0
        else:
            Rm[m - 16, m] = 1.0
    bones = np.zeros((128, 128), f32)
    bones[:64, :64] = 1.0
    bones[64:, 64:] = 1.0
    cmat = np.concatenate([ident, Rm, bones], axis=1)
    k = np.arange(128)
    angc = 2 * np.pi * ((k[:, None] * k[None, :]) % 128) / 128.0
    CS = np.concatenate([np.cos(angc), -np.sin(angc)], axis=1) / np.sqrt(128.0)
    t = np.arange(NX, dtype=np.int64)
    angt = 2 * np.pi * ((t[:, None] * t[None, :]) % NX) / float(NX)
    Ct = (np.cos(angt) / np.sqrt(float(NX))).astype(f32).astype(ml_dtypes.bfloat16)
    St = (np.sin(angt) / np.sqrt(float(NX))).astype(f32).astype(ml_dtypes.bfloat16)
    return dict(cosT=cosT, sinT=sinT, cmat=cmat, CS=CS.astype(f32), Ct=Ct, St=St)


TI0 = 9
NTI = 9 + 8 + 8
TF0 = NTI + 9
NTILE = NTI + 9 + 15 + 8


def rpb_table(rpb):
    H = rpb.shape[0]
    cp = np.arange(64)[:, None]
    c = np.arange(64)[None, :]
    cs = np.clip(c - 8, 0, 48)
    valid = (cp >= cs) & (cp < cs + 16)
    dc = np.clip(cp - c + 15, 0, 30)
    tiles = np.full((H, NTILE, 64, 64), NEG, np.float32)

    def tile_for(idx):
        dr = 7 - idx
        g = rpb[:, dr + 7][:, dc]
        return np.where(valid[None], g, np.float32(NEG))

    for idx in range(4, 12):
        tiles[:, TI0 + idx - 4] = tile_for(idx)
    for idx in range(0, 15):
        tiles[:, TF0 + idx] = tile_for(idx)
    t = tiles.transpose(0, 2, 1, 3)
    sh = np.full_like(t, NEG)
    sh[:, :, 1:, :] = t[:, :, :-1, :]
    return np.ascontiguousarray(np.concatenate([t, sh], axis=1)).astype(ml_dtypes.bfloat16)


def natten_segments(kb, c):
    rp = 2 * kb
    segs = []
    rows = list(range(8 * c, 8 * c + 8))
    i = 0
    while i < 8:
        r = rows[i]
        j = i
        if 4 <= r <= 28:
            while j < 8 and 4 <= rows[j] <= 28:
                j += 1
            pos0 = TI0 + (r - rp + 7) - 4
            assert 1 <= pos0 and pos0 + (j - i) <= NTI, (kb, c, pos0)
            segs.append((i, j - i, pos0))
        else:
            lo = r <= 3
            while j < 8 and ((rows[j] <= 3) if lo else (rows[j] >= 29)):
                j += 1
            w0 = 0 if lo else 24
            if w0 <= rp < w0 + 8:
                pos0 = TF0 + (r - rp + 7)
                assert TF0 <= pos0 - 1 and pos0 + (j - i) <= TF0 + 15, (kb, c, pos0)
            else:
                pos0 = NTI + 1
            segs.append((i, j - i, pos0))
        i = j
    return segs


class K:
    pass


def build_program(stop_after=99, debug=False, sub=99):
    nc = bass.Bass("TRN2", target_bir_lowering=False)
    k = K()
    k.nc = nc
    k.sub = sub
    k.fmp = FMPipe()
    k.ut_it = 0

    def din(name, shape, dt=F32):
        return nc.dram_tensor(name, list(shape), dt, kind="ExternalInput").ap()

    k.x_d = din("x", [NX, D])
    k.ctx_d = din("ctx", [NCTX, D])
    k.c_d = din("c", [8, 128])
    k.cctx_d = din("c_ctx", [8, 128])
    k.modw_d = din("mod_w", [2, D, 6 * D])
    k.modb48_d = din("mod_b48", [2, 48, 128])
    k.modb6_d = din("mod_b6", [2, 6, D])
    k.lng_d = din("ln_g", [2, 2, D])
    k.lnb_d = din("ln_b", [2, 2, D])
    k.w1_d = din("ffn_w1", [2, D, 4 * D])
    k.w2_d = din("ffn_w2", [2, 4 * D, D])
    k.abwin_d = din("ab_w_in", [D, 2304])
    k.abwout_d = din("ab_w_out", [D, D])
    k.small_d = din("small", [128, 4])
    k.lam_d = din("b_lambda", [1, 256])
    k.cdwin_d = din("cd_w_in", [D, 2048])
    k.cdwout_d = din("cd_w_out", [D, D])
    k.rpbt_d = din("rpbt", [8, 128, NTILE * 64], BF16)
    k.cosT_d = din("cosT", [128, NX])
    k.sinT_d = din("sinT", [128, NX])
    k.cmat_d = din("cmat", [128, 384])
    k.CS_d = din("CS", [128, 256])
    k.Ct_d = din("Ct", [NX, NX], BF16)
    k.St_d = din("St", [NX, NX], BF16)
    k.out_d = nc.dram_tensor("out", [NX, D], F32, kind="ExternalOutput").ap()
    k.QS_d = nc.dram_tensor("qs_scratch", [8, 128, NT], BF16, kind="Internal").ap()
    k.GS_d = nc.dram_tensor("gs_scratch", [2, 2, 2, D], F32, kind="Internal").ap()
    k.MW1_d = nc.dram_tensor("modw1_bf16", [6, D, D], BF16, kind="Internal").ap()
    if debug:
        k.dbg_d = nc.dram_tensor("dbg", [128, 18, D], F32, kind="ExternalOutput").ap()

    untracked = ["x", "ctx", "c", "c_ctx", "mod_w", "mod_b48", "mod_b6", "ln_g", "ln_b", "ffn_w1", "ffn_w2",
                 "ab_w_in", "ab_w_out", "small", "b_lambda", "cd_w_in", "cd_w_out", "rpbt", "cosT", "sinT",
                 "cmat", "CS", "Ct", "St"]

    with ExitStack() as g:
        P = Prog(nc, g, untracked=untracked)
        k.P = P
        sb = lambda n, s, d: g.enter_context(nc.sbuf_tensor(n, s, d))
        k.XR = sb("XR", [128, 18, D], F32)
        k.cmat = sb("cmat_sb", [128, 384], F32)
        k.identb = sb("identb", [128, 128], BF16)[:]
        k.cmatb = sb("cmatb", [128, 256], BF16)
        k.Rmb = k.cmatb[:, 0:128]
        k.bonesb = k.cmatb[:, 128:256]
        k.FM = sb("FM", [128, 2, 2, 4, 8], F32)
        k.small = sb("small_sb", [128, 4], F32)
        k.epsc = sb("epsc", [128, 1], F32)
        k.misc = sb("misc", [128, 16], F32)
        k.sc2 = sb("sc2", [128, 8, 2], BF16)
        k.zb = sb("zb", [128, 384], BF16)
        k.mbT = sb("mbT", [128, 2, 48], F32)
        k.PS = g.enter_context(nc.psum_tensor("PS", [128, 8, 512], F32))
        ARENA_BYTES = 130 * 1024
        at = sb("arena", [128, ARENA_BYTES // 2], BF16)
        k.A = Arena(at, ARENA_BYTES)
        k.ident = k.cmat[:, 0:128]
        k.Rm = k.cmat[:, 128:256]
        k.bones = k.cmat[:, 256:384]

        for i in range(4):
            P.dma("sp", k.XR[:, 4 * i:4 * i + 4, :],
                  k.x_d[512 * i:512 * (i + 1), :].rearrange("(t p) d -> p t d", p=128))
        P.dma("sp", k.XR[:, 16:18, :], k.ctx_d.rearrange("(t p) d -> p t d", p=128))
        P.dma("sp", k.cmat[:], k.cmat_d)
        P.dma("sp", k.small[:], k.small_d)
        P.memset("dve", k.epsc[:], EPS)
        P.copy("dve", k.identb, k.ident)
        P.memset("pool", k.zb[:], 0.0)
        P.copy("dve", k.cmatb[:], k.cmat[:, 128:384])

        mod_prologue(k)
        phase_mod(k, 0, (0, 1))

        def ln_with_ut(l_ln, j, nb_ln, l_ut, vsh, vsc, nb_ut, mod_l=None, mod_vs=(), out=False, mod_early=False,
                       pre_stats=None):
            pre = k.A.tile([8, nb_ut * 128], BF16) if nb_ut else None
            tiles = [k.A.tile([8, 1024], BF16) for _ in range(3)] if mod_vs else []
            ring = [t for _, t in tiles]
            groups = {t0 + nt - 1 + 2: (t0, nt) for (t0, nt) in ut_groups(nb_ut)} if nb_ut else {}
            mv = list(mod_vs)

            def hook(tb):
                if mv and (mod_early or tb % 3 == 0):
                    mod_step(k, mod_l, mv.pop(0), ring, banks=(6, 7, 6, 7))
                if tb in groups:
                    assert not (mod_early and mv)
                    t0, nt = groups.pop(tb)
                    build_UT_group(k, pre[1], l_ut, vsh, vsc, t0, nt)

            layer_norm(k, l_ln, j, nb_ln, out=out, hook=hook, pre_stats=pre_stats)
            while mv:
                mod_step(k, mod_l, mv.pop(0), ring, banks=(6, 7, 6, 7))
            for tb in sorted(groups):
                t0, nt = groups[tb]
                build_UT_group(k, pre[1], l_ut, vsh, vsc, t0, nt)
            for o, _ in tiles:
                k.A.free(o)
            return pre

        if stop_after >= 1:
            layer0_mixer(k)
        if stop_after >= 3:
            pre = ln_with_ut(0, 0, 18, 0, 2, 3, 18, mod_l=1, mod_vs=(0, 1, 2))
            o_st, st = k.A.tile([18, 2, 6], F32)
            o_mv, mv = k.A.tile([18, 2], F32)
            ffn(k, 0, 18, pre, stats=(o_st, st, o_mv, mv))
            pre = ln_with_ut(0, 1, 18, 1, 0, 1, 18, pre_stats=(o_st, st, o_mv, mv))
        if stop_after >= 4:
            layer1_mixer(k, pre)
        if stop_after >= 6:
            pre = ln_with_ut(1, 0, 16, 1, 2, 3, 16, mod_l=1, mod_vs=(3, 4, 5), mod_early=True)
            o_st, st = k.A.tile([16, 2, 6], F32)
            o_mv, mv = k.A.tile([16, 2], F32)
            ffn(k, 1, 16, pre, stats=(o_st, st, o_mv, mv))
            ln_with_ut(1, 1, 16, 1, 0, 1, 0, out=True, pre_stats=(o_st, st, o_mv, mv))
        if debug:
            P.dma("sp", k.dbg_d, k.XR[:])
            P.fence("sp", [k.dbg_d])
        P.fence("sp", [k.out_d])
        P.flush()
    return nc


def ps_bank(k, b, n=512):
    return k.PS[:, b, 0:n]


def mod_prologue(k):
    P, A, PS = k.P, k.A, k.PS
    o_c16, c16 = A.tile([128], F32)
    o_m48, m48 = A.tile([2, 128], F32)
    sc2 = k.sc2
    mbT = k.mbT
    P.dma("sp", c16[0:8, :], k.c_d)
    P.dma("sp", c16[8:16, :], k.cctx_d)
    P.tr(PS[:, 0, 0:16], c16[0:16, :], k.ident[0:16, 0:16])
    P.act(sc2[:, :, 0], PS[:, 0, 0:8], AF.Silu)
    P.act(sc2[:, :, 1], PS[:, 0, 8:16], AF.Silu)
    for l in range(2):
        P.dma("sp", m48[0:48, l, :], k.modb48_d[l])
        P.tr(PS[:, 1, l * 48:(l + 1) * 48], m48[0:48, l, :], k.ident[0:48, 0:48])
        P.copy("dve", mbT[:, l, :], PS[:, 1, l * 48:(l + 1) * 48])
    A.free(o_c16)
    A.free(o_m48)
    k.mod_it = 0


def mod_step(k, l, v, wv_ring, banks=(2, 3, 4, 5)):
    P, A, PS = k.P, k.A, k.PS
    sc2, mbT = k.sc2, k.mbT
    it = k.mod_it
    k.mod_it += 1
    Wv = wv_ring[it % len(wv_ring)]
    if l == 0:
        src = k.modw_d[l, :, v * 1024:(v + 1) * 1024].rearrange("(k p) n -> p k n", p=128)
        P.dma("poolq", Wv[:, 0:4, :], src[:, 0:4, :])
        P.dma("poolq", Wv[:, 4:8, :], src[:, 4:8, :])
    else:
        src = k.MW1_d[v].rearrange("(k p) n -> p k n", p=128)
        P.dma("sp", Wv[:, 0:4, :], src[:, 0:4, :], reads=[("MW1", v)])
        P.dma("sp", Wv[:, 4:8, :], src[:, 4:8, :], reads=[("MW1", v)])
    if v in (0, 1, 3, 4):
        vi = {0: 0, 1: 1, 3: 2, 4: 3}[v]
        bank = banks[it % 2]
        for cc in range(8):
            for kk in range(8):
                P.mm(PS[:, bank, cc * 2:cc * 2 + 2], Wv[:, kk, cc * 128:(cc + 1) * 128], sc2[:, kk, :],
                     kk == 0, kk == 7)
        psv = PS[:, bank, 0:16].rearrange("p (c w) -> p c w", w=2)
        for w in range(2):
            P.stt("dve", k.FM[:, l, w, vi, :], psv[:, :, w], 1.0 if v in (1, 4) else 0.0,
                  mbT[:, l, v * 8:(v + 1) * 8], ALU.add, ALU.add)
    else:
        vi = 0 if v == 2 else 1
        o_brow, brow = A.tile([1024], F32)
        o_grow, grow = A.tile([2, 1024], F32)
        P.dma("sp", brow[0:1, :], k.modb6_d[l, v:v + 1, :])
        for w in range(2):
            if l == 1 and w == 1:
                continue
            for hf in range(2):
                bank = banks[2 + hf]
                for kk in range(8):
                    P.mm(PS[0:1, bank, 0:512], sc2[:, kk, w:w + 1], Wv[:, kk, hf * 512:(hf + 1) * 512],
                         kk == 0, kk == 7)
                P.tt("dve", grow[0:1, w, hf * 512:(hf + 1) * 512], PS[0:1, bank, 0:512],
                     brow[0:1, hf * 512:(hf + 1) * 512], ALU.add)
            P.dma("sp", k.GS_d[l, w, vi:vi + 1, :], grow[0:1, w, :], writes=[("GS", l, w, vi)])
        A.free(o_brow)
        A.free(o_grow)


def phase_mod(k, l, vs, ring=None, banks=(2, 3, 4, 5)):
    P, A = k.P, k.A
    tiles = None
    if ring is None:
        tiles = [A.tile([8, 1024], BF16) for _ in range(2)]
        ring = [t for _, t in tiles]
    for v in vs:
        mod_step(k, l, v, ring, banks=banks)
    if tiles is not None:
        for o, _ in tiles:
            A.free(o)


def load_gate(k, dst, l, w, vi, queue="sp"):
    k.P.dma(queue, dst, k.GS_d[l, w, vi:vi + 1, :].partition_broadcast(128), reads=[("GS", l, w, vi)])


def build_UT_group(k, UT, l, v_shift, v_scale, t0, nt):
    P, PS = k.P, k.PS
    w = 0 if t0 < 16 else 1
    for kk in range(8):
        it = k.ut_it
        k.ut_it += 1
        bank = PS[:, it % 2, :]
        for j in range(nt):
            P.tr(bank[:, j * 128:(j + 1) * 128], k.XR[:, t0 + j, kk * 128:(kk + 1) * 128], k.ident)
        dst = UT[:, kk, t0 * 128:(t0 + nt) * 128]
        sc = k.FM[:, l, w, v_scale, kk:kk + 1]
        bi = k.FM[:, l, w, v_shift, kk:kk + 1]
        if it % 2 == 0:
            P.act(dst, bank[:, 0:nt * 128], AF.Identity, bias=bi, scale=sc)
        else:
            P.ts("dve", dst, bank[:, 0:nt * 128], sc, bi, ALU.mult, ALU.add)


def ut_groups(nblocks):
    groups = [(0, 4), (4, 4), (8, 4), (12, 4)]
    if nblocks > 16:
        groups.append((16, 2))
    return groups


def build_UT(k, UT, l, v_shift, v_scale, nblocks):
    for (t0, nt) in ut_groups(nblocks):
        build_UT_group(k, UT, l, v_shift, v_scale, t0, nt)


class FMPipe:
    def __init__(self):
        self.items = []

    def push(self, st):
        self.items.append([st, 0])
        self._step()

    def _step(self):
        for it in list(self.items[::-1]):
            st, idx = it
            st[idx]()
            it[1] += 1
        self.items = [it for it in self.items if it[1] < 3]

    def drain(self):
        while self.items:
            self._step()


def fm_block(k, UT, wt, chunks, dest_fn, tmps, norm_col=None, rope=False, after=None, itbase=0, pre_scale=None):
    P, PS = k.P, k.PS
    nt = len(tmps)
    for ci, (c0, n) in enumerate(chunks):
        it = itbase + ci
        bank = PS[:, it % 2, 0:n]
        is_ctx = c0 >= NX
        do_rope = rope and not is_ctx
        q32, sq, t1, qb = [t[:, 0:n] for t in tmps[it % nt]]
        bn = PS[:, 2 + it % 2, 0:n]
        br = PS[:, 4 + it % 2, 0:n]

        def st1(bank=bank, c0=c0, n=n, ci=ci, do_rope=do_rope, q32=q32, qb=qb):
            for kk in range(8):
                P.mm(bank, wt[:, kk, :], UT[:, kk, c0:c0 + n], kk == 0, kk == 7)
            if norm_col is None and not do_rope:
                dest = dest_fn(ci, c0, n)
                if pre_scale is not None:
                    P.add("act", lambda e, o_=dest, i_=bank, m_=pre_scale: e.mul(o_, i_, m_), reads=[bank],
                          writes=[dest])
                else:
                    P.copy("act", dest, bank)
                if after is not None:
                    after(ci, c0, n, dest)
                return
            P.copy("act", q32, bank)
            if norm_col is not None:
                P.act(qb, bank, AF.Square)
            else:
                P.copy("act", qb, bank)

        def st2(bn=bn, q32=q32, sq=sq, qb=qb, do_rope=do_rope):
            if norm_col is None:
                return
            P.mm(bn, k.bonesb, qb, True, True)
            P.act(sq, bn, AF.Ln, bias=k.epsc[:, 0:1], scale=1.0 / 64.0)
            P.act(sq, sq, AF.Exp, scale=-0.5)
            P.stt("dve", q32, q32, norm_col, sq, ALU.mult, ALU.mult)
            if do_rope:
                P.copy("act", qb, q32)

        def st3(br=br, q32=q32, sq=sq, t1=t1, qb=qb, do_rope=do_rope, c0=c0, n=n, ci=ci):
            if norm_col is None and not do_rope:
                return
            dest = dest_fn(ci, c0, n)
            if do_rope:
                P.mm(br, k.Rmb, qb, True, True)
                P.tt("pool", t1, q32, k.cosT[:, c0:c0 + n], ALU.mult)
                P.tt("dve", sq, br, k.sinT[:, c0:c0 + n], ALU.mult)
                P.tt("dve", dest, t1, sq, ALU.add)
            else:
                P.copy("act", dest, q32)
            if after is not None:
                after(ci, c0, n, dest)

        k.fmp.push([st1, st2, st3])


def alloc_fm_tmps(k, nsets=3):
    offs, tmps = [], []
    for _ in range(nsets):
        s = []
        for _ in range(3):
            o, t = k.A.tile([512], F32)
            offs.append(o)
            s.append(t)
        o, t = k.A.tile([512], BF16)
        offs.append(o)
        s.append(t)
        tmps.append(s)
    return offs, tmps


def load_w_cols(k, dst, wd, c0, n, queue="poolq"):
    k.P.dma(queue, dst, wd[:, c0:c0 + n].rearrange("(k p) n -> p k n", p=128))


class Pipe:
    def __init__(self, L=2):
        self.q = []
        self.L = L
        self.tasks = []

    def push(self, S, E_PV, post=None):
        bank = S()
        self.q.append((bank, E_PV, post))
        while len(self.q) > self.L:
            self._pop()

    def _pop(self):
        bank, E_PV, post = self.q.pop(0)
        E_PV(bank)
        if self.tasks:
            self.tasks.pop(0)()
        if post is not None:
            post()

    def drain(self):
        while self.q:
            self._pop()
        while self.tasks:
            self.tasks.pop(0)()


def attn_head_chunk(k, kT, qT, c0, n, kbs, v_fn, po_fn, nv, bias_fn=None, cnt=[0], bank_first=(0,), post=None,
                    exp_scale=ATTN_SCALE, bias_mm=None, zero_regs=()):
    P, PS = k.P, k.PS
    nq = n // 128
    nk = len(kbs)

    def S(kb):
        bank = PS[:, cnt[0] % 3, 0:n]
        extra = bias_mm(kb) if bias_mm is not None else None
        P.mm(bank, kT[:, kb * 128:(kb + 1) * 128], qT[:, c0:c0 + n], True, not extra)
        if extra:
            for ei, (col0, ncol, rhs) in enumerate(extra):
                P.mm(bank[:, col0:col0 + ncol], k.identb, rhs, False, ei == len(extra) - 1)
        cnt[0] += 1
        return bank

    def E_PV(bank, idx, kb):
        pt = k.pt_ring[k.pt_cnt % len(k.pt_ring)][:, 0:n]
        k.pt_cnt += 1
        sbias = bias_fn(kb, bank, n) if bias_fn is not None else None
        if sbias is not None:
            P.act(pt, sbias, AF.Exp)
        else:
            P.act(pt, bank, AF.Exp, scale=exp_scale)
        if idx == 0:
            for (reg, ncol) in zero_regs:
                P.mm(reg, k.zb[:, 0:128], k.zb[:, 0:ncol], True, False)
        for qs in range(nq):
            P.mm(po_fn(qs), pt[:, qs * 128:(qs + 1) * 128], v_fn(kb), False,
                 idx == nk - 1 and (qs == nq - 1 or (qs + 1) in bank_first))

    for idx, kb in enumerate(kbs):
        k.pipe.push(lambda kb=kb: S(kb), lambda bank, idx=idx, kb=kb: E_PV(bank, idx, kb),
                    post if idx == nk - 1 else None)


def layer0_mixer(k):
    P, A, PS = k.P, k.A, k.PS
    l = 0
    o_UT, UT = A.tile([8, NT], BF16)
    o_cos, cosT = A.tile([NX], F32)
    o_sin, sinT = A.tile([NX], F32)
    k.cosT, k.sinT = cosT, sinT
    P.dma("sp", cosT, k.cosT_d)
    P.dma("sp", sinT, k.sinT_d)
    build_UT(k, UT, l, 0, 1, 18)
    if k.sub <= 0:
        return
    toffs, tmps = alloc_fm_tmps(k)
    qn_col = k.small[:, 0:1]
    kn_col = k.small[:, 1:2]
    wq_ring = [A.tile([8, 256], BF16) for _ in range(2)]
    qst_ring = [A.tile([512], BF16) for _ in range(3)]
    mtiles = [A.tile([8, 1024], BF16) for _ in range(2)]
    mring = [t for _, t in mtiles]
    mvs = [2, 3, 4, 5]
    itb = 0
    load_w_cols(k, wq_ring[0][1], k.abwin_d, 0, 256)
    for pi in range(4):
        _, wq = wq_ring[pi % 2]
        if pi + 1 < 4:
            load_w_cols(k, wq_ring[(pi + 1) % 2][1], k.abwin_d, (pi + 1) * 256, 256)
        if mvs:
            mod_step(k, 0, mvs.pop(0), mring, banks=(6, 7, 6, 7))
        for sub in range(2):
            i = pi * 2 + sub
            st = {"c": 0}

            def dest_fn(ci, c0, n, itb=itb):
                return qst_ring[(itb + ci) % 3][1][:, 0:n]

            def after(ci, c0, n, dest, i=i):
                P.dma("sp", k.QS_d[i, :, c0:c0 + n], dest, writes=[("QS", i, ci)])

            fm_block(k, UT, wq[:, :, sub * 128:(sub + 1) * 128], CHUNKS_ALL, dest_fn, tmps,
                     norm_col=qn_col if i < 4 else None, rope=True, after=after, itbase=itb)
            itb += 5
    k.fmp.drain()
    for o, _ in wq_ring + qst_ring + mtiles:
        A.free(o)
    if k.sub <= 1:
        return
    o_KT, KT = A.tile([6, NT], BF16, top=True)
    o_wk, wk = A.tile([8, 512], BF16)
    load_w_cols(k, wk, k.abwin_d, 1280, 512)
    o_wkd, wkd = A.tile([2, 8, 128], BF16)
    for kvh in range(2):
        for hf in range(2):
            load_w_cols(k, wkd[:, kvh, :, hf * 64:(hf + 1) * 64], k.abwin_d, 1024 + kvh * 64, 64)
    for j in range(6):
        wt = wkd[:, j, :, :] if j < 2 else wk[:, :, (j - 2) * 128:(j - 1) * 128]
        fm_block(k, UT, wt, CHUNKS_ALL, lambda ci, c0, n, j=j: KT[:, j, c0:c0 + n], tmps,
                 norm_col=kn_col if j < 2 else None, rope=True, itbase=itb)
        itb += 5
    k.fmp.drain()
    A.free(o_wk)
    A.free(o_wkd)
    for o in toffs:
        A.free(o)
    A.free(o_cos)
    A.free(o_sin)
    if k.sub <= 2:
        return
    o_Va, Va1 = A.tile([18, 2, 80], BF16, top=True)
    o_Vb, Vb1 = A.tile([18, 4, 144], BF16, top=True)
    o_wv, wv = A.tile([8, 640], BF16)
    load_w_cols(k, wv[:, :, 0:128], k.abwin_d, 1152, 128)
    load_w_cols(k, wv[:, :, 128:640], k.abwin_d, 1792, 512)
    P.memset("pool", Va1[:, :, :, 64:65], 1.0)
    P.memset("pool", Vb1[:, :, :, 128:129], 1.0)
    for t in range(18):
        ba = PS[:, 2 * (t % 2), 0:128]
        bb = PS[:, 2 * (t % 2) + 1, 0:512]
        for kk in range(8):
            P.mm(ba, UT[:, kk, t * 128:(t + 1) * 128], wv[:, kk, 0:128], kk == 0, kk == 7)
        for kk in range(8):
            P.mm(bb, UT[:, kk, t * 128:(t + 1) * 128], wv[:, kk, 128:640], kk == 0, kk == 7)
        P.copy("act", Va1[:, t, :, 0:64], ba.rearrange("p (h d) -> p h d", h=2))
        P.copy("dve", Vb1[:, t, :, 0:128], bb.rearrange("p (h d) -> p h d", h=4))
    A.free(o_wv)
    A.free(o_UT)

    if k.sub <= 3:
        return
    o_lb, lb = A.tile([256], F32)
    P.dma("sp", lb, k.lam_d.partition_broadcast(128))
    lb4 = lb.rearrange("p (a b d) -> p a b d", a=2, b=2)
    o_lp, lp = A.tile([2, 64], F32)
    P.tt("dve", lp, lb4[:, :, 0, :], lb4[:, :, 1, :], ALU.mult)
    lsum = k.misc[:, 0:2]
    P.rsum("dve", lsum, lp)
    P.act(lsum, lsum, AF.Exp)
    lam_init = 0.8 - 0.6 * math.exp(0.0)
    neglam = k.misc[:, 2:3]
    P.stt("dve", neglam, k.misc[:, 1:2], -lam_init, k.misc[:, 0:1], ALU.add, ALU.subtract)
    rowsc = k.misc[:, 3:4]
    P.ts("dve", rowsc, k.small[:, 2:3], 1.0 - lam_init, None, ALU.mult)
    A.free(o_lb)
    A.free(o_lp)
    setup = attention_setup(k)
    o_wox, WoX = A.tile([8, D], BF16)
    o_woc, WoC = A.tile([8, D], BF16)
    o_g, Gt = A.tile([2, D], F32)
    o_ws, wst = A.tile([2, D], F32)
    load_gate(k, Gt[:, 0, :], l, 0, 0)
    load_gate(k, Gt[:, 1, :], l, 1, 0)
    for kk in range(8):
        P.dma("sp", wst[:, kk % 2, :], k.abwout_d[kk * 128:(kk + 1) * 128, :])
        rs = rowsc if kk >= 4 else 1.0
        P.stt("dve", WoX[:, kk, :], wst[:, kk % 2, :], rs, Gt[:, 0, :], ALU.mult, ALU.mult)
        P.stt("dve", WoC[:, kk, :], wst[:, kk % 2, :], rs, Gt[:, 1, :], ALU.mult, ALU.mult)
    A.free(o_g)
    A.free(o_ws)
    if k.sub <= 4:
        return
    k.bg_dmas = [(lambda v=v: P.dma("poolq", k.MW1_d[v], k.modw_d[1, :, v * 1024:(v + 1) * 1024],
                                    writes=[("MW1", v)], carry=True)) for v in range(6)]
    attention_core(k, 0, KT, (Va1, Vb1), (WoX, WoC), setup)
    while k.bg_dmas:
        k.bg_dmas.pop(0)()
    for o in (o_KT, o_Va, o_Vb, o_wox, o_woc):
        A.free(o)


def attention_setup(k):
    P, A = k.P, k.A
    qt_ring = [A.tile([NT], BF16) for _ in range(4)]
    for sl in range(2):
        P.memset("pool", qt_ring[2 * sl][1][64:128, :], 0.0)
        P.memset("pool", qt_ring[2 * sl + 1][1][0:64, :], 0.0)

    def load_q(i):
        sl = i % 2
        rd = [("QS", i, ci) for ci in range(5)]
        P.dma("sp", qt_ring[2 * sl][1][0:64, :], k.QS_d[i, 0:64, :], reads=rd)
        P.dma("sp", qt_ring[2 * sl + 1][1][64:128, :], k.QS_d[i, 64:128, :], reads=rd)

    load_q(0)
    return qt_ring, load_q


def attention_core(k, l, KT, V, Wo, setup):
    P, A, PS = k.P, k.A, k.PS
    Va1, Vb1 = V
    WoX, WoC = Wo
    qt_ring, load_q = setup

    pt_tiles = [A.tile([512], BF16) for _ in range(4)]
    k.pt_ring = [t for _, t in pt_tiles]
    k.pt_cnt = 0
    k.pipe = Pipe(2)
    o_ot, otok_r = A.tile([3, 4, 128], BF16)
    o_oc, otc_r = A.tile([2, 512], BF16)
    o_ta, tacc_r = A.tile([2, 4, 128], F32)
    o_sq, sqt = A.tile([4, 128], F32)
    o_rc, rcs = A.tile([4, 16], F32)
    PST = PS[:, 5, :].bitcast(BF16)
    chunks = CHUNKS_ALL
    pocnt = [0]
    ycnt = [0]
    cc = 0

    def finish_chunk(i, c0, n, otok, otc, rstd_col):
        nq = n // 128

        def t_transpose():
            for qs in range(nq):
                P.tr(PST[:, qs * 128:(qs + 1) * 128], otok[:, qs, :], k.identb)
            P.copy("dve", otc[:, 0:n], PST[:, 0:n])

        k.pipe.tasks.append(t_transpose)
        for qs in range(nq):
            tb = c0 // 128 + qs
            W = WoX if tb < 16 else WoC
            for h2 in range(2):
                def t_y(qs=qs, tb=tb, W=W, h2=h2):
                    bank = PS[:, 6 + ycnt[0] % 2, :]
                    ycnt[0] += 1
                    P.mm(bank, otc[:, qs * 128:(qs + 1) * 128], W[:, i, h2 * 512:(h2 + 1) * 512], True, True)
                    xs = k.XR[:, tb, h2 * 512:(h2 + 1) * 512]
                    if rstd_col is not None:
                        P.stt("dve", xs, bank, rstd_col[:, qs:qs + 1], xs, ALU.mult, ALU.add)
                    elif i == 0:
                        P.stt("dve", xs, xs, ALPHA, bank, ALU.mult, ALU.add)
                    else:
                        P.tt("dve", xs, bank, xs, ALU.add)

                k.pipe.tasks.append(t_y)

    for i in range(8):
        qth = (qt_ring[2 * (i % 2)][1], qt_ring[2 * (i % 2) + 1][1])
        if i + 1 < 8:
            load_q(i + 1)
        if i >= 1 and k.bg_dmas:
            k.bg_dmas.pop(0)()
        for (c0, n) in chunks:
            nq = n // 128
            kbs = list(range(18)) if c0 < NX else [16, 17]
            otok = otok_r[:, cc % 3]
            otc = otc_r[:, cc % 2, :]
            tacc = tacc_r[:, cc % 2]
            rcv = rcs[:, cc % 4, :]
            cc += 1
            while len(k.pipe.tasks) > 17:
                k.pipe.tasks.pop(0)()
            if i < 4:
                kvh = i // 2
                for hf in range(2):
                    po = PS[:, 3 + pocnt[0] % 2, 0:260].rearrange("p (q e) -> p q e", e=65)
                    pocnt[0] += 1

                    def post(hf=hf, po=po, otok=otok, otc=otc, nq=nq, rcv=rcv, i=i, c0=c0, n=n):
                        rc = rcv[:, 4 * hf:4 * hf + 4]
                        P.recip(rc[:, 0:nq], po[:, 0:nq, 64])
                        P.tt("dve", otok[:, 0:nq, 64 * hf:64 * hf + 64], po[:, 0:nq, 0:64],
                             rc[:, 0:nq].unsqueeze(2).to_broadcast([128, nq, 64]), ALU.mult)
                        if hf == 1:
                            finish_chunk(i, c0, n, otok, otc, None)

                    attn_head_chunk(k, KT[:, kvh, :], qth[hf], c0, n, kbs,
                                    lambda kb, kvh=kvh: Va1[:, kb, kvh, 0:65], lambda qs, po=po: po[:, qs, :], 65,
                                    post=post, zero_regs=[(po.rearrange("p q e -> p (q e)")[:, 0:65 * nq], 65 * nq)])
            else:
                h = i - 4
                for j in range(2):
                    poA = PS[:, 3, 0:258].rearrange("p (q e) -> p q e", e=129)
                    poB = PS[:, 4, 0:258].rearrange("p (q e) -> p q e", e=129)
                    pof = lambda qs, poA=poA, poB=poB: (poA if qs < 2 else poB)[:, qs % 2, :]

                    def post(j=j, pof=pof, otok=otok, otc=otc, tacc=tacc, nq=nq, rcv=rcv, i=i, c0=c0, n=n,
                             poA_=poA, poB_=poB):
                        rc = rcv[:, 0:4]
                        rc1 = rcv[:, 4:8]
                        ss4 = rcv[:, 8:12]
                        pA = pof(0)
                        banks2 = [(0, poA_)] + ([(2, poB_)] if nq > 2 else [])
                        for q0, pb in banks2:
                            P.recip(rc[:, q0:q0 + 2], pb[:, 0:2, 128])
                        if j == 0:
                            for q0, pb in banks2:
                                P.tt("dve", tacc[:, q0:q0 + 2, :], pb[:, 0:2, 0:128],
                                     rc[:, q0:q0 + 2].unsqueeze(2).to_broadcast([128, 2, 128]), ALU.mult)
                            return
                        P.ts("dve", rc1[:, 0:nq], rc[:, 0:nq], k.misc[:, 2:3], None, ALU.mult)
                        for q0, pb in banks2:
                            P.tt("dve", sqt[:, q0:q0 + 2, :], pb[:, 0:2, 0:128],
                                 rc1[:, q0:q0 + 2].unsqueeze(2).to_broadcast([128, 2, 128]), ALU.mult)
                        P.tt("dve", tacc[:, 0:nq, :], tacc[:, 0:nq, :], sqt[:, 0:nq, :], ALU.add)
                        P.tt("dve", sqt[:, 0:nq, :], tacc[:, 0:nq, :], tacc[:, 0:nq, :], ALU.mult)
                        P.rsum("dve", ss4[:, 0:nq], sqt[:, 0:nq, :])
                        P.act(ss4[:, 0:nq], ss4[:, 0:nq], AF.Ln, bias=k.epsc[:, 0:1], scale=1.0 / 128.0)
                        P.act(ss4[:, 0:nq], ss4[:, 0:nq], AF.Exp, scale=-0.5)
                        P.copy("dve", otok[:, 0:nq, :], tacc[:, 0:nq, :])
                        finish_chunk(i, c0, n, otok, otc, ss4)

                    attn_head_chunk(k, KT[:, 2 + h, :], qth[j], c0, n, kbs,
                                    lambda kb, h=h: Vb1[:, kb, h, 0:129], pof, 129, bank_first=(0, 2), post=post,
                                    zero_regs=[(pb.rearrange("p q e -> p (q e)"), 258)
                                               for pb in ([poA, poB] if nq > 2 else [poA])])
    k.pipe.drain()
    for o, _ in qt_ring:
        A.free(o)
    for o in (o_ot, o_oc, o_ta, o_sq, o_rc):
        A.free(o)
    for o, _ in pt_tiles:
        A.free(o)


def ln_stats_block(k, st, mv, tb):
    P = k.P
    for hh in range(2):
        P.add("dve", lambda e, o_=st[:, tb, hh, :], i_=k.XR[:, tb, hh * 512:(hh + 1) * 512]: e.bn_stats(o_, i_),
              reads=[k.XR[:, tb, hh * 512:(hh + 1) * 512]], writes=[st[:, tb, hh, :]])
    P.add("dve", lambda e, o_=mv[:, tb, :], i_=st[:, tb, :, :].rearrange("p a b -> p (a b)"): e.bn_aggr(o_, i_),
          reads=[st[:, tb, :, :]], writes=[mv[:, tb, :]])


def layer_norm(k, l, j, nblocks, out=False, hook=None, pre_stats=None):
    P, A = k.P, k.A
    o_g, gam = A.tile([D], F32)
    o_b, bet = A.tile([D], F32)
    P.dma("sp", gam, k.lng_d[l, j:j + 1, :].partition_broadcast(128))
    P.dma("sp", bet, k.lnb_d[l, j:j + 1, :].partition_broadcast(128))
    if pre_stats is None:
        o_st, st = A.tile([nblocks, 2, 6], F32)
        o_mv, mv = A.tile([nblocks, 2], F32)
        for tb in range(nblocks):
            ln_stats_block(k, st, mv, tb)
    else:
        o_st, st, o_mv, mv = pre_stats
    o_rs, rs = A.tile([2, nblocks], F32)
    o_y, ytmp = A.tile([2, D], F32)
    P.act(rs[:, 0, :], mv[:, :, 1], AF.Ln, bias=k.epsc[:, 0:1], scale=1.0)
    P.act(rs[:, 0, :], rs[:, 0, :], AF.Exp, scale=-0.5)
    P.stt("dve", rs[:, 1, :], mv[:, :, 0], -1.0, rs[:, 0, :], ALU.mult, ALU.mult)
    for tb in range(nblocks):
        r = tb % 2
        xs = k.XR[:, tb, :]
        y = ytmp[:, r, :]
        P.act(y, xs, AF.Identity, bias=rs[:, 1, tb:tb + 1], scale=rs[:, 0, tb:tb + 1])
        P.tt("dve", y, y, gam, ALU.mult)
        P.tt("dve", k.XR[:, tb, 0:640], y[:, 0:640], bet[:, 0:640], ALU.add)
        P.tt("pool", k.XR[:, tb, 640:D], y[:, 640:D], bet[:, 640:D], ALU.add)
        if out:
            P.dma("sp", k.out_d[tb * 128:(tb + 1) * 128, :], xs)
        if hook is not None:
            hook(tb)
    for o in (o_g, o_b, o_st, o_mv, o_rs, o_y):
        A.free(o)


def ffn(k, l, nblocks, pre, stats=None):
    P, A, PS = k.P, k.A, k.PS
    ntok = nblocks * 128
    chunks = CHUNKS_ALL if nblocks == 18 else CHUNKS_X
    o_u, u2T = pre
    o_h, hT = A.tile([8, ntok], BF16)
    o_g, Gt = A.tile([2, D], F32)
    load_gate(k, Gt[:, 0, :], l, 0, 1)
    if nblocks > 16:
        load_gate(k, Gt[:, 1, :], l, 1, 1)
    w1r = [A.tile([8, 512], BF16) for _ in range(2)]
    w2r = [A.tile([4, D], BF16) for _ in range(2)]
    o_r, rl = A.tile([2, 512], F32)
    o_t2, tmp2 = A.tile([2, 512], F32)
    hcnt = 0
    ycnt = 0
    for qd in range(4):
        for hh in range(2):
            c0 = qd * 1024 + hh * 512
            P.dma("poolq", w1r[hh][1], k.w1_d[l, :, c0:c0 + 512].rearrange("(k p) n -> p k n", p=128))
        for hh in range(2):
            r0 = qd * 1024 + hh * 512
            P.dma("poolq", w2r[hh][1], k.w2_d[l, r0:r0 + 512, :].rearrange("(j p) n -> p j n", p=128))
        if k.sub <= 10:
            break
        for hc in range(8):
            w1t = w1r[hc // 4][1]
            for (c0, n) in chunks:
                bank = PS[:, hcnt % 3, 0:n]
                for kk in range(8):
                    P.mm(bank, w1t[:, kk, (hc % 4) * 128:(hc % 4 + 1) * 128], u2T[:, kk, c0:c0 + n], kk == 0, kk == 7)
                r = rl[:, hcnt % 2, 0:n]
                P.act(r, bank, AF.Relu)
                P.act(hT[:, hc, c0:c0 + n], r, AF.Square)
                hcnt += 1
        if k.sub <= 11:
            break
        for tb in range(nblocks):
            G = Gt[:, 0, :] if tb < 16 else Gt[:, 1, :]
            for h2 in range(2):
                bank = PS[:, 3 + ycnt % 4, :]
                for hc in range(8):
                    P.mm(bank, hT[:, hc, tb * 128:(tb + 1) * 128], w2r[hc // 4][1][:, hc % 4, h2 * 512:(h2 + 1) * 512],
                         hc == 0, hc == 7)
                t2 = tmp2[:, ycnt % 2, :]
                ycnt += 1
                P.tt("dve", t2, bank, G[:, h2 * 512:(h2 + 1) * 512], ALU.mult)
                xs = k.XR[:, tb, h2 * 512:(h2 + 1) * 512]
                if qd == 0:
                    P.stt("dve", xs, xs, ALPHA, t2, ALU.mult, ALU.add)
                else:
                    P.tt("dve", xs, xs, t2, ALU.add)
            if qd == 3 and stats is not None:
                ln_stats_block(k, stats[1], stats[3], tb)
        if k.sub <= 12:
            break
    for o in (o_u, o_h, o_g, o_r, o_t2, w1r[0][0], w1r[1][0], w2r[0][0], w2r[1][0]):
        A.free(o)


def layer1_mixer(k, pre):
    P, A, PS = k.P, k.A, k.PS
    l = 1
    o_UT, UT = pre
    o_cs, CS = A.tile([256], F32)
    P.dma("sp", CS, k.CS_d)
    toffs, tmps = alloc_fm_tmps(k)
    itb = 0
    wq_ring = [A.tile([8, 256], BF16) for _ in range(2)]
    qst_ring = [A.tile([512], BF16) for _ in range(3)]
    for pi in range(2):
        _, wq = wq_ring[pi % 2]
        load_w_cols(k, wq, k.cdwin_d, pi * 256, 256)
        for sub in range(2):
            i = pi * 2 + sub

            def dest_fn(ci, c0, n, itb=itb):
                return qst_ring[(itb + ci) % 3][1][:, 0:n]

            def after(ci, c0, n, dest, i=i):
                P.dma("sp", k.QS_d[i, :, c0:c0 + n], dest, writes=[("QS", i, ci)])

            fm_block(k, UT, wq[:, :, sub * 128:(sub + 1) * 128], CHUNKS_X, dest_fn, tmps, after=after, itbase=itb,
                     pre_scale=ATTN_SCALE)
            itb += 4
    k.fmp.drain()
    for o, _ in wq_ring + qst_ring:
        A.free(o)
    o_AB, AB = A.tile([16, 4, 256], BF16, top=True)
    o_wf, wf = A.tile([8, 512], BF16)
    load_w_cols(k, wf, k.cdwin_d, 512, 512)
    abcnt = [0]
    k.fd_pending = []
    for gi in range(4):
        def dest_fn(ci, c0, n, itb=itb):
            return tmps[(itb + ci) % 3][0][:, 0:n]

        def after_now(ci, c0, n, dest, gi=gi):
            k.fd_pending.append(lambda: after(ci, c0, n, dest, gi))
            while len(k.fd_pending) > 1:
                k.fd_pending.pop(0)()

        def after(ci, c0, n, dest, gi=gi):
            for pair in range(2):
                bank = PS[:, 2 + abcnt[0] % 2, :]
                abcnt[0] += 1
                for j in range(2):
                    qs = pair * 2 + j
                    P.mm(bank[:, j * 256:(j + 1) * 256], dest[:, qs * 128:(qs + 1) * 128], CS, True, True)
                tb = c0 // 128 + pair * 2
                P.copy("dve", AB[:, tb:tb + 2, gi, :], bank.rearrange("p (a b) -> p a b", a=2))

        fm_block(k, UT, wf[:, :, gi * 128:(gi + 1) * 128], CHUNKS_X, dest_fn, tmps, after=after_now, itbase=itb)
        itb += 4
    k.fmp.drain()
    while k.fd_pending:
        k.fd_pending.pop(0)()
    A.free(o_wf)
    o_KT, KT = A.tile([4, NT], BF16, top=True)
    o_wk, wk = A.tile([8, 512], BF16)
    load_w_cols(k, wk, k.cdwin_d, 1024, 512)
    for j in range(4):
        fm_block(k, UT, wk[:, :, j * 128:(j + 1) * 128], CHUNKS_ALL, lambda ci, c0, n, j=j: KT[:, j, c0:c0 + n], tmps,
                 itbase=itb)
        itb += 5
    k.fmp.drain()
    A.free(o_wk)
    for o in toffs:
        A.free(o)
    o_V, V1 = A.tile([18, 8, 80], BF16, top=True)
    o_wv, wv = A.tile([8, 512], BF16)
    load_w_cols(k, wv, k.cdwin_d, 1536, 512)
    P.memset("pool", V1[:, :, :, 64:65], 1.0)
    for t in range(18):
        bb = PS[:, t % 2, :]
        for kk in range(8):
            P.mm(bb, UT[:, kk, t * 128:(t + 1) * 128], wv[:, kk, :], kk == 0, kk == 7)
        P.copy("act" if t % 2 == 0 else "dve", V1[:, t, :, 0:64], bb.rearrange("p (h d) -> p h d", h=8))
    A.free(o_wv)
    A.free(o_UT)
    A.free(o_cs)
    o_wox, WoX = A.tile([8, D], BF16)
    o_g, Gt = A.tile([D], F32)
    o_ws, wst = A.tile([2, D], F32)
    load_gate(k, Gt, l, 0, 0)
    for kk in range(8):
        P.dma("sp", wst[:, kk % 2, :], k.cdwout_d[kk * 128:(kk + 1) * 128, :])
        P.tt("dve" if kk % 2 == 0 else "pool", WoX[:, kk, :], wst[:, kk % 2, :], Gt, ALU.mult)
    A.free(o_g)
    A.free(o_ws)
    qt_ring = [A.tile([NX], BF16) for _ in range(2)]
    P.memset("pool", qt_ring[0][1][64:128, :], 0.0)
    P.memset("pool", qt_ring[1][1][0:64, :], 0.0)

    def load_q(i):
        rd = [("QS", i, ci) for ci in range(4)]
        P.dma("sp", qt_ring[0][1][0:64, :], k.QS_d[i, 0:64, 0:NX], reads=rd)
        P.dma("sp", qt_ring[1][1][64:128, :], k.QS_d[i, 64:128, 0:NX], reads=rd)

    tb_ring = [A.tile([NTILE, 64], BF16) for _ in range(2)]
    sb_ring = []
    pt_tiles = [A.tile([512], BF16) for _ in range(4)]
    k.pt_ring = [t for _, t in pt_tiles]
    k.pt_cnt = 0
    o_ot, otok = A.tile([16, 128], BF16)
    o_oc, otc_r = A.tile([2, 512], BF16)
    rc = k.misc[:, 4:8]
    PST = PS[:, 5, :].bitcast(BF16)
    ycnt = [0]
    sbc = [0]

    def wout_accum(otc, kchunk, c0, first, defer=None):
        for qs in range(4):
            tb = c0 // 128 + qs
            for h2 in range(2):
                def t_y(qs=qs, tb=tb, h2=h2):
                    bank = PS[:, 6 + ycnt[0] % 2, :]
                    ycnt[0] += 1
                    P.mm(bank, otc[:, qs * 128:(qs + 1) * 128], WoX[:, kchunk, h2 * 512:(h2 + 1) * 512], True, True)
                    xs = k.XR[:, tb, h2 * 512:(h2 + 1) * 512]
                    if first:
                        P.stt("dve", xs, xs, ALPHA, bank, ALU.mult, ALU.add)
                    else:
                        P.tt("dve", xs, bank, xs, ALU.add)

                if defer is not None:
                    defer.append(t_y)
                else:
                    t_y()

    P.dma("sp", tb_ring[0][1], k.rpbt_d[0].rearrange("p (a b) -> p a b", b=64))
    k.pipe = Pipe(2)
    o_rc, rcs = A.tile([4, 4], F32)
    pcnt = 0
    for i in range(4):
        load_q(i)
        for hf in range(2):
            h = 2 * i + hf
            TBt = tb_ring[h % 2][1]
            if h + 1 < 8:
                P.dma("sp", tb_ring[(h + 1) % 2][1], k.rpbt_d[h + 1].rearrange("p (a b) -> p a b", b=64))
            for c in range(4):
                c0, n = c * 512, 512
                kbs = list(range(max(0, 4 * c - 2), min(16, 4 * c + 6))) + [16, 17]

                def bias_mm(kb, c=c, TBt=TBt):
                    if kb >= 16:
                        return None
                    return [(r0 * 64, nr * 64, TBt[:, pos0:pos0 + nr, :].rearrange("p a b -> p (a b)"))
                            for (r0, nr, pos0) in natten_segments(kb, c)]

                po = PS[:, 3 + pcnt % 2, 0:260].rearrange("p (q e) -> p q e", e=65)
                rc = rcs[:, pcnt % 4, :]
                pcnt += 1

                def post(po=po, rc=rc, c=c, hf=hf, i=i):
                    P.recip(rc[:, 0:4], po[:, 0:4, 64])
                    P.tt("dve", otok[:, c * 4:c * 4 + 4, 64 * hf:64 * hf + 64], po[:, 0:4, 0:64],
                         rc[:, 0:4].unsqueeze(2).to_broadcast([128, 4, 64]), ALU.mult)
                    if hf == 1:
                        otc = otc_r[:, c % 2, :]

                        def t_transpose(otc=otc, c=c):
                            for qs in range(4):
                                P.tr(PST[:, qs * 128:(qs + 1) * 128], otok[:, c * 4 + qs, :], k.identb)
                            P.copy("dve", otc, PST[:, 0:512])

                        k.pipe.tasks.append(t_transpose)
                        wout_accum(otc, i, c * 512, i == 0, defer=k.pipe.tasks)

                attn_head_chunk(k, KT[:, i, :], qt_ring[hf][1], c0, n, kbs,
                                lambda kb, h=h: V1[:, kb, h, 0:65], lambda qs, po=po: po[:, qs, :], 65, bias_mm=bias_mm,
                                post=post, exp_scale=1.0, zero_regs=[(po.rearrange("p q e -> p (q e)"), 260)])
    k.pipe.drain()
    A.free(o_rc)
    for o, _ in qt_ring + tb_ring + sb_ring + pt_tiles:
        A.free(o)
    A.free(o_ot)
    A.free(o_KT)
    A.free(o_V)
    c_ring = [A.tile([16, 512], BF16) for _ in range(2)]
    s_ring = [A.tile([16, 512], BF16) for _ in range(2)]
    fcnt = 0
    fpend = []
    for tc in range(4):
        Cs = c_ring[tc % 2][1]
        Ss = s_ring[tc % 2][1]
        P.dma("sp", Cs, k.Ct_d[:, tc * 512:(tc + 1) * 512].rearrange("(j p) t -> p j t", p=128))
        P.dma("sp", Ss, k.St_d[:, tc * 512:(tc + 1) * 512].rearrange("(j p) t -> p j t", p=128))
        for gi in range(4):
            bank = PS[:, fcnt % 2, :]
            for j in range(16):
                P.mm(bank, AB[:, j, gi, 0:128], Cs[:, j, :], j == 0, False)
                P.mm(bank, AB[:, j, gi, 128:256], Ss[:, j, :], False, j == 15)
            otc = otc_r[:, fcnt % 2, :]
            fcnt += 1
            P.copy("act", otc, bank)
            prev = fpend
            fpend = []
            wout_accum(otc, 4 + gi, tc * 512, False, defer=fpend)
            for t_ in prev:
                t_()
    for t_ in fpend:
        t_()
    for o, _ in c_ring + s_ring:
        A.free(o)
    for o in (o_oc, o_AB, o_wox):
        A.free(o)


_NC_CACHE = {}


def _in_maps(inp, cores):
    hc = host_consts()
    rp = rpb_table(np.asarray(inp["c_rpb"], np.float32)[0])
    small = np.zeros((128, 4), np.float32)
    small[:, 0] = np.tile(np.asarray(inp["a_q_norm"], np.float32)[0], 2)
    small[:, 1] = np.tile(np.asarray(inp["a_k_norm"], np.float32)[0], 2)
    small[:, 2] = np.asarray(inp["b_subln"], np.float32)[0]
    f = lambda a: np.ascontiguousarray(np.asarray(a, np.float32))
    mod_w = f(inp["mod_w"])
    mod_b = f(inp["mod_b"])
    shared = {
        "c_ctx": f(inp["c_ctx"]).reshape(8, 128),
        "mod_w": mod_w, "mod_b48": mod_b.reshape(2, 48, 128), "mod_b6": mod_b.reshape(2, 6, 1024),
        "ln_g": f(inp["ln_g"]), "ln_b": f(inp["ln_b"]), "ffn_w1": f(inp["ffn_w1"]), "ffn_w2": f(inp["ffn_w2"]),
        "ab_w_in": f(inp["ab_w_in"])[0], "ab_w_out": f(inp["ab_w_out"])[0], "small": small,
        "b_lambda": f(inp["b_lambda"])[0].reshape(1, 256), "cd_w_in": f(inp["cd_w_in"])[0],
        "cd_w_out": f(inp["cd_w_out"])[0], "rpbt": rp.reshape(8, 128, -1),
        "cosT": hc["cosT"], "sinT": hc["sinT"], "cmat": hc["cmat"], "CS": hc["CS"], "Ct": hc["Ct"], "St": hc["St"],
    }
    x = f(inp["x"])
    ctx = f(inp["ctx"])
    c = f(inp["c"])
    maps = []
    for b in cores:
        m = dict(shared)
        m["x"] = np.ascontiguousarray(x[b])
        m["ctx"] = np.ascontiguousarray(ctx[b])
        m["c"] = np.ascontiguousarray(c[b].reshape(8, 128))
        maps.append(m)
    return maps


def kernel(**inputs):
    n = 8
    nc = build_program()
    maps = _in_maps(inputs, list(range(n)))
    res = run_bass_kernel_spmd(nc, maps, core_ids=list(range(n)))
    out = np.stack([np.asarray(r["out"], np.float32) for r in res.results], axis=0)
    return out
```
